# Optimizing a Trainium2 kernel written in Bass

```python
import math
import jax
import jax.numpy as jnp
from jax import lax
import numpy as np

D_MODEL = 4096
BATCH = 1
SEQ = 16384
DEPTH = 2

GRID_W = 64
CTX_LEN = 256
N_BRANCH = 4
W_BR = D_MODEL // N_BRANCH
A_HEAD = 64
A_HEADS = W_BR // A_HEAD
R_LORA = 64
A_FEAT = 3 * W_BR + 2 * R_LORA
RWKV_GN_EPS = 64e-5
B_SUB = 64
B_HEADS = W_BR // (2 * B_SUB)
C_HEAD = 128
C_HEADS = W_BR // C_HEAD
CONV_K = 3
CHUNK = 64
D_HEAD = 128
D_HEADS = W_BR // D_HEAD
D_KV_HEADS = 2
Q_BLOCK = 128
ROPE_THETA = 10000.0
NORM_EPS = 1e-6

COL_SIZES = (A_FEAT, W_BR,
             3 * W_BR, W_BR,
             3 * W_BR + 4 * C_HEADS, W_BR,
             W_BR + 2 * D_KV_HEADS * D_HEAD, W_BR,
             N_BRANCH * D_MODEL)
N_IN = sum(COL_SIZES)

kernel_name = 'hybrid_flow_backbone'


def _rmsnorm(x, w, eps=NORM_EPS):
    x32 = x.astype(jnp.float32)
    y = x32 * lax.rsqrt(jnp.mean(x32 * x32, axis=-1, keepdims=True) + eps)
    return (y * w.astype(jnp.float32)).astype(x.dtype)


def _l2norm(x, eps=1e-6):
    x32 = x.astype(jnp.float32)
    return (x32 * lax.rsqrt(jnp.sum(x32 * x32, axis=-1, keepdims=True) + eps)).astype(x.dtype)


def _heads(t, n_heads):
    return t.reshape(t.shape[:-1] + (n_heads, t.shape[-1] // n_heads))


def _axial_angles(pos, head_dim):
    m = head_dim // 2
    inv = jnp.power(ROPE_THETA, -jnp.arange(0, m, 2, dtype=jnp.float32) / m)
    return pos[:, None] * inv[None, :]


def _rope_half(x, ang):
    x1, x2 = jnp.split(x, 2, axis=-1)
    cos = jnp.cos(ang).astype(x.dtype)
    sin = jnp.sin(ang).astype(x.dtype)
    return jnp.concatenate([x1 * cos - x2 * sin, x2 * cos + x1 * sin], axis=-1)


def _rope_2d(x, ang_row, ang_col):
    shape = (1, x.shape[1]) + (1,) * (x.ndim - 3) + (ang_row.shape[-1],)
    x_row, x_col = jnp.split(x, 2, axis=-1)
    return jnp.concatenate([_rope_half(x_row, ang_row.reshape(shape)),
                            _rope_half(x_col, ang_col.reshape(shape))], axis=-1)


def _sweep_query_blocks(fn, q):
    b, t = q.shape[:2]
    qb = jnp.moveaxis(q.reshape((b, t // Q_BLOCK, Q_BLOCK) + q.shape[2:]), 1, 0)
    out = lax.map(fn, qb)
    return jnp.moveaxis(out, 0, 1).reshape((b, t) + out.shape[3:])


def _centred_shift(f):
    prev = jnp.pad(f, ((0, 0), (1, 0), (0, 0)))[:, :-1]
    nxt = jnp.pad(f, ((0, 0), (0, 1), (0, 0)))[:, 1:]
    return 0.5 * (prev + nxt)


def _dwconv_centred(x, w):
    k = w.shape[0]
    return lax.conv_general_dilated(x, w[:, None, :], window_strides=(1,),
                                    padding=[((k - 1) // 2, k // 2)],
                                    dimension_numbers=('NWC', 'WIO', 'NWC'),
                                    feature_group_count=x.shape[-1])


def _wkv7_scan(r, w, k, v, kk, a, s0, reverse):
    out_dtype = r.dtype
    xs = tuple(jnp.moveaxis(t.astype(jnp.float32), 1, 0) for t in (r, w, k, v, kk, a))

    def step(s, inp):
        r_t, w_t, k_t, v_t, kk_t, a_t = inp
        sa = jnp.einsum('bhvk,bhk->bhv', s, -kk_t)
        s = (s * w_t[:, :, None, :] + sa[..., None] * (kk_t * a_t)[:, :, None, :]
             + v_t[..., None] * k_t[:, :, None, :])
        return s, jnp.einsum('bhvk,bhk->bhv', s, r_t)

    s_fin, y = lax.scan(step, s0.astype(jnp.float32), xs, reverse=reverse)
    return jnp.moveaxis(y, 0, 1).astype(out_dtype), s_fin


def _rwkv7_inputs(f, mu, w0, w2, a0, a2, k_k, k_a):
    f = f + mu * (_centred_shift(f) - f)
    r, k, v, wlo, alo = jnp.split(f, [W_BR, 2 * W_BR, 3 * W_BR, 3 * W_BR + R_LORA], axis=-1)
    kk = _l2norm(_heads(k * k_k, A_HEADS))
    per_dir = []
    for d in range(2):
        w_raw = (w0[d] + jnp.tanh(wlo) @ w2[d]).astype(jnp.float32)
        log_decay = -jnp.exp(-jax.nn.softplus(-w_raw) - 0.5)
        a = jax.nn.sigmoid(a0[d] + alo @ a2[d])
        k_d = k * (1.0 + (a - 1.0) * k_a)
        per_dir.append((_heads(jnp.exp(log_decay), A_HEADS), _heads(k_d, A_HEADS), _heads(a, A_HEADS)))
    return _heads(r, A_HEADS), _heads(v, A_HEADS), kk, per_dir


def _head_groupnorm(y, w, b):
    y32 = y.astype(jnp.float32)
    mu = jnp.mean(y32, axis=-1, keepdims=True)
    var = jnp.mean(jnp.square(y32 - mu), axis=-1, keepdims=True)
    return ((y32 - mu) * lax.rsqrt(var + RWKV_GN_EPS)).astype(y.dtype) * w + b


def _rwkv7_branch(fc, fl, mu, w0, w2, a0, a2, k_k, k_a, r_k, ln_w, ln_b):
    rc, vc, kkc, dc = _rwkv7_inputs(fc, mu, w0, w2, a0, a2, k_k, k_a)
    rl, vl, kkl, dl = _rwkv7_inputs(fl, mu, w0, w2, a0, a2, k_k, k_a)
    s0 = jnp.zeros((fc.shape[0], A_HEADS, A_HEAD, A_HEAD), jnp.float32)
    ys_c, ys_l = [], []
    for d, rev in enumerate((False, True)):
        (wc, kc, ac), (wl, kl, al) = dc[d], dl[d]
        y_c, s_c = _wkv7_scan(rc, wc, kc, vc, kkc, ac, s0, rev)
        y_l, _ = _wkv7_scan(rl, wl, kl, vl, kkl, al, s_c, rev)
        ys_c.append(y_c)
        ys_l.append(y_l)
    gn_w, gn_b, = _heads(ln_w, A_HEADS), _heads(ln_b, A_HEADS)

    def finish(y, r, v, dirs):
        k_bonus = 0.5 * (dirs[0][1] + dirs[1][1])
        y = _head_groupnorm(y, gn_w, gn_b)
        y = y + jnp.sum(r * k_bonus * r_k, axis=-1, keepdims=True) * v
        return y.reshape(y.shape[:-2] + (W_BR,))

    return finish(ys_c[0] + ys_c[1], rc, vc, dc), finish(ys_l[0] + ys_l[1], rl, vl, dl)


def _diff_attend(q, k, v, lam):
    s = jnp.einsum('bqhmd,bkhmd->bhmqk', q, k).astype(jnp.float32) * (B_SUB ** -0.5)
    p = jax.nn.softmax(s, axis=-1)
    p_diff = p[:, :, 0] - lam * p[:, :, 1]
    return jnp.einsum('bhqk,bkhd->bqhd', p_diff.astype(v.dtype), v)


def _diff_attn_branch(pc, pl, ang_row, ang_col, q_nw, k_nw, lam_vec, subln_w, lam_init):
    def prep(p):
        q, k, v = jnp.split(p, 3, axis=-1)
        q = _rmsnorm(q.reshape(q.shape[:-1] + (B_HEADS, 2, B_SUB)), q_nw)
        k = _rmsnorm(k.reshape(k.shape[:-1] + (B_HEADS, 2, B_SUB)), k_nw)
        return q, k, _heads(v, B_HEADS)

    qc, kc, vc = prep(pc)
    ql, kl, vl = prep(pl)
    ql = _rope_2d(ql, ang_row, ang_col)
    kl = _rope_2d(kl, ang_row, ang_col)
    lv = lam_vec.astype(jnp.float32)
    lam = jnp.exp(jnp.sum(lv[0] * lv[1])) - jnp.exp(jnp.sum(lv[2] * lv[3])) + lam_init
    k_all = jnp.concatenate([kc, kl], axis=1)
    v_all = jnp.concatenate([vc, vl], axis=1)
    oc = _diff_attend(qc, kc, vc, lam)
    ol = _sweep_query_blocks(lambda qb: _diff_attend(qb, k_all, v_all, lam), ql)

    def finish(o):
        o = _rmsnorm(o, subln_w) * (1.0 - lam_init)
        return o.reshape(o.shape[:-2] + (W_BR,))

    return finish(oc), finish(ol)


def _gated_delta_chunked(q, k, v, g, beta, s0):
    out_dtype = v.dtype
    b, t, h, _ = q.shape
    dv = v.shape[-1]
    n = t // CHUNK

    def chunks(x):
        return jnp.moveaxis(x.astype(jnp.float32).reshape((b, n, CHUNK, h) + x.shape[3:]), 3, 1)

    q, k, v, g, beta = chunks(q), chunks(k), chunks(v), chunks(g), chunks(beta)
    g = jnp.cumsum(g, axis=-1)
    incl = jnp.tril(jnp.ones((CHUNK, CHUNK), dtype=bool))
    strict = jnp.tril(jnp.ones((CHUNK, CHUNK), dtype=bool), -1)
    decay = jnp.exp(jnp.where(incl, g[..., :, None] - g[..., None, :], -jnp.inf))
    k_beta = k * beta[..., None]
    a_low = jnp.where(strict, jnp.einsum('bhncd,bhnsd->bhncs', k_beta, k) * decay, 0.0)
    eye = jnp.eye(CHUNK, dtype=jnp.float32)
    rhs = jnp.concatenate([v * beta[..., None], k_beta * jnp.exp(g)[..., None]], axis=-1)
    sol = lax.linalg.triangular_solve(eye + a_low, rhs, left_side=True, lower=True)
    u, w = sol[..., :dv], sol[..., dv:]
    attn = jnp.where(incl, jnp.einsum('bhncd,bhnsd->bhncs', q, k) * decay, 0.0)

    def step(s, inp):
        q_i, k_i, u_i, w_i, g_i, attn_i = inp
        v_new = u_i - jnp.einsum('bhcd,bhde->bhce', w_i, s)
        o = (jnp.einsum('bhcd,bhde->bhce', q_i * jnp.exp(g_i)[..., None], s)
             + jnp.einsum('bhcs,bhse->bhce', attn_i, v_new))
        g_last = g_i[..., -1:]
        s = (s * jnp.exp(g_last)[..., None]
             + jnp.einsum('bhcd,bhce->bhde', k_i * jnp.exp(g_last - g_i)[..., None], v_new))
        return s, o

    xs = tuple(jnp.moveaxis(x, 2, 0) for x in (q, k, u, w, g, attn))
    s_fin, o = lax.scan(step, s0.astype(jnp.float32), xs)
    o = jnp.transpose(o, (1, 0, 3, 2, 4)).reshape(b, t, h, dv)
    return o.astype(out_dtype), s_fin


def _gdn_branch(pc, pl, conv_w, a_log, dt_bias, norm_w):
    def prep(p):
        qkv, beta_l, alpha = jnp.split(p, [3 * W_BR, 3 * W_BR + 2 * C_HEADS], axis=-1)
        qkv = jax.nn.silu(_dwconv_centred(qkv, conv_w))
        q, k, v = jnp.split(qkv, 3, axis=-1)
        q = _l2norm(_heads(q, C_HEADS)) * (C_HEAD ** -0.5)
        k = _l2norm(_heads(k, C_HEADS))
        beta = jax.nn.sigmoid(beta_l).reshape(beta_l.shape[:-1] + (2, C_HEADS))
        alpha = alpha.astype(jnp.float32).reshape(alpha.shape[:-1] + (2, C_HEADS))
        g = -jnp.exp(a_log.astype(jnp.float32)) * jax.nn.softplus(alpha + dt_bias.astype(jnp.float32))
        return q, k, _heads(v, C_HEADS), beta, g

    qc, kc, vc, bc, gc = prep(pc)
    ql, kl, vl, bl, gl = prep(pl)
    s0 = jnp.zeros((pc.shape[0], C_HEADS, C_HEAD, C_HEAD), jnp.float32)
    os_c, os_l = [], []
    for d in range(2):
        fl = (lambda t: jnp.flip(t, axis=1)) if d == 1 else (lambda t: t)
        o_c, s_c = _gated_delta_chunked(fl(qc), fl(kc), fl(vc), fl(gc[:, :, d]), fl(bc[:, :, d]), s0)
        o_l, _ = _gated_delta_chunked(fl(ql), fl(kl), fl(vl), fl(gl[:, :, d]), fl(bl[:, :, d]), s_c)
        os_c.append(fl(o_c))
        os_l.append(fl(o_l))

    def finish(o):
        o = _rmsnorm(o, norm_w)
        return o.reshape(o.shape[:-2] + (W_BR,))

    return finish(os_c[0] + os_c[1]), finish(os_l[0] + os_l[1])


def _gqa_attend(q, k, v):
    s = jnp.einsum('bqngd,bknd->bngqk', q, k).astype(jnp.float32) * (D_HEAD ** -0.5)
    p = jax.nn.softmax(s, axis=-1).astype(v.dtype)
    return jnp.einsum('bngqk,bknd->bqngd', p, v)


def _gqa_branch(pc, pl, ang_row, ang_col, q_nw, k_nw):
    groups = D_HEADS // D_KV_HEADS

    def prep(p):
        q, k, v = jnp.split(p, [W_BR, W_BR + D_KV_HEADS * D_HEAD], axis=-1)
        q = _rmsnorm(q.reshape(q.shape[:-1] + (D_KV_HEADS, groups, D_HEAD)), q_nw)
        k = _rmsnorm(_heads(k, D_KV_HEADS), k_nw)
        return q, k, _heads(v, D_KV_HEADS)

    qc, kc, vc = prep(pc)
    ql, kl, vl = prep(pl)
    ql = _rope_2d(ql, ang_row, ang_col)
    kl = _rope_2d(kl, ang_row, ang_col)
    k_all = jnp.concatenate([kc, kl], axis=1)
    v_all = jnp.concatenate([vc, vl], axis=1)
    oc = _gqa_attend(qc, kc, vc)
    ol = _sweep_query_blocks(lambda qb: _gqa_attend(qb, k_all, v_all), ql)
    return oc.reshape(oc.shape[:2] + (W_BR,)), ol.reshape(ol.shape[:2] + (W_BR,))


def _merge(ys, zs, gate_logits, w_branch, w_out):
    gates = jnp.split(gate_logits, N_BRANCH, axis=-1)
    acc = jax.nn.sigmoid(gates[0]) * ((ys[0] * jax.nn.silu(zs[0])) @ w_branch[0])
    for i in range(1, N_BRANCH):
        acc = acc + jax.nn.sigmoid(gates[i]) * ((ys[i] * jax.nn.silu(zs[i])) @ w_branch[i])
    return acc @ w_out


def setup_inputs(seed: int = 0) -> dict:
    key = jax.random.key(seed)
    ks = iter(jax.random.split(key, 32))
    f32 = jnp.float32
    L, D = DEPTH, D_MODEL

    def nrm(shape, scale):
        return jax.random.normal(next(ks), shape, f32) * scale

    x = nrm((BATCH, SEQ, D), 1.0)
    c = nrm((BATCH, D), 1.0)
    ctx = nrm((BATCH, CTX_LEN, D), 1.0)
    c_ctx = nrm((D,), 1.0)
    norm_w = 1.0 + nrm((L, D), 0.02)
    w_mod = nrm((L, D, 3 * D), 0.5 * D ** -0.5)
    b_mod = nrm((L, 3 * D), 0.01)
    w_in = nrm((L, D, N_IN), D ** -0.5)
    rwkv_mu = jax.random.uniform(next(ks), (L, A_FEAT), f32)
    rwkv_w0 = -2.0 + nrm((L, 2, W_BR), 1.0)
    rwkv_w2 = nrm((L, 2, R_LORA, W_BR), 0.5 * R_LORA ** -0.5)
    rwkv_a0 = nrm((L, 2, W_BR), 0.5)
    rwkv_a2 = nrm((L, 2, R_LORA, W_BR), 0.5 * R_LORA ** -0.5)
    rwkv_kk = 0.85 + nrm((L, W_BR), 0.05)
    rwkv_ka = 1.0 + nrm((L, W_BR), 0.05)
    rwkv_rk = nrm((L, A_HEADS, A_HEAD), 0.1)
    rwkv_ln_w = 1.0 + nrm((L, W_BR), 0.02)
    rwkv_ln_b = nrm((L, W_BR), 0.02)
    diff_qn = 1.0 + nrm((L, B_SUB), 0.02)
    diff_kn = 1.0 + nrm((L, B_SUB), 0.02)
    diff_lam = nrm((L, 4, B_SUB), 0.1)
    diff_subln = 1.0 + nrm((L, 2 * B_SUB), 0.02)
    gdn_conv = nrm((L, CONV_K, 3 * W_BR), CONV_K ** -0.5)
    gdn_a_log = jnp.log(jax.random.uniform(next(ks), (L, 2, C_HEADS), f32, 1.0, 16.0))
    dt = jnp.exp(jax.random.uniform(next(ks), (L, 2, C_HEADS), f32, math.log(1e-3), math.log(1e-1)))
    gdn_dt_bias = dt + jnp.log(-jnp.expm1(-dt))
    gdn_norm = 1.0 + nrm((L, C_HEAD), 0.02)
    gqa_qn = 1.0 + nrm((L, D_HEAD), 0.02)
    gqa_kn = 1.0 + nrm((L, D_HEAD), 0.02)
    w_branch = nrm((L, N_BRANCH, W_BR, D), W_BR ** -0.5)
    w_out = nrm((L, D, D), D ** -0.5)
    return {'x': x, 'c': c, 'ctx': ctx, 'c_ctx': c_ctx, 'norm_w': norm_w, 'w_mod': w_mod,
            'b_mod': b_mod, 'w_in': w_in, 'rwkv_mu': rwkv_mu, 'rwkv_w0': rwkv_w0,
            'rwkv_w2': rwkv_w2, 'rwkv_a0': rwkv_a0, 'rwkv_a2': rwkv_a2, 'rwkv_kk': rwkv_kk,
            'rwkv_ka': rwkv_ka, 'rwkv_rk': rwkv_rk, 'rwkv_ln_w': rwkv_ln_w, 'rwkv_ln_b': rwkv_ln_b,
            'diff_qn': diff_qn, 'diff_kn': diff_kn, 'diff_lam': diff_lam, 'diff_subln': diff_subln,
            'gdn_conv': gdn_conv, 'gdn_a_log': gdn_a_log, 'gdn_dt_bias': gdn_dt_bias,
            'gdn_norm': gdn_norm, 'gqa_qn': gqa_qn, 'gqa_kn': gqa_kn, 'w_branch': w_branch,
            'w_out': w_out}


def reference(x, c, ctx, c_ctx, norm_w, w_mod, b_mod, w_in, rwkv_mu, rwkv_w0, rwkv_w2,
              rwkv_a0, rwkv_a2, rwkv_kk, rwkv_ka, rwkv_rk, rwkv_ln_w, rwkv_ln_b, diff_qn,
              diff_kn, diff_lam, diff_subln, gdn_conv, gdn_a_log, gdn_dt_bias, gdn_norm,
              gqa_qn, gqa_kn, w_branch, w_out):
    n_lat = x.shape[1]
    ROWS = n_lat // GRID_W
    rows = jnp.repeat(jnp.arange(ROWS, dtype=jnp.float32), GRID_W)
    cols = jnp.tile(jnp.arange(GRID_W, dtype=jnp.float32), ROWS)
    ang_b_row, ang_b_col = _axial_angles(rows, B_SUB), _axial_angles(cols, B_SUB)
    ang_d_row, ang_d_col = _axial_angles(rows, D_HEAD), _axial_angles(cols, D_HEAD)
    cuts = np.cumsum(COL_SIZES)[:-1].tolist()
    cond_lat = jax.nn.silu(c)
    cond_ctx = jax.nn.silu(c_ctx)[None, :]

    for l in range(DEPTH):
        lam_init = 0.8 - 0.6 * math.exp(-0.3 * l)
        sh_l, sc_l, ga_l = jnp.split((cond_lat @ w_mod[l] + b_mod[l])[:, None, :], 3, axis=-1)
        sh_c, sc_c, ga_c = jnp.split((cond_ctx @ w_mod[l] + b_mod[l])[:, None, :], 3, axis=-1)
        pl = (_rmsnorm(x, norm_w[l]) * (1.0 + sc_l) + sh_l) @ w_in[l]
        pc = (_rmsnorm(ctx, norm_w[l]) * (1.0 + sc_c) + sh_c) @ w_in[l]
        a_fc, a_zc, b_fc, b_zc, c_fc, c_zc, d_fc, d_zc, gate_c = jnp.split(pc, cuts, axis=-1)
        a_fl, a_zl, b_fl, b_zl, c_fl, c_zl, d_fl, d_zl, gate_l = jnp.split(pl, cuts, axis=-1)

        ya_c, ya_l = _rwkv7_branch(a_fc, a_fl, rwkv_mu[l], rwkv_w0[l], rwkv_w2[l], rwkv_a0[l],
                                   rwkv_a2[l], rwkv_kk[l], rwkv_ka[l], rwkv_rk[l],
                                   rwkv_ln_w[l], rwkv_ln_b[l])
        yb_c, yb_l = _diff_attn_branch(b_fc, b_fl, ang_b_row, ang_b_col, diff_qn[l], diff_kn[l],
                                       diff_lam[l], diff_subln[l], lam_init)
        yc_c, yc_l = _gdn_branch(c_fc, c_fl, gdn_conv[l], gdn_a_log[l], gdn_dt_bias[l], gdn_norm[l])
        yd_c, yd_l = _gqa_branch(d_fc, d_fl, ang_d_row, ang_d_col, gqa_qn[l], gqa_kn[l])

        x = x + ga_l * _merge((ya_l, yb_l, yc_l, yd_l), (a_zl, b_zl, c_zl, d_zl), gate_l,
                              w_branch[l], w_out[l])
        if l < DEPTH - 1:
            ctx = ctx + ga_c * _merge((ya_c, yb_c, yc_c, yd_c), (a_zc, b_zc, c_zc, d_zc), gate_c,
                                      w_branch[l], w_out[l])
    return x
```

```python
import math
import numpy as np
import ml_dtypes
from contextlib import ExitStack, contextmanager
import concourse.bass as bass
import concourse.mybir as mybir
from concourse.bass_utils import run_bass_kernel_spmd

F32 = mybir.dt.float32
BF16 = mybir.dt.bfloat16
AF = mybir.ActivationFunctionType
ALU = mybir.AluOpType
AX = mybir.AxisListType

D = 4096
KC = 32
CTX = 256
GRID_W = 64
W_BR = 1024
ROPE_THETA = 10000.0
NORM_EPS = 1e-6
O_AF, O_AZ, O_BF, O_BZ, O_CF, O_CZ, O_DF, O_DZ, O_G = 0, 3200, 4224, 7296, 8320, 11424, 12448, 13984, 15008
N_IN = 31392
import os
GSTOP = int(os.environ.get('GSTOP', '99'))
GLIM = int(os.environ.get('GLIM', '100000'))
GFWD = int(os.environ.get('GFWD', '0'))
GNL = int(os.environ.get('GNL', '6'))
GBLK = int(os.environ.get('GBLK', '1000'))
GSW = int(os.environ.get('GSW', '2'))
GBANKS = int(os.environ.get('GBANKS', '0'))


class Dep:
    __slots__ = ("w", "r", "sem", "cnt", "excl")

    def __init__(self, excl=False):
        self.w = {}
        self.r = {}
        self.sem = None
        self.cnt = 0
        self.excl = excl


def deps(n):
    return [Dep() for _ in range(n)]


class FW:
    def __init__(self, nc):
        self.nc = nc
        self.stack = ExitStack()
        self.stacks = [self.stack]
        self.E = {"pe": nc.tensor, "dve": nc.vector, "act": nc.scalar, "pool": nc.gpsimd, "sp": nc.sync}
        self.sems = []
        self.semi = {}
        self.cnt = {}
        self.waited = {k: {} for k in self.E}
        self.dma_deps = []
        for k in ("pe", "dve", "act", "pool"):
            self.semi[k] = self.newsem("e_" + k)
            self.cnt[k] = 0
        self.nt = 0

    def newsem(self, name):
        h = self.stack.enter_context(self.nc.semaphore(name))
        self.sems.append(h)
        return len(self.sems) - 1

    @contextmanager
    def scope(self):
        st = ExitStack()
        self.stacks.append(st)
        try:
            yield
        finally:
            self.stacks.pop()
            st.close()

    def sb(self, shape, dt, name=None):
        self.nt += 1
        return self.stacks[-1].enter_context(self.nc.sbuf_tensor(name or f"t{self.nt}", list(shape), dt))

    def ps(self, shape, dt=F32, name=None):
        self.nt += 1
        return self.stack.enter_context(self.nc.psum_tensor(name or f"p{self.nt}", list(shape), dt))

    def _collect(self, reads, writes):
        t = {}
        for d in reads:
            for s, v in d.w.items():
                if t.get(s, 0) < v:
                    t[s] = v
            if d.excl:
                for s, v in d.r.items():
                    if t.get(s, 0) < v:
                        t[s] = v
        for d in writes:
            for dd in (d.w, d.r):
                for s, v in dd.items():
                    if t.get(s, 0) < v:
                        t[s] = v
        return t

    def _wait(self, e, toks):
        w = self.waited[e]
        for s, v in toks.items():
            if e == "pe" and s == self.semi["pe"]:
                continue
            if w.get(s, 0) >= v:
                continue
            self.E[e].wait_ge(self.sems[s], v)
            w[s] = v

    def op(self, e, thunk, reads=(), writes=(), accw=()):
        self._wait(e, self._collect(reads, writes))
        ins = thunk(self.E[e])
        self.cnt[e] += 1
        s = self.semi[e]
        v = self.cnt[e]
        ins.then_inc(self.sems[s], 1)
        for d in reads:
            d.r[s] = v
        for d in writes:
            d.w = {s: v}
            d.r = {}
        for d in accw:
            d.w[s] = v

    def dma(self, q, out, in_, sdep, reads=(), writes=(), accw=()):
        self._wait(q, self._collect(reads, writes))
        ins = self.E[q].dma_start(out=out, in_=in_)
        if sdep.sem is None:
            sdep.sem = self.newsem(f"d{len(self.sems)}")
            self.dma_deps.append(sdep)
        sdep.cnt += 16
        ins.then_inc(self.sems[sdep.sem], 16)
        s, v = sdep.sem, sdep.cnt
        for d in reads:
            d.r[s] = v
        for d in writes:
            d.w = {s: v}
            d.r = {}
        for d in accw:
            d.w[s] = v

    def finish(self, e, dl):
        self._wait(e, self._collect(dl, ()))

    def barrier(self):
        t = {self.semi[k]: self.cnt[k] for k in self.cnt if self.cnt[k] > 0}
        for d in self.dma_deps:
            t[d.sem] = d.cnt
        for e in self.E:
            self._wait(e, t)


def _consts():
    c = {}
    c["ident"] = np.eye(128, dtype=np.float32)
    c["ones"] = np.ones((128, 128), np.float32)
    b = np.zeros((128, 128), np.float32)
    b[:64, :64] = 1
    b[64:, 64:] = 1
    c["blk64"] = b
    pd = np.zeros((128, 128), np.float32)
    for m in range(128):
        pd[m + 32 if (m % 64) < 32 else m - 32, m] = 1
    c["permD"] = pd
    pb = np.zeros((128, 128), np.float32)
    for m in range(128):
        pb[m + 16 if (m % 32) < 16 else m - 16, m] = 1
    c["permB"] = pb
    BIG = 30000.0
    i = np.arange(128)[:, None]
    j = np.arange(128)[None, :]

    def blk(m):
        return np.ascontiguousarray(m.astype(np.float32))
    c["UI"] = blk((i <= j).astype(np.float32))
    c["LI"] = blk((i >= j).astype(np.float32))
    c["SU"] = blk((i < j).astype(np.float32))
    c["SL"] = blk((i > j).astype(np.float32))
    c["POS_SL"] = blk(np.where(i > j, 0.0, BIG))
    c["POS_SU"] = blk(np.where(i < j, 0.0, BIG))
    c["NEG_SU"] = blk(np.where(i < j, 0.0, -BIG))
    c["NEG_SL"] = blk(np.where(i > j, 0.0, -BIG))
    c["NEG_UI"] = blk(np.where(i <= j, 0.0, -BIG))
    c["NEG_LI"] = blk(np.where(i >= j, 0.0, -BIG))
    return c


CST_NAMES = ["ident", "ones", "blk64", "permD", "permB", "UI", "LI", "SU", "SL", "POS_SL", "POS_SU", "NEG_SU", "NEG_SL", "NEG_UI", "NEG_LI"]


def _cst_array():
    c = _consts()
    return np.concatenate([c[n] for n in CST_NAMES], axis=1)


def _rope_tables(TL):
    t = np.arange(TL)
    rows = (t // GRID_W).astype(np.float32)
    cols = (t % GRID_W).astype(np.float32)

    def tab(head_dim, nrep):
        m = head_dim // 2
        inv = np.power(np.float32(ROPE_THETA), -np.arange(0, m, 2, dtype=np.float32) / np.float32(m)).astype(np.float32)
        q = head_dim // 4
        cos = np.zeros((head_dim, TL), np.float32)
        sin = np.zeros((head_dim, TL), np.float32)
        for p in range(head_dim):
            pos = rows if p < m else cols
            i = p % q
            ang = (pos * inv[i]).astype(np.float32)
            cos[p] = np.cos(ang)
            sg = -1.0 if (p % m) < q else 1.0
            sin[p] = sg * np.sin(ang)
        return np.tile(cos, (nrep, 1)), np.tile(sin, (nrep, 1))

    cd, sd = tab(128, 1)
    cb, sbb = tab(64, 2)
    return cd, sd, cb, sbb


def _fm(v):
    v = np.asarray(v, np.float32).reshape(-1, 128)
    return np.ascontiguousarray(v.T)


PV = {"qnD": 0, "knD": 1, "qnB": 2, "knB": 3, "subln": 4, "conv": 5, "gnorm": 14, "alog": 15, "dtb": 17, "mu": 19, "w0": 23, "a0": 25, "kk": 27, "ka": 28, "rk": 29, "lnw": 30, "lnb": 31}
NPV = 64
NCA_D = 512
NCA_B = 512
NCA_C = 516
NCA_A = 640
NCA = NCA_D + NCA_B + NCA_C + NCA_A


def prep_A(inp, l, xT_all, TL):
    cd, sd, cb, sbb = _rope_tables(TL)
    cst = _cst_array()
    cvec = np.stack([_fm(inp["c"][0]), _fm(inp["c_ctx"])], axis=2).reshape(128, 64)
    wmod = np.ascontiguousarray(inp["w_mod"][l][:, :2 * D])
    bmod = _fm(inp["b_mod"][l][:2 * D])
    normw = _fm(inp["norm_w"][l])
    w_in = inp["w_in"][l]
    lam_init = 0.8 - 0.6 * math.exp(-0.3 * l)
    lamv = np.concatenate([inp["diff_lam"][l].reshape(1, 256), np.full((1, 1), lam_init, np.float32)], axis=1).astype(np.float32)
    maps = []
    for c in range(8):
        n = c // 4
        cols = []
        cols += list(range(O_DF + 128 * c, O_DF + 128 * c + 128))
        cols += list(range(O_DF + 1024 + 128 * n, O_DF + 1024 + 128 * n + 128))
        cols += list(range(O_DF + 1280 + 128 * n, O_DF + 1280 + 128 * n + 128))
        cols += list(range(O_DZ + 128 * c, O_DZ + 128 * c + 128))
        for o in (O_BF, O_BF + 1024, O_BF + 2048, O_BZ):
            cols += list(range(o + 128 * c, o + 128 * c + 128))
        for o in (O_CF, O_CF + 1024, O_CF + 2048, O_CZ):
            cols += list(range(o + 128 * c, o + 128 * c + 128))
        cols += [O_CF + 3072 + c, O_CF + 3072 + 8 + c, O_CF + 3088 + c, O_CF + 3088 + 8 + c]
        for o in (O_AF, O_AF + 1024, O_AF + 2048, O_AZ):
            cols += list(range(o + 128 * c, o + 128 * c + 128))
        cols += list(range(O_AF + 3072, O_AF + 3200))
        wA = np.ascontiguousarray(w_in[:, cols])
        cs_ = slice(128 * c, 128 * c + 128)
        lora = np.concatenate([np.concatenate([inp["rwkv_w2"][l][d_][:, cs_], inp["rwkv_a2"][l][d_][:, cs_]], axis=0) for d_ in range(2)], axis=1)
        lora = np.ascontiguousarray(lora.astype(np.float32))
        pv = np.zeros((128, NPV), np.float32)
        pv[:, PV["qnD"]] = inp["gqa_qn"][l]
        pv[:, PV["knD"]] = inp["gqa_kn"][l]
        pv[:, PV["qnB"]] = np.tile(inp["diff_qn"][l], 2)
        pv[:, PV["knB"]] = np.tile(inp["diff_kn"][l], 2)
        pv[:, PV["subln"]] = inp["diff_subln"][l]
        for j in range(3):
            for i in range(3):
                pv[:, PV["conv"] + j * 3 + i] = inp["gdn_conv"][l][i, j * 1024 + 128 * c:j * 1024 + 128 * c + 128]
        pv[:, PV["gnorm"]] = inp["gdn_norm"][l]
        for d_ in range(2):
            pv[:, PV["alog"] + d_] = inp["gdn_a_log"][l][d_, c]
            pv[:, PV["dtb"] + d_] = inp["gdn_dt_bias"][l][d_, c]
            pv[:, PV["w0"] + d_] = inp["rwkv_w0"][l][d_, cs_]
            pv[:, PV["a0"] + d_] = inp["rwkv_a0"][l][d_, cs_]
        mu_ = inp["rwkv_mu"][l]
        for j in range(3):
            pv[:, PV["mu"] + j] = mu_[j * 1024 + 128 * c:j * 1024 + 128 * c + 128]
        pv[:, PV["mu"] + 3] = mu_[3072:3200]
        pv[:, PV["kk"]] = inp["rwkv_kk"][l][cs_]
        pv[:, PV["ka"]] = inp["rwkv_ka"][l][cs_]
        pv[:, PV["rk"]] = inp["rwkv_rk"][l].reshape(-1)[cs_]
        pv[:, PV["lnw"]] = inp["rwkv_ln_w"][l][cs_]
        pv[:, PV["lnb"]] = inp["rwkv_ln_b"][l][cs_]
        maps.append({"xT": xT_all, "cvec": cvec, "wmod": wmod, "bmod": bmod, "normw": normw, "wA": wA,
                     "cst": cst, "pvec": pv, "lamv": lamv, "lora": lora, "ropeDc": cd, "ropeDs": sd, "ropeBc": cb, "ropeBs": sbb})
    return maps


def emit_cond_norm(fw, P, dP, CB, d_cb, cvec, wmod, bmod, normw, xT_v, xnT_v, d_xnT, nblocks, NMOD, mod, d_mod):
    sc_ = fw.scope()
    sc_.__enter__()
    cv = fw.sb([128, 64], F32, "cv")
    d_cv = Dep()
    fw.dma("sp", cv[:], cvec[:, :], d_cv, writes=[d_cv])
    scv = fw.sb([128, 64], F32, "scv")
    d_scv = Dep()
    fw.op("act", lambda e: e.activation(out=scv[:], in_=cv[:], func=AF.Silu), reads=[d_cv], writes=[d_scv])
    bm = fw.sb([128, NMOD], F32, "bm")
    d_bm = Dep()
    fw.dma("sp", bm[:], bmod[:, :], d_bm, writes=[d_bm])
    nw = fw.sb([128, 32], F32, "nw")
    d_nw = Dep()
    fw.dma("sp", nw[:], normw[:, :], d_nw, writes=[d_nw])
    wm = [fw.sb([128, 32, 256], F32, f"wm{i}") for i in range(2)]
    d_wm = deps(2)
    wmod_v = wmod.rearrange("(kc p) n -> p kc n", p=128)
    modp = P[0]
    for j in range(NMOD // 2):
        b = j % 2
        for h in range(4):
            fw.dma("sp", wm[b][:, h * 8:(h + 1) * 8, :], wmod_v[:, h * 8:(h + 1) * 8, j * 256:(j + 1) * 256], d_wm[b],
                   writes=[d_wm[b]] if h == 0 else (), accw=() if h == 0 else [d_wm[b]])
        for cc in range(2):
            ch = j * 2 + cc
            for kc in range(32):
                fw.op("pe", lambda e: e.matmul(modp[:, ch * 2:ch * 2 + 2], lhsT=wm[b][:, kc, cc * 128:(cc + 1) * 128],
                                               rhs=scv[:, kc * 2:kc * 2 + 2], start=(kc == 0), stop=(kc == 31)),
                      reads=[d_wm[b], d_scv], writes=[dP[0]])
    modp_v = modp[:, 0:2 * NMOD].rearrange("p (c s) -> p s c", s=2)
    for s in range(2):
        fw.op("dve", lambda e: e.tensor_tensor(out=mod[:, s, :], in0=modp_v[:, s, :], in1=bm[:], op=ALU.add),
              reads=[dP[0], d_bm], writes=[d_mod])
    g = fw.sb([128, 2, 32], F32, "g")
    d_g = Dep()
    for s in range(2):
        fw.op("dve", lambda e: e.scalar_tensor_tensor(out=g[:, s, :], in0=mod[:, s, 32:64], scalar=1.0, in1=nw[:],
                                                      op0=ALU.add, op1=ALU.mult),
              reads=[d_mod, d_nw], writes=[d_g])

    NB = 256
    xb = [fw.sb([128, 32, NB], F32, f"xb{i}") for i in range(2)]
    d_xb = deps(2)
    xnb = [fw.sb([128, 32, NB], BF16, f"xnb{i}") for i in range(2)]
    d_xnb = [deps(32) for _ in range(2)]
    d_xnbo = deps(2)
    sq = [fw.sb([128, NB], BF16, f"sq{i}") for i in range(4)]
    d_sq = deps(4)
    rstd = [fw.sb([128, NB], F32, f"rstd{i}") for i in range(2)]
    d_rstd = deps(2)
    tmp = [fw.sb([128, NB], F32, f"tmp{i}") for i in range(4)]
    d_tmp = deps(4)
    for bi, (st, n, s) in enumerate(nblocks):
        b = bi % 2
        for h in range(4):
            fw.dma("sp", xb[b][:, h * 8:(h + 1) * 8, :], xT_v[:, h * 8:(h + 1) * 8, st:st + n], d_xb[b],
                   writes=[d_xb[b]] if h == 0 else (), accw=() if h == 0 else [d_xb[b]])
        ssp = P[1 + b]
        for kc in range(32):
            q = kc % 4
            fw.op("act", lambda e: e.activation(out=sq[q][:], in_=xb[b][:, kc, :], func=AF.Square),
                  reads=[d_xb[b]], writes=[d_sq[q]])
            fw.op("pe", lambda e: e.matmul(ssp[:, 0:NB], lhsT=CB("ones"), rhs=sq[q][:], start=(kc == 0), stop=(kc == 31)),
                  reads=[d_sq[q], d_cb], writes=[dP[1 + b]])
        fw.op("act", lambda e: e.activation(out=rstd[b][:], in_=ssp[:, 0:NB], func=AF.Sqrt, scale=1.0 / D, bias=NORM_EPS),
              reads=[dP[1 + b]], writes=[d_rstd[b]])
        fw.op("dve", lambda e: e.reciprocal(out=rstd[b][:], in_=rstd[b][:]), reads=[d_rstd[b]], writes=[d_rstd[b]])
        for kc in range(32):
            q = kc % 4
            fw.op("dve", lambda e: e.scalar_tensor_tensor(out=tmp[q][:], in0=xb[b][:, kc, :], scalar=g[:, s, kc:kc + 1],
                                                          in1=rstd[b][:], op0=ALU.mult, op1=ALU.mult),
                  reads=[d_xb[b], d_g, d_rstd[b]], writes=[d_tmp[q]])
            fw.op("act", lambda e: e.activation(out=xnb[b][:, kc, :], in_=tmp[q][:], func=AF.Identity,
                                                bias=mod[:, s, kc:kc + 1]),
                  reads=[d_tmp[q], d_mod], writes=[d_xnb[b][kc]])
        fw.dma("pool", xnT_v[:, :, st:st + n], xnb[b][:], d_xnbo[b], reads=d_xnb[b], accw=[d_xnT])
    fw.barrier()

    sc_.__exit__(None, None, None)


def build_A(TL, do=("d", "b"), lam_l=0):
    TA = CTX + TL
    NKC = TA // 128
    nc = bass.Bass("TRN2", target_bir_lowering=False)
    fw = FW(nc)

    def din(n, s, dt=F32):
        return nc.dram_tensor(n, list(s), dt, kind="ExternalInput").ap()

    xT = din("xT", [D, TA])
    cvec = din("cvec", [128, 64])
    wmod = din("wmod", [D, 2 * D])
    bmod = din("bmod", [128, 64])
    normw = din("normw", [128, 32])
    wA = din("wA", [D, NCA])
    cst = din("cst", [128, 128 * len(CST_NAMES)])
    pvec = din("pvec", [128, NPV])
    lamv = din("lamv", [1, 257])
    lora = din("lora", [128, 256])
    ropeDc = din("ropeDc", [128, TL])
    ropeDs = din("ropeDs", [128, TL])
    ropeBc = din("ropeBc", [128, TL])
    ropeBs = din("ropeBs", [128, TL])
    uT = nc.dram_tensor("uT", [512, TA], BF16, kind="ExternalOutput").ap()
    DBG = int(os.environ.get("GDBG", "0"))
    dbg_map = {}
    if DBG:
        dbgT = nc.dram_tensor("dbg", [128, 4096], F32, kind="ExternalOutput").ap()
    dbg_state = {"col": 0}
    d_dbg = Dep()

    def dbg_dump(name, ap, dep, rows, cols):
        if not DBG:
            return
        c0_ = dbg_state["col"]
        if cols == 1:
            c0_ += c0_ % 2
            cols2 = 2
            fw.dma("sp", dbgT[0:rows, c0_:c0_ + 1], ap, dep, reads=[dep], accw=[d_dbg]) if False else None
            ins_ = fw.E["sp"]
            fw._wait("sp", fw._collect([dep], ()))
            i_ = ins_.dma_start(out=dbgT[0:rows, c0_:c0_ + 1], in_=ap, allow_slow_non_contiguous=True)
            if dep.sem is None:
                dep.sem = fw.newsem(f"d{len(fw.sems)}")
                fw.dma_deps.append(dep)
            dep.cnt += 16
            i_.then_inc(fw.sems[dep.sem], 16)
            dep.r[dep.sem] = dep.cnt
            d_dbg.w[dep.sem] = dep.cnt
            cols = 2
        else:
            fw.dma("sp", dbgT[0:rows, c0_:c0_ + cols], ap, dep, reads=[dep], accw=[d_dbg])
        dbg_map[name] = (rows, c0_, 1 if name.endswith(("Gc", "eGl")) else cols)
        dbg_state["col"] = c0_ + cols
    build_A.dbg_map = dbg_map
    xnT = nc.dram_tensor("xnT", [D, TA], BF16).ap()
    d_uT = Dep()
    d_xnT = Dep()
    xT_v = xT.rearrange("(kc p) t -> p kc t", p=128)
    xnT_v = xnT.rearrange("(kc p) t -> p kc t", p=128)
    wA_v = wA.rearrange("(kc p) n -> p kc n", p=128)

    cf = fw.sb([128, 128 * len(CST_NAMES)], F32, "cf")
    d_cf = Dep()
    fw.dma("sp", cf[:], cst[:, :], d_cf, writes=[d_cf])
    cb16 = fw.sb([128, 128 * len(CST_NAMES)], BF16, "cb16")
    d_cb = Dep()
    fw.op("dve", lambda e: e.tensor_copy(out=cb16[:], in_=cf[:]), reads=[d_cf], writes=[d_cb])

    def CF(n):
        i = CST_NAMES.index(n)
        return cf[:, i * 128:(i + 1) * 128]

    def CB(n):
        i = CST_NAMES.index(n)
        return cb16[:, i * 128:(i + 1) * 128]

    pvs = fw.sb([128, NPV], F32, "pvs")
    d_pv = Dep()
    fw.dma("sp", pvs[:], pvec[:, :], d_pv, writes=[d_pv])

    P = [fw.ps([128, 512], F32, f"bank{i}") for i in range(8)]
    dP = [Dep(excl=True) for _ in range(8)]

    mod = fw.sb([128, 2, 64], F32, "mod")
    d_mod = Dep()
    NBN = 256
    nblocks = [(0, CTX, 1)] + [(CTX + i * NBN, NBN, 0) for i in range(TL // NBN)]
    emit_cond_norm(fw, P, dP, CB, d_cb, cvec, wmod, bmod, normw, xT_v, xnT_v, d_xnT, nblocks, 64, mod, d_mod)

    _sc2 = fw.scope()
    _sc2.__enter__()
    TB = 512
    blocks = [(0, CTX, 1)] + [(CTX + i * TB, TB, 0) for i in range(TL // TB)]
    Wb = fw.sb([128, 32, 512], BF16, "Wb")
    d_W = Dep()
    xk = [fw.sb([128, 32, TB], BF16, "xk0")] * 2
    d_xk = [Dep()] * 2
    qT = fw.sb([128, TA], BF16, "qT")
    kT = fw.sb([128, TA], BF16, "kT")
    vA = fw.sb([128, NKC, 128], BF16, "vA")
    szD = nc.dram_tensor("szD", [128, TA], BF16).ap()
    szst = [fw.sb([128, TB], BF16, f"szst{i}") for i in range(2)]
    d_szst = deps(2)
    d_q, d_k, d_v, d_sz = Dep(), Dep(), Dep(), Dep()
    cosb = [fw.sb([128, TB], F32, "cosb0")] * 2
    sinb = [fw.sb([128, TB], F32, "sinb0")] * 2
    d_cos = [Dep()] * 2
    d_sin = [Dep()] * 2
    sqb = fw.sb([128, TB], BF16, "sqb")
    d_sqb = Dep()
    rs = fw.sb([128, TB], F32, "rs")
    d_rs = Dep()
    qn = fw.sb([128, TB], F32, "qn")
    d_qn = Dep()
    t1 = fw.sb([128, TB], F32, "t1")
    t2 = fw.sb([128, TB], F32, "t2")
    d_t1, d_t2 = Dep(), Dep()
    pT = [fw.sb([128, TB], BF16, f"pT{i}") for i in range(4)]
    d_pT = deps(4)
    ust = [fw.sb([128, TB], BF16, f"ust{i}") for i in range(2)]
    d_ust = deps(2)
    lam_sb = fw.sb([128, 257], F32, "lam_sb")
    d_lam = Dep()
    lamw = fw.sb([128, 8], F32, "lamw")
    d_lamw = Dep()

    def load_W(c0, ncol):
        for h in range(8):
            fw.dma("pool", Wb[:, h * 4:(h + 1) * 4, 0:ncol], wA_v[:, h * 4:(h + 1) * 4, c0:c0 + ncol], d_W,
                   writes=[d_W] if h == 0 else (), accw=() if h == 0 else [d_W])

    def load_x(bi, st, n):
        b = bi % 2
        for h in range(2):
            fw.dma("sp", xk[b][:, h * 16:(h + 1) * 16, 0:n], xnT_v[:, h * 16:(h + 1) * 16, st:st + n], d_xk[b],
                   reads=[d_xnT], writes=[d_xk[b]] if h == 0 else (), accw=() if h == 0 else [d_xk[b]])
        return b

    def proj_fm(bank, c0, ncol, b, n):
        for kc in range(32):
            fw.op("pe", lambda e: e.matmul(P[bank][0:ncol, 0:n], lhsT=Wb[:, kc, c0:c0 + ncol], rhs=xk[b][:, kc, 0:n],
                                           start=(kc == 0), stop=(kc == 31)),
                  reads=[d_W, d_xk[b]], writes=[dP[bank]])

    def proj_tm(bank, c0, ncol, b, t0, nt):
        for kc in range(32):
            fw.op("pe", lambda e: e.matmul(P[bank][0:nt, 0:ncol], lhsT=xk[b][:, kc, t0:t0 + nt], rhs=Wb[:, kc, c0:c0 + ncol],
                                           start=(kc == 0), stop=(kc == 31)),
                  reads=[d_W, d_xk[b]], writes=[dP[bank]])

    def headnorm_rope(bank, n, wcol, ones_name, hd, perm, cosT, sinT, st, rope, dst, d_dst, bi):
        fw.op("act", lambda e: e.activation(out=sqb[:, 0:n], in_=P[bank][:, 0:n], func=AF.Square),
              reads=[dP[bank]], writes=[d_sqb])
        fw.op("pe", lambda e: e.matmul(P[3][:, 0:n], lhsT=CB(ones_name), rhs=sqb[:, 0:n], start=True, stop=True),
              reads=[d_sqb, d_cb], writes=[dP[3]])
        fw.op("act", lambda e: e.activation(out=rs[:, 0:n], in_=P[3][:, 0:n], func=AF.Sqrt, scale=1.0 / hd, bias=NORM_EPS),
              reads=[dP[3]], writes=[d_rs])
        fw.op("dve", lambda e: e.reciprocal(out=rs[:, 0:n], in_=rs[:, 0:n]), reads=[d_rs], writes=[d_rs])
        if not rope:
            fw.op("dve", lambda e: e.scalar_tensor_tensor(out=dst[:, st:st + n], in0=P[bank][:, 0:n], scalar=pvs[:, wcol:wcol + 1],
                                                          in1=rs[:, 0:n], op0=ALU.mult, op1=ALU.mult),
                  reads=[dP[bank], d_pv, d_rs], accw=[d_dst])
            return
        fw.op("dve", lambda e: e.scalar_tensor_tensor(out=qn[:, 0:n], in0=P[bank][:, 0:n], scalar=pvs[:, wcol:wcol + 1],
                                                      in1=rs[:, 0:n], op0=ALU.mult, op1=ALU.mult),
              reads=[dP[bank], d_pv, d_rs], writes=[d_qn])
        fw.op("pe", lambda e: e.matmul(P[3][:, 0:n], lhsT=CF(perm), rhs=qn[:, 0:n], start=True, stop=True),
              reads=[d_qn, d_cf], writes=[dP[3]])
        cb_ = bi % 2
        fw.op("dve", lambda e: e.tensor_tensor(out=t1[:, 0:n], in0=qn[:, 0:n], in1=cosb[cb_][:, 0:n], op=ALU.mult),
              reads=[d_qn, d_cos[cb_]], writes=[d_t1])
        fw.op("dve", lambda e: e.tensor_tensor(out=t2[:, 0:n], in0=P[3][:, 0:n], in1=sinb[cb_][:, 0:n], op=ALU.mult),
              reads=[dP[3], d_sin[cb_]], writes=[d_t2])
        fw.op("dve", lambda e: e.tensor_tensor(out=dst[:, st:st + n], in0=t1[:, 0:n], in1=t2[:, 0:n], op=ALU.add),
              reads=[d_t1, d_t2], accw=[d_dst])

    def attn_pass(name, c0, hd, ones_name, perm, ropec, ropes, wq, wk, nsub, urow):
        load_W(c0, 512)
        for bi, (st, n, s) in enumerate(blocks):
            b = load_x(bi, st, n)
            rope = (s == 0)
            if rope:
                cb_ = bi % 2
                fw.dma("sp", cosb[cb_][:, 0:n], ropec[:, st - CTX:st - CTX + n], d_cos[cb_], writes=[d_cos[cb_]])
                fw.dma("sp", sinb[cb_][:, 0:n], ropes[:, st - CTX:st - CTX + n], d_sin[cb_], writes=[d_sin[cb_]])
            proj_fm(0, 0, 128, b, n)
            headnorm_rope(0, n, wq, ones_name, hd, perm, None, None, st, rope, qT, d_q, bi)
            proj_fm(1, 128, 128, b, n)
            headnorm_rope(1, n, wk, ones_name, hd, perm, None, None, st, rope, kT, d_k, bi)
            proj_fm(2, 384, 128, b, n)
            zb_ = bi % 2
            fw.op("act", lambda e: e.activation(out=szst[zb_][:, 0:n], in_=P[2][:, 0:n], func=AF.Silu),
                  reads=[dP[2]], writes=[d_szst[zb_]])
            fw.dma("sp", szD[:, st:st + n], szst[zb_][:, 0:n], d_szst[zb_], reads=[d_szst[zb_]], accw=[d_sz])
            for j in range(n // 128):
                bank = 4 + (j % 2)
                proj_tm(bank, 256, 128, b, j * 128, 128)
                fw.op("dve", lambda e: e.tensor_copy(out=vA[:, st // 128 + j, :], in_=P[bank][:, 0:128]),
                      reads=[dP[bank]], accw=[d_v])
        fw.barrier()
        scale = float(hd) ** -0.5
        if nsub == 2:
            fw.dma("sp", lam_sb[:], lamv[0:1, :].partition_broadcast(128), d_lam, writes=[d_lam])
            fw.op("dve", lambda e: e.tensor_tensor(out=t1[:, 0:64], in0=lam_sb[:, 0:64], in1=lam_sb[:, 64:128], op=ALU.mult),
                  reads=[d_lam], writes=[d_t1])
            fw.op("dve", lambda e: e.reduce_sum(out=lamw[:, 0:1], in_=t1[:, 0:64], axis=AX.X), reads=[d_t1], writes=[d_lamw])
            fw.op("dve", lambda e: e.tensor_tensor(out=t1[:, 0:64], in0=lam_sb[:, 128:192], in1=lam_sb[:, 192:256], op=ALU.mult),
                  reads=[d_lam], writes=[d_t1])
            fw.op("dve", lambda e: e.reduce_sum(out=lamw[:, 1:2], in_=t1[:, 0:64], axis=AX.X), reads=[d_t1], writes=[d_lamw])
            fw.op("act", lambda e: e.activation(out=lamw[:, 2:4], in_=lamw[:, 0:2], func=AF.Exp), reads=[d_lamw], writes=[d_lamw])
            fw.op("dve", lambda e: e.tensor_tensor(out=lamw[:, 4:5], in0=lamw[:, 2:3], in1=lamw[:, 3:4], op=ALU.subtract),
                  reads=[d_lamw], writes=[d_lamw])
            fw.op("dve", lambda e: e.tensor_tensor(out=lamw[:, 4:5], in0=lamw[:, 4:5], in1=lam_sb[:, 256:257], op=ALU.add),
                  reads=[d_lamw, d_lam], writes=[d_lamw])
            fw.op("dve", lambda e: e.tensor_scalar(out=lamw[:, 5:6], in0=lamw[:, 4:5], scalar1=-1.0, scalar2=None, op0=ALU.mult),
                  reads=[d_lamw], writes=[d_lamw])
            fw.op("dve", lambda e: e.tensor_scalar(out=lamw[:, 6:7], in0=lam_sb[:, 256:257], scalar1=-1.0, scalar2=1.0,
                                                   op0=ALU.mult, op1=ALU.add), reads=[d_lam, d_lamw], writes=[d_lamw])
            fw.op("dve", lambda e: e.tensor_tensor(out=lamw[:, 6:7], in0=lamw[:, 6:7], in1=pvs[:, PV["subln"]:PV["subln"] + 1], op=ALU.mult),
                  reads=[d_lamw, d_pv], writes=[d_lamw])
        hs = 128 // nsub
        for bi, (st, n, s) in enumerate(blocks):
            kcs = [0, 1] if s == 1 else list(range(NKC))
            nk = len(kcs)
            def S(i):
                for m in range(nsub):
                    bank = m * 2 + (i % 2)
                    kc = kcs[i]
                    fw.op("pe", lambda e: e.matmul(P[bank][:, 0:n], lhsT=kT[m * hs:(m + 1) * hs, kc * 128:(kc + 1) * 128],
                                                   rhs=qT[m * hs:(m + 1) * hs, st:st + n], start=True, stop=True),
                          reads=[d_k, d_q], writes=[dP[bank]])
            S(0)
            for i in range(nk):
                if i + 1 < nk:
                    S(i + 1)
                kc = kcs[i]
                for m in range(nsub):
                    bank = m * 2 + (i % 2)
                    pi = (i * nsub + m) % 4
                    fw.op("act", lambda e: e.activation(out=pT[pi][:, 0:n], in_=P[bank][:, 0:n], func=AF.Exp, scale=scale),
                          reads=[dP[bank]], writes=[d_pT[pi]])
                    fw.op("pe", lambda e: e.matmul(P[4 + m][:, 0:n], lhsT=vA[:, kc, :], rhs=pT[pi][:, 0:n], start=(i == 0), stop=(i == nk - 1)),
                          reads=[d_v, d_pT[pi]], writes=[dP[4 + m]])
                    fw.op("pe", lambda e: e.matmul(P[6 + m][:, 0:n], lhsT=CB("ones"), rhs=pT[pi][:, 0:n], start=(i == 0), stop=(i == nk - 1)),
                          reads=[d_cb, d_pT[pi]], writes=[dP[6 + m]])
            ub = bi % 2
            if nsub == 1:
                fw.op("dve", lambda e: e.reciprocal(out=rs[:, 0:n], in_=P[6][:, 0:n]), reads=[dP[6]], writes=[d_rs])
                fw.op("dve", lambda e: e.tensor_tensor(out=t1[:, 0:n], in0=P[4][:, 0:n], in1=rs[:, 0:n], op=ALU.mult),
                      reads=[dP[4], d_rs], writes=[d_t1])
            else:
                fw.op("dve", lambda e: e.reciprocal(out=rs[:, 0:n], in_=P[6][:, 0:n]), reads=[dP[6]], writes=[d_rs])
                fw.op("dve", lambda e: e.tensor_tensor(out=t1[:, 0:n], in0=P[4][:, 0:n], in1=rs[:, 0:n], op=ALU.mult),
                      reads=[dP[4], d_rs], writes=[d_t1])
                fw.op("dve", lambda e: e.reciprocal(out=rs[:, 0:n], in_=P[7][:, 0:n]), reads=[dP[7]], writes=[d_rs])
                fw.op("dve", lambda e: e.tensor_tensor(out=t2[:, 0:n], in0=P[5][:, 0:n], in1=rs[:, 0:n], op=ALU.mult),
                      reads=[dP[5], d_rs], writes=[d_t2])
                fw.op("dve", lambda e: e.scalar_tensor_tensor(out=t1[:, 0:n], in0=t2[:, 0:n], scalar=lamw[:, 5:6], in1=t1[:, 0:n],
                                                              op0=ALU.mult, op1=ALU.add), reads=[d_t2, d_lamw, d_t1], writes=[d_t1])
                fw.op("act", lambda e: e.activation(out=sqb[:, 0:n], in_=t1[:, 0:n], func=AF.Square), reads=[d_t1], writes=[d_sqb])
                fw.op("pe", lambda e: e.matmul(P[0][:, 0:n], lhsT=CB("ones"), rhs=sqb[:, 0:n], start=True, stop=True),
                      reads=[d_sqb, d_cb], writes=[dP[0]])
                fw.op("act", lambda e: e.activation(out=rs[:, 0:n], in_=P[0][:, 0:n], func=AF.Sqrt, scale=1.0 / 128, bias=NORM_EPS),
                      reads=[dP[0]], writes=[d_rs])
                fw.op("dve", lambda e: e.reciprocal(out=rs[:, 0:n], in_=rs[:, 0:n]), reads=[d_rs], writes=[d_rs])
                fw.op("dve", lambda e: e.scalar_tensor_tensor(out=t1[:, 0:n], in0=t1[:, 0:n], scalar=lamw[:, 6:7], in1=rs[:, 0:n],
                                                              op0=ALU.mult, op1=ALU.mult), reads=[d_t1, d_lamw, d_rs], writes=[d_t1])
            fw.dma("sp", szst[ub][:, 0:n], szD[:, st:st + n], d_szst[ub], reads=[d_sz], writes=[d_szst[ub]])
            fw.op("dve", lambda e: e.tensor_tensor(out=ust[ub][:, 0:n], in0=t1[:, 0:n], in1=szst[ub][:, 0:n], op=ALU.mult),
                  reads=[d_t1, d_szst[ub]], writes=[d_ust[ub]])
            fw.dma("sp", uT[urow:urow + 128, st:st + n], ust[ub][:, 0:n], d_ust[ub], reads=[d_ust[ub]], accw=[d_uT])
        fw.barrier()

    if "d" in do:
        attn_pass("d", 0, 128, "ones", "permD", ropeDc, ropeDs, PV["qnD"], PV["knD"], 1, 384)
    if "b" in do:
        attn_pass("b", NCA_D, 64, "blk64", "permB", ropeBc, ropeBs, PV["qnB"], PV["knB"], 2, 128)

    _sc2.__exit__(None, None, None)

    def T(shape, dt, name):
        return fw.sb(shape, dt, name), Dep()

    def psr(bank, r0, r1, c0, c1):
        return P[bank][r0:r1, c0:c1]

    def gdn_pass():
        c0 = NCA_D + NCA_B
        sc = fw.scope()
        sc.__enter__()
        Wg, d_Wg = T([128, 32, NCA_C], BF16, "Wg")
        for h in range(8):
            fw.dma("pool", Wg[:, h * 4:(h + 1) * 4, :], wA_v[:, h * 4:(h + 1) * 4, c0:c0 + NCA_C], d_Wg,
                   writes=[d_Wg] if h == 0 else (), accw=() if h == 0 else [d_Wg])
        xg = [fw.sb([128, 32, TB + 2], BF16, "xg0")] * 2
        d_xg = [Dep()] * 2
        oT_all, d_oT = T([128, TA], F32, "oT_all")
        raw = fw.sb([128, 3, TB + 2], F32, "raw")
        d_raw = deps(3)
        cvt, d_cvt = T([128, TB], F32, "cvt")
        act3 = fw.sb([128, 3, TB], F32, "act3")
        d_act3 = deps(3)
        sqg, d_sqg = T([128, TB], BF16, "sqg")
        rsg, d_rsg = T([128, TB], F32, "rsg")
        bgs, d_bgs = T([128, 4, 4], F32, "bgs")
        negA, d_negA = T([128, 2], F32, "negA")
        S, d_S = T([128, 128], F32, "S")
        Stmp, d_Stmp = T([128, 128], F32, "Stmp")
        gB, d_gB = T([128, 128], F32, "gB")
        bB, d_bB = T([128, 128], F32, "bB")
        Gc, d_Gc = T([128, 1], F32, "Gc")
        Gl, d_Gl = T([128, 1], F32, "Gl")
        eGl, d_eGl = T([128, 1], F32, "eGl")
        eGbc, d_eGbc = T([128, 128], F32, "eGbc")
        scl, d_scl = T([128, 4], F32, "scl")
        X1, d_X1 = T([128, 128], F32, "X1")
        X2, d_X2 = T([128, 128], F32, "X2")
        X3, d_X3 = T([128, 128], F32, "X3")
        A_sb = [fw.sb([128, 128], F32, f"A_sb{i}") for i in range(2)]
        B_sb = [fw.sb([128, 128], F32, f"B_sb{i}") for i in range(2)]
        d_A = deps(2)
        d_B = deps(2)
        attnT, d_attnT = T([128, 128], F32, "attnT")
        M_sb, d_M = T([128, 128], F32, "M_sb")
        Ru, d_Ru = T([128, 128], F32, "Ru")
        Rw, d_Rw = T([128, 128], F32, "Rw")
        kdec, d_kdec = T([128, 128], F32, "kdec")
        u_sb, d_u = T([128, 128], F32, "u_sb")
        wT_sb, d_wT = T([128, 128], F32, "wT_sb")
        qgT, d_qgT = T([128, 128], F32, "qgT")
        vnew, d_vnew = T([128, 128], F32, "vnew")
        szb, d_szb = T([128, TB], F32, "szb")
        ustg = [fw.sb([128, TB], BF16, f"ustg{i}") for i in range(2)]
        d_ustg = deps(2)
        dph = dpbg = dP[1]
        dpss = dP[2]
        ident = CF("ident")
        ones = CF("ones")

        fw.op("act", lambda e: e.activation(out=negA[:, 0:2], in_=pvs[:, PV["alog"]:PV["alog"] + 2], func=AF.Exp),
              reads=[d_pv], writes=[d_negA])
        fw.op("dve", lambda e: e.tensor_scalar(out=negA[:, 0:2], in0=negA[:, 0:2], scalar1=-1.0, scalar2=None, op0=ALU.mult),
              reads=[d_negA], writes=[d_negA])

        def chunk(ci, st, dirn):
            cs = ci * 128
            t0 = st + cs
            gcol = bgs[:, ci, 1:2]
            bcol = bgs[:, ci, 0:1]
            qfc = act3[:, 0, cs:cs + 128]
            kfc = act3[:, 1, cs:cs + 128]
            vfc = act3[:, 2, cs:cs + 128]
            if dirn == 0:
                TRI, POSD, NEGS, NEGI, last = CF("UI"), CF("POS_SL"), CF("NEG_SU"), CF("NEG_UI"), 127
            else:
                TRI, POSD, NEGS, NEGI, last = CF("LI"), CF("POS_SU"), CF("NEG_SL"), CF("NEG_LI"), 0
            fw.op("dve", lambda e: e.tensor_scalar(out=gB[:, :], in0=ones, scalar1=gcol, scalar2=None, op0=ALU.mult),
                  reads=[d_cf, d_bgs], writes=[d_gB])
            fw.op("dve", lambda e: e.tensor_scalar(out=bB[:, :], in0=ones, scalar1=bcol, scalar2=None, op0=ALU.mult),
                  reads=[d_cf, d_bgs], writes=[d_bB])
            Gbc = psr(3, 0, 128, 0, 128)
            bbc = psr(3, 0, 128, 128, 256)
            KK = psr(3, 0, 128, 256, 384)
            QKT = psr(3, 0, 128, 384, 512)
            ktok = psr(4, 0, 128, 0, 128)
            vtok = psr(4, 0, 128, 128, 256)
            Gcol = psr(4, 0, 128, 256, 257)
            fw.op("pe", lambda e: e.matmul(Gbc, lhsT=gB[:, :], rhs=TRI, start=True, stop=True), reads=[d_gB, d_cf], writes=[dP[3]])
            fw.op("pe", lambda e: e.matmul(bbc, lhsT=bB[:, :], rhs=ident, start=True, stop=True), reads=[d_bB, d_cf], writes=[dP[3]])
            fw.op("pe", lambda e: e.matmul(KK, lhsT=kfc, rhs=kfc, start=True, stop=True), reads=[d_act3[1]], writes=[dP[3]])
            fw.op("pe", lambda e: e.matmul(QKT, lhsT=kfc, rhs=qfc, start=True, stop=True), reads=[d_act3[1], d_act3[0]], writes=[dP[3]])
            fw.op("pe", lambda e: e.transpose(ktok, kfc, ident), reads=[d_act3[1], d_cf], writes=[dP[4]])
            fw.op("pe", lambda e: e.transpose(vtok, vfc, ident), reads=[d_act3[2], d_cf], writes=[dP[4]])
            fw.op("pe", lambda e: e.matmul(Gcol, lhsT=TRI, rhs=gcol, start=True, stop=True), reads=[d_bgs, d_cf], writes=[dP[4]])
            if GSTOP <= 4:
                return
            fw.op("act", lambda e: e.activation(out=Gc[:, :], in_=Gcol, func=AF.Copy), reads=[dP[4]], writes=[d_Gc])
            fw.op("act", lambda e: e.activation(out=Gl[:, :], in_=Gbc[:, last:last + 1], func=AF.Copy), reads=[dP[3]], writes=[d_Gl])
            fw.op("act", lambda e: e.activation(out=eGl[:, :], in_=Gl[:, :], func=AF.Exp), reads=[d_Gl], writes=[d_eGl])
            fw.op("act", lambda e: e.activation(out=eGbc[:, :], in_=Gbc, func=AF.Exp), reads=[dP[3]], writes=[d_eGbc])
            fw.op("act", lambda e: e.activation(out=scl[:, 0:1], in_=Gc[:, :], func=AF.Exp), reads=[d_Gc], writes=[d_scl])
            fw.op("act", lambda e: e.activation(out=scl[:, 1:2], in_=Gc[:, :], func=AF.Exp, scale=-1.0, bias=Gl[:, 0:1]),
                  reads=[d_Gc, d_Gl, d_scl], writes=[d_scl])
            fw.op("dve", lambda e: e.tensor_tensor(out=scl[:, 2:3], in0=scl[:, 0:1], in1=bcol, op=ALU.mult),
                  reads=[d_scl, d_bgs], writes=[d_scl])
            fw.op("dve", lambda e: e.scalar_tensor_tensor(out=X1[:, :], in0=Gbc, scalar=Gc[:, 0:1], in1=POSD,
                                                          op0=ALU.subtract, op1=ALU.add), reads=[dP[3], d_Gc, d_cf], writes=[d_X1])
            fw.op("act", lambda e: e.activation(out=X1[:, :], in_=X1[:, :], func=AF.Exp, scale=-1.0), reads=[d_X1], writes=[d_X1])
            fw.op("dve", lambda e: e.scalar_tensor_tensor(out=X2[:, :], in0=Gbc, scalar=Gc[:, 0:1], in1=NEGS,
                                                          op0=ALU.subtract, op1=ALU.add), reads=[dP[3], d_Gc, d_cf], writes=[d_X2])
            fw.op("act", lambda e: e.activation(out=X2[:, :], in_=X2[:, :], func=AF.Exp), reads=[d_X2], writes=[d_X2])
            fw.op("dve", lambda e: e.scalar_tensor_tensor(out=X3[:, :], in0=Gbc, scalar=Gc[:, 0:1], in1=NEGI,
                                                          op0=ALU.subtract, op1=ALU.add), reads=[dP[3], d_Gc, d_cf], writes=[d_X3])
            fw.op("act", lambda e: e.activation(out=X3[:, :], in_=X3[:, :], func=AF.Exp), reads=[d_X3], writes=[d_X3])
            fw.op("dve", lambda e: e.scalar_tensor_tensor(out=A_sb[0][:, :], in0=KK, scalar=bcol, in1=X1[:, :], op0=ALU.mult, op1=ALU.mult),
                  reads=[dP[3], d_bgs, d_X1], writes=[d_A[0]])
            fw.op("dve", lambda e: e.tensor_tensor(out=B_sb[0][:, :], in0=KK, in1=X2[:, :], op=ALU.mult), reads=[dP[3], d_X2], writes=[d_B[0]])
            fw.op("dve", lambda e: e.tensor_tensor(out=B_sb[0][:, :], in0=B_sb[0][:, :], in1=bbc, op=ALU.mult), reads=[d_B[0], dP[3]], writes=[d_B[0]])
            fw.op("dve", lambda e: e.tensor_tensor(out=attnT[:, :], in0=QKT, in1=X3[:, :], op=ALU.mult), reads=[dP[3], d_X3], writes=[d_attnT])
            fw.op("dve", lambda e: e.tensor_tensor(out=M_sb[:, :], in0=ident, in1=B_sb[0][:, :], op=ALU.subtract),
                  reads=[d_cf, d_B[0]], writes=[d_M])
            if GSTOP <= 5:
                return
            cur = 0
            NL = GNL
            for lvl in range(NL):
                nxt = 1 - cur
                if GBANKS:
                    A2, B2, MM = psr(5, 0, 128, 0, 128), psr(2, 0, 128, 0, 128), psr(0, 0, 128, 0, 128)
                    dA2, dB2, dMM = dP[5], dP[2], dP[0]
                else:
                    A2, B2, MM = psr(5, 0, 128, 0, 128), psr(5, 0, 128, 128, 256), psr(5, 0, 128, 256, 384)
                    dA2 = dB2 = dMM = dP[5]
                eA = "dve" if GBANKS == 2 else "act"
                fw.op("pe", lambda e: e.matmul(A2, lhsT=B_sb[cur][:, :], rhs=A_sb[cur][:, :], start=True, stop=True),
                      reads=[d_A[cur], d_B[cur]], writes=[dA2])
                if lvl < NL - 1:
                    fw.op("pe", lambda e: e.matmul(B2, lhsT=A_sb[cur][:, :], rhs=B_sb[cur][:, :], start=True, stop=True),
                          reads=[d_A[cur], d_B[cur]], writes=[dB2])
                if eA == "act":
                    fw.op("act", lambda e: e.activation(out=A_sb[nxt][:, :], in_=A2, func=AF.Copy), reads=[dA2], writes=[d_A[nxt]])
                else:
                    fw.op("dve", lambda e: e.tensor_copy(out=A_sb[nxt][:, :], in_=A2), reads=[dA2], writes=[d_A[nxt]])
                if lvl < NL - 1:
                    fw.op("dve", lambda e: e.tensor_copy(out=B_sb[nxt][:, :], in_=B2), reads=[dB2], writes=[d_B[nxt]])
                fw.op("pe", lambda e: e.matmul(MM, lhsT=A_sb[nxt][:, :], rhs=M_sb[:, :], start=True, stop=True),
                      reads=[d_A[nxt], d_M], writes=[dMM])
                fw.op("dve", lambda e: e.tensor_tensor(out=M_sb[:, :], in0=M_sb[:, :], in1=MM, op=ALU.add), reads=[d_M, dMM], writes=[d_M])
                cur = nxt
            if GSTOP <= 6:
                return
            fw.op("dve", lambda e: e.tensor_scalar(out=Ru[:, :], in0=vtok, scalar1=bcol, scalar2=None, op0=ALU.mult),
                  reads=[dP[4], d_bgs], writes=[d_Ru])
            fw.op("act", lambda e: e.activation(out=Rw[:, :], in_=ktok, func=AF.Copy, scale=scl[:, 2:3]), reads=[dP[4], d_scl], writes=[d_Rw])
            fw.op("act", lambda e: e.activation(out=kdec[:, :], in_=ktok, func=AF.Copy, scale=scl[:, 1:2]), reads=[dP[4], d_scl], writes=[d_kdec])
            ups = psr(6, 0, 128, 0, 128)
            wTps = psr(6, 0, 128, 128, 256)
            vnps = psr(6, 0, 128, 256, 384)
            fw.op("pe", lambda e: e.matmul(ups, lhsT=M_sb[:, :], rhs=Ru[:, :], start=True, stop=True), reads=[d_M, d_Ru], writes=[dP[6]])
            fw.op("pe", lambda e: e.matmul(wTps, lhsT=Rw[:, :], rhs=M_sb[:, :], start=True, stop=True), reads=[d_M, d_Rw], writes=[dP[6]])
            fw.op("act", lambda e: e.activation(out=u_sb[:, :], in_=ups, func=AF.Copy), reads=[dP[6]], writes=[d_u])
            fw.op("dve", lambda e: e.tensor_copy(out=wT_sb[:, :], in_=wTps), reads=[dP[6]], writes=[d_wT])
            fw.op("dve", lambda e: e.tensor_tensor(out=qgT[:, :], in0=qfc, in1=eGbc[:, :], op=ALU.mult), reads=[d_act3[0], d_eGbc], writes=[d_qgT])
            if GSTOP <= 7:
                return
            fw.op("pe", lambda e: e.matmul(vnps, lhsT=wT_sb[:, :], rhs=S[:, :], start=True, stop=True), reads=[d_wT, d_S], writes=[dP[6]])
            fw.op("dve", lambda e: e.tensor_tensor(out=vnew[:, :], in0=u_sb[:, :], in1=vnps, op=ALU.subtract), reads=[d_u, dP[6]], writes=[d_vnew])
            if GSTOP <= 8:
                return
            oTps = psr(7, 0, 128, 0, 128)
            Sps = psr(7, 0, 128, 128, 256)
            fw.op("pe", lambda e: e.matmul(oTps, lhsT=S[:, :], rhs=qgT[:, :], start=True, stop=False), reads=[d_S, d_qgT], writes=[dP[7]])
            fw.op("pe", lambda e: e.matmul(oTps, lhsT=vnew[:, :], rhs=attnT[:, :], start=False, stop=True), reads=[d_vnew, d_attnT], writes=[dP[7]])
            fw.op("pe", lambda e: e.matmul(Sps, lhsT=kdec[:, :], rhs=vnew[:, :], start=True, stop=True), reads=[d_kdec, d_vnew], writes=[dP[7]])
            if GSTOP <= 9:
                return
            if dirn == 0:
                fw.op("act", lambda e: e.activation(out=oT_all[:, t0:t0 + 128], in_=oTps, func=AF.Copy), reads=[dP[7]], accw=[d_oT])
            else:
                fw.op("dve", lambda e: e.tensor_tensor(out=oT_all[:, t0:t0 + 128], in0=oT_all[:, t0:t0 + 128], in1=oTps, op=ALU.add),
                      reads=[dP[7], d_oT], accw=[d_oT])
            if GSTOP <= 10:
                return
            fw.op("act", lambda e: e.activation(out=Stmp[:, :], in_=Sps, func=AF.Copy), reads=[dP[7]], writes=[d_Stmp])
            fw.op("dve", lambda e: e.scalar_tensor_tensor(out=S[:, :], in0=S[:, :], scalar=eGl[:, 0:1], in1=Stmp[:, :], op0=ALU.mult, op1=ALU.add),
                  reads=[d_S, d_eGl, d_Stmp], writes=[d_S])

        def gblock(bi, st, n, s, dirn):
            b = bi % 2
            lo, hi = st - 1, st + n + 1
            s_lo, s_hi = (0, CTX) if s == 1 else (CTX, TA)
            has_l, has_r = lo >= s_lo, hi <= s_hi
            a_ = lo if has_l else st
            b_ = hi if has_r else st + n
            for h in range(2):
                fw.dma("sp", xg[b][:, h * 16:(h + 1) * 16, a_ - lo:b_ - lo], xnT_v[:, h * 16:(h + 1) * 16, a_:b_], d_xg[b],
                       reads=[d_xnT], writes=[d_xg[b]] if h == 0 else (), accw=() if h == 0 else [d_xg[b]])
            if not has_l:
                fw.op("dve", lambda e: e.memset(xg[b][:, :, 0:1], 0.0), writes=[d_xg[b]])
            if not has_r:
                fw.op("dve", lambda e: e.memset(xg[b][:, :, n + 1:n + 2], 0.0), writes=[d_xg[b]])
            for j in range(3):
                for kc in range(32):
                    fw.op("pe", lambda e: e.matmul(P[0][:, 0:n], lhsT=Wg[:, kc, j * 128:(j + 1) * 128], rhs=xg[b][:, kc, 1:n + 1],
                                                   start=(kc == 0), stop=(kc == 31)), reads=[d_Wg, d_xg[b]], writes=[dP[0]])
                for kc in range(32):
                    fw.op("pe", lambda e: e.matmul(P[1][:, 0:2], lhsT=Wg[:, kc, j * 128:(j + 1) * 128], rhs=xg[b][:, kc, 0:n + 2:n + 1],
                                                   start=(kc == 0), stop=(kc == 31)), reads=[d_Wg, d_xg[b]], writes=[dph])
                fw.op("act", lambda e: e.activation(out=raw[:, j, 1:n + 1], in_=P[0][:, 0:n], func=AF.Copy), reads=[dP[0]], writes=[d_raw[j]])
                fw.op("dve", lambda e: e.tensor_copy(out=raw[:, j, 0:n + 2:n + 1], in_=P[1][:, 0:2]), reads=[dph], writes=[d_raw[j]])
                cw = PV["conv"] + j * 3
                fw.op("dve", lambda e: e.tensor_scalar(out=cvt[:, 0:n], in0=raw[:, j, 0:n], scalar1=pvs[:, cw:cw + 1], scalar2=None, op0=ALU.mult),
                      reads=[d_raw[j], d_pv], writes=[d_cvt])
                fw.op("dve", lambda e: e.scalar_tensor_tensor(out=cvt[:, 0:n], in0=raw[:, j, 1:n + 1], scalar=pvs[:, cw + 1:cw + 2], in1=cvt[:, 0:n],
                                                              op0=ALU.mult, op1=ALU.add), reads=[d_raw[j], d_pv, d_cvt], writes=[d_cvt])
                fw.op("dve", lambda e: e.scalar_tensor_tensor(out=cvt[:, 0:n], in0=raw[:, j, 2:n + 2], scalar=pvs[:, cw + 2:cw + 3], in1=cvt[:, 0:n],
                                                              op0=ALU.mult, op1=ALU.add), reads=[d_raw[j], d_pv, d_cvt], writes=[d_cvt])
                fw.op("act", lambda e: e.activation(out=act3[:, j, 0:n], in_=cvt[:, 0:n], func=AF.Silu), reads=[d_cvt], writes=[d_act3[j]])
                if j < 2:
                    fw.op("act", lambda e: e.activation(out=sqg[:, 0:n], in_=act3[:, j, 0:n], func=AF.Square), reads=[d_act3[j]], writes=[d_sqg])
                    fw.op("pe", lambda e: e.matmul(P[2][:, 0:n], lhsT=CB("ones"), rhs=sqg[:, 0:n], start=True, stop=True),
                          reads=[d_sqg, d_cb], writes=[dpss])
                    fw.op("act", lambda e: e.activation(out=rsg[:, 0:n], in_=P[2][:, 0:n], func=AF.Sqrt, bias=1e-6), reads=[dpss], writes=[d_rsg])
                    fw.op("dve", lambda e: e.reciprocal(out=rsg[:, 0:n], in_=rsg[:, 0:n]), reads=[d_rsg], writes=[d_rsg])
                    sc_ = (128.0 ** -0.5) if j == 0 else 1.0
                    fw.op("dve", lambda e: e.scalar_tensor_tensor(out=act3[:, j, 0:n], in0=act3[:, j, 0:n], scalar=sc_, in1=rsg[:, 0:n],
                                                                  op0=ALU.mult, op1=ALU.mult), reads=[d_act3[j], d_rsg], writes=[d_act3[j]])
            nch = n // 128
            for ci in range(nch):
                for kc in range(32):
                    fw.op("pe", lambda e: e.matmul(P[1][0:128, 8 + ci * 4:12 + ci * 4], lhsT=xg[b][:, kc, 1 + ci * 128:129 + ci * 128],
                                                   rhs=Wg[:, kc, 512:516], start=(kc == 0), stop=(kc == 31)),
                          reads=[d_Wg, d_xg[b]], writes=[dpbg])
            bgp = P[1][0:128, 8:8 + nch * 4].rearrange("p (c f) -> p c f", f=4)
            fw.op("act", lambda e: e.activation(out=bgs[:, 0:nch, 0], in_=bgp[:, :, dirn], func=AF.Sigmoid), reads=[dpbg], writes=[d_bgs])
            fw.op("act", lambda e: e.activation(out=bgs[:, 0:nch, 1], in_=bgp[:, :, 2 + dirn], func=AF.Exp,
                                                bias=pvs[:, PV["dtb"] + dirn:PV["dtb"] + dirn + 1]), reads=[dpbg, d_pv, d_bgs], writes=[d_bgs])
            fw.op("act", lambda e: e.activation(out=bgs[:, 0:nch, 1], in_=bgs[:, 0:nch, 1], func=AF.Ln, bias=1.0), reads=[d_bgs], writes=[d_bgs])
            fw.op("dve", lambda e: e.tensor_scalar(out=bgs[:, 0:nch, 1], in0=bgs[:, 0:nch, 1], scalar1=negA[:, dirn:dirn + 1], scalar2=None, op0=ALU.mult),
                  reads=[d_bgs, d_negA], writes=[d_bgs])
            order = range(nch) if dirn == 0 else range(nch - 1, -1, -1)
            for ci in order:
                if GFWD and dirn == 1:
                    continue
                if gstate["n"] >= GLIM:
                    continue
                gstate["n"] += 1
                chunk(ci, st, dirn)
            if dirn == 1:
                ub = bi % 2
                for kc in range(32):
                    fw.op("pe", lambda e: e.matmul(P[0][:, 0:n], lhsT=Wg[:, kc, 384:512], rhs=xg[b][:, kc, 1:n + 1],
                                                   start=(kc == 0), stop=(kc == 31)), reads=[d_Wg, d_xg[b]], writes=[dP[0]])
                fw.op("act", lambda e: e.activation(out=szb[:, 0:n], in_=P[0][:, 0:n], func=AF.Silu), reads=[dP[0]], writes=[d_szb])
                fw.op("act", lambda e: e.activation(out=sqg[:, 0:n], in_=oT_all[:, st:st + n], func=AF.Square), reads=[d_oT], writes=[d_sqg])
                fw.op("pe", lambda e: e.matmul(P[2][:, 0:n], lhsT=CB("ones"), rhs=sqg[:, 0:n], start=True, stop=True),
                      reads=[d_sqg, d_cb], writes=[dpss])
                fw.op("act", lambda e: e.activation(out=rsg[:, 0:n], in_=P[2][:, 0:n], func=AF.Sqrt, scale=1.0 / 128, bias=NORM_EPS),
                      reads=[dpss], writes=[d_rsg])
                fw.op("dve", lambda e: e.reciprocal(out=rsg[:, 0:n], in_=rsg[:, 0:n]), reads=[d_rsg], writes=[d_rsg])
                fw.op("dve", lambda e: e.scalar_tensor_tensor(out=cvt[:, 0:n], in0=oT_all[:, st:st + n], scalar=pvs[:, PV["gnorm"]:PV["gnorm"] + 1],
                                                              in1=rsg[:, 0:n], op0=ALU.mult, op1=ALU.mult), reads=[d_oT, d_pv, d_rsg], writes=[d_cvt])
                fw.op("dve", lambda e: e.tensor_tensor(out=ustg[ub][:, 0:n], in0=cvt[:, 0:n], in1=szb[:, 0:n], op=ALU.mult),
                      reads=[d_cvt, d_szb], writes=[d_ustg[ub]])
                fw.dma("sp", uT[256:384, st:st + n], ustg[ub][:, 0:n], d_ustg[ub], reads=[d_ustg[ub]], accw=[d_uT])

        gstate = {"n": 0}
        if GLIM < 100000 or GFWD:
            fw.op("dve", lambda e: e.memset(oT_all[:, :], 0.0), writes=[d_oT])
        for dirn in range(2):
            gstate["n"] = 0
            fw.op("dve", lambda e: e.memset(S[:, :], 0.0), writes=[d_S])
            lat = blocks[1:]
            seq = [blocks[0]] + (lat if dirn == 0 else lat[::-1])
            for bi, (st, n, s) in enumerate(seq):
                if bi >= GBLK or dirn >= GSW:
                    continue
                gblock(bi, st, n, s, dirn)
        fw.barrier()
        sc.__exit__(None, None, None)

    if "c" in do:
        gdn_pass()

    def rwkv_pass():
        c0 = NCA_D + NCA_B + NCA_C
        sc = fw.scope()
        sc.__enter__()
        Wr, d_Wr = T([128, 32, NCA_A], BF16, "Wr")
        for h in range(8):
            fw.dma("pool", Wr[:, h * 4:(h + 1) * 4, :], wA_v[:, h * 4:(h + 1) * 4, c0:c0 + NCA_A], d_Wr,
                   writes=[d_Wr] if h == 0 else (), accw=() if h == 0 else [d_Wr])
        lw, d_lw = T([128, 2, 128], F32, "lw")
        fw.dma("sp", lw[:], lora[:, :].rearrange("p (d m) -> p d m", d=2), d_lw, writes=[d_lw])
        xg, d_xg = T([128, 32, TB + 2], BF16, "xgr")
        raw = fw.sb([128, 4, TB + 2], F32, "rawr")
        d_raw = deps(4)
        fr = fw.sb([128, 4, TB], F32, "fr")
        d_fr = deps(4)
        tmpa, d_tmpa = T([128, TB], F32, "tmpa")
        tmpb, d_tmpb = T([128, TB], F32, "tmpb")
        Lf, d_Lf = T([128, TB], F32, "Lf")
        af, d_af = T([128, TB], F32, "af")
        ao, d_ao = T([128, TB], F32, "ao")
        kkf, d_kkf = T([128, TB], F32, "kkf")
        kdf, d_kdf = T([128, TB], F32, "kdf")
        bf_, d_bf = T([128, TB], F32, "bf_")
        szr, d_szr = T([128, TB], F32, "szr")
        yb, d_yb = T([128, TB], F32, "yb")
        yst = [fw.sb([128, 128], F32, f"yst{i}") for i in range(2)]
        d_yst = deps(2)
        ustr = [fw.sb([128, TB], BF16, f"ustr{i}") for i in range(2)]
        d_ustr = deps(2)
        pc, d_pc = T([128, 8], F32, "pc")
        ST, d_ST = T([128, 128], F32, "ST")
        names = ["Ltok", "kktok", "btok", "kdtok", "eI", "enI", "eE", "eRtok", "eEtok", "at", "qt", "bt", "kt", "Bh", "Kh",
                 "MV", "W2", "W1T", "U", "STt", "eLC"]
        tl = {}
        dl = {}
        for nm in names:
            tl[nm], dl[nm] = T([128, 128], F32, "r_" + nm)
        Vpad, d_Vpad = T([128, 2, 128], F32, "Vpad")
        Upad, d_Upad = T([128, 2, 128], F32, "Upad")
        Apad, d_Apad = T([128, 2, 128], F32, "Apad")
        A_sb = [[fw.sb([128, 128], F32, f"rA{h}{i}") for i in range(2)] for h in range(2)]
        B_sb = [[fw.sb([128, 128], F32, f"rB{h}{i}") for i in range(2)] for h in range(2)]
        d_A = [deps(2) for _ in range(2)]
        d_B = [deps(2) for _ in range(2)]
        Mi = [fw.sb([128, 128], F32, f"rM{h}") for h in range(2)]
        d_Mi = deps(2)
        MakT = [fw.sb([128, 128], F32, f"rMak{h}") for h in range(2)]
        MqbT = [fw.sb([128, 128], F32, f"rMqb{h}") for h in range(2)]
        MqkT = [fw.sb([128, 128], F32, f"rMqk{h}") for h in range(2)]
        d_Mak, d_Mqb, d_Mqk = deps(2), deps(2), deps(2)
        yscr = nc.dram_tensor("yscr", [128, TA], F32).ap()
        d_yscr = Dep()
        ident = CF("ident")
        blk = CF("blk64")
        for t_, d_ in ((Vpad, d_Vpad), (Upad, d_Upad), (Apad, d_Apad)):
            fw.op("dve", lambda e: e.memset(t_[:], 0.0), writes=[d_])
        fw.op("dve", lambda e: e.tensor_scalar(out=pc[:, 0:4], in0=pvs[:, PV["mu"]:PV["mu"] + 4], scalar1=-1.0, scalar2=1.0,
                                               op0=ALU.mult, op1=ALU.add), reads=[d_pv], writes=[d_pc])
        fw.op("dve", lambda e: e.tensor_scalar(out=pc[:, 4:8], in0=pvs[:, PV["mu"]:PV["mu"] + 4], scalar1=0.5, scalar2=None, op0=ALU.mult),
              reads=[d_pv, d_pc], writes=[d_pc])
        NEGC = -math.exp(-0.5)

        def rchunk(ci, st, dirn):
            cs = ci * 128
            t0 = st + cs
            sl = slice(cs, cs + 128)
            if dirn == 0:
                TI, TS, TR, M01S, M01ST, M01IT, last = CF("UI"), CF("SU"), CF("SL"), CF("SL"), CF("SU"), CF("UI"), 127
            else:
                TI, TS, TR, M01S, M01ST, M01IT, last = CF("LI"), CF("SL"), CF("SU"), CF("SU"), CF("SL"), CF("LI"), 0
            for i_, (src, d_src) in enumerate(((Lf, d_Lf), (fr[:, 2, :], d_fr[2]), (kkf, d_kkf), (bf_, d_bf))):
                src_ap = src[:, sl]
                fw.op("pe", lambda e: e.transpose(P[3][:, i_ * 128:(i_ + 1) * 128], src_ap, ident), reads=[d_src, d_cf], writes=[dP[3]])
            fw.op("pe", lambda e: e.transpose(P[4][:, 0:128], kdf[:, sl], ident), reads=[d_kdf, d_cf], writes=[dP[4]])
            fw.op("act", lambda e: e.activation(out=tl["Ltok"][:], in_=P[3][:, 0:128], func=AF.Copy), reads=[dP[3]], writes=[dl["Ltok"]])
            fw.op("dve", lambda e: e.tensor_copy(out=Vpad[:, 0, 0:64], in_=P[3][:, 128:192]), reads=[dP[3]], writes=[d_Vpad])
            fw.op("dve", lambda e: e.tensor_copy(out=Vpad[:, 1, 64:128], in_=P[3][:, 192:256]), reads=[dP[3]], writes=[d_Vpad])
            fw.op("act", lambda e: e.activation(out=tl["kktok"][:], in_=P[3][:, 256:384], func=AF.Copy), reads=[dP[3]], writes=[dl["kktok"]])
            fw.op("dve", lambda e: e.tensor_copy(out=tl["btok"][:], in_=P[3][:, 384:512]), reads=[dP[3]], writes=[dl["btok"]])
            fw.op("act", lambda e: e.activation(out=tl["kdtok"][:], in_=P[4][:, 0:128], func=AF.Copy), reads=[dP[4]], writes=[dl["kdtok"]])
            Lt = tl["Ltok"]
            fw.op("pe", lambda e: e.matmul(P[4][:, 128:256], lhsT=Lt[:], rhs=TI, start=True, stop=True), reads=[dl["Ltok"], d_cf], writes=[dP[4]])
            fw.op("pe", lambda e: e.matmul(P[4][:, 256:384], lhsT=Lt[:], rhs=TS, start=True, stop=True), reads=[dl["Ltok"], d_cf], writes=[dP[4]])
            fw.op("pe", lambda e: e.matmul(P[4][:, 384:512], lhsT=TR, rhs=Lt[:], start=True, stop=True), reads=[dl["Ltok"], d_cf], writes=[dP[4]])
            fw.op("pe", lambda e: e.matmul(P[5][:, 0:128], lhsT=TS, rhs=Lt[:], start=True, stop=True), reads=[dl["Ltok"], d_cf], writes=[dP[5]])
            fw.op("act", lambda e: e.activation(out=tl["eI"][:], in_=P[4][:, 128:256], func=AF.Exp), reads=[dP[4]], writes=[dl["eI"]])
            fw.op("act", lambda e: e.activation(out=tl["enI"][:], in_=P[4][:, 128:256], func=AF.Exp, scale=-1.0), reads=[dP[4]], writes=[dl["enI"]])
            fw.op("act", lambda e: e.activation(out=tl["eE"][:], in_=P[4][:, 256:384], func=AF.Exp), reads=[dP[4]], writes=[dl["eE"]])
            fw.op("act", lambda e: e.activation(out=tl["eRtok"][:], in_=P[4][:, 384:512], func=AF.Exp), reads=[dP[4]], writes=[dl["eRtok"]])
            fw.op("act", lambda e: e.activation(out=tl["eEtok"][:], in_=P[5][:, 0:128], func=AF.Exp), reads=[dP[5]], writes=[dl["eEtok"]])
            fw.op("act", lambda e: e.activation(out=tl["eLC"][:, 0:1], in_=tl["eI"][:, last:last + 1], func=AF.Copy), reads=[dl["eI"]], writes=[dl["eLC"]])
            fw.op("dve", lambda e: e.scalar_tensor_tensor(out=tl["at"][:], in0=kkf[:, sl], scalar=-1.0, in1=tl["eE"][:], op0=ALU.mult, op1=ALU.mult),
                  reads=[d_kkf, dl["eE"]], writes=[dl["at"]])
            fw.op("dve", lambda e: e.tensor_tensor(out=tl["qt"][:], in0=fr[:, 0, sl], in1=tl["eI"][:], op=ALU.mult), reads=[d_fr[0], dl["eI"]], writes=[dl["qt"]])
            fw.op("dve", lambda e: e.tensor_tensor(out=tl["bt"][:], in0=bf_[:, sl], in1=tl["enI"][:], op=ALU.mult), reads=[d_bf, dl["enI"]], writes=[dl["bt"]])
            fw.op("dve", lambda e: e.tensor_tensor(out=tl["kt"][:], in0=kdf[:, sl], in1=tl["enI"][:], op=ALU.mult), reads=[d_kdf, dl["enI"]], writes=[dl["kt"]])
            fw.op("dve", lambda e: e.tensor_tensor(out=tl["Bh"][:], in0=tl["btok"][:], in1=tl["eRtok"][:], op=ALU.mult), reads=[dl["btok"], dl["eRtok"]], writes=[dl["Bh"]])
            fw.op("dve", lambda e: e.tensor_tensor(out=tl["Kh"][:], in0=tl["kdtok"][:], in1=tl["eRtok"][:], op=ALU.mult), reads=[dl["kdtok"], dl["eRtok"]], writes=[dl["Kh"]])
            for h in range(2):
                hs_ = slice(64 * h, 64 * h + 64)
                fw.op("dve", lambda e: e.scalar_tensor_tensor(out=Apad[:, h, hs_], in0=tl["kktok"][:, hs_], scalar=-1.0, in1=tl["eEtok"][:, hs_],
                                                              op0=ALU.mult, op1=ALU.mult), reads=[dl["kktok"], dl["eEtok"]], writes=[d_Apad])
            at, qt, bt, kt = tl["at"], tl["qt"], tl["bt"], tl["kt"]
            for h in range(2):
                hs_ = slice(64 * h, 64 * h + 64)
                fw.op("pe", lambda e: e.matmul(P[6][:, h * 128:(h + 1) * 128], lhsT=at[hs_, :], rhs=bt[hs_, :], start=True, stop=True),
                      reads=[dl["at"], dl["bt"]], writes=[dP[6]])
                fw.op("pe", lambda e: e.matmul(P[6][:, 256 + h * 128:384 + h * 128], lhsT=bt[hs_, :], rhs=at[hs_, :], start=True, stop=True),
                      reads=[dl["at"], dl["bt"]], writes=[dP[6]])
                fw.op("pe", lambda e: e.matmul(P[7][:, h * 128:(h + 1) * 128], lhsT=kt[hs_, :], rhs=at[hs_, :], start=True, stop=True),
                      reads=[dl["at"], dl["kt"]], writes=[dP[7]])
                fw.op("pe", lambda e: e.matmul(P[7][:, 256 + h * 128:384 + h * 128], lhsT=bt[hs_, :], rhs=qt[hs_, :], start=True, stop=True),
                      reads=[dl["qt"], dl["bt"]], writes=[dP[7]])
                fw.op("pe", lambda e: e.matmul(P[5][:, 128 + h * 128:256 + h * 128], lhsT=kt[hs_, :], rhs=qt[hs_, :], start=True, stop=True),
                      reads=[dl["qt"], dl["kt"]], writes=[dP[5]])
            for h in range(2):
                fw.op("dve", lambda e: e.scalar_tensor_tensor(out=A_sb[h][0][:], in0=P[6][:, h * 128:(h + 1) * 128], scalar=-1.0, in1=M01S,
                                                              op0=ALU.mult, op1=ALU.mult), reads=[dP[6], d_cf], writes=[d_A[h][0]])
                fw.op("dve", lambda e: e.scalar_tensor_tensor(out=B_sb[h][0][:], in0=P[6][:, 256 + h * 128:384 + h * 128], scalar=-1.0, in1=M01ST,
                                                              op0=ALU.mult, op1=ALU.mult), reads=[dP[6], d_cf], writes=[d_B[h][0]])
                fw.op("dve", lambda e: e.tensor_tensor(out=MakT[h][:], in0=P[7][:, h * 128:(h + 1) * 128], in1=M01ST, op=ALU.mult),
                      reads=[dP[7], d_cf], writes=[d_Mak[h]])
                fw.op("dve", lambda e: e.tensor_tensor(out=MqbT[h][:], in0=P[7][:, 256 + h * 128:384 + h * 128], in1=M01IT, op=ALU.mult),
                      reads=[dP[7], d_cf], writes=[d_Mqb[h]])
                fw.op("dve", lambda e: e.tensor_tensor(out=MqkT[h][:], in0=P[5][:, 128 + h * 128:256 + h * 128], in1=M01IT, op=ALU.mult),
                      reads=[dP[5], d_cf], writes=[d_Mqk[h]])
                fw.op("dve", lambda e: e.tensor_tensor(out=Mi[h][:], in0=ident, in1=B_sb[h][0][:], op=ALU.subtract),
                      reads=[d_cf, d_B[h][0]], writes=[d_Mi[h]])
            for h in range(2):
                bk = 6 + h
                cur = 0
                for lvl in range(6):
                    nxt = 1 - cur
                    fw.op("pe", lambda e: e.matmul(P[bk][:, 0:128], lhsT=B_sb[h][cur][:], rhs=A_sb[h][cur][:], start=True, stop=True),
                          reads=[d_A[h][cur], d_B[h][cur]], writes=[dP[bk]])
                    if lvl < 5:
                        fw.op("pe", lambda e: e.matmul(P[bk][:, 128:256], lhsT=A_sb[h][cur][:], rhs=B_sb[h][cur][:], start=True, stop=True),
                              reads=[d_A[h][cur], d_B[h][cur]], writes=[dP[bk]])
                    fw.op("act", lambda e: e.activation(out=A_sb[h][nxt][:], in_=P[bk][:, 0:128], func=AF.Copy), reads=[dP[bk]], writes=[d_A[h][nxt]])
                    if lvl < 5:
                        fw.op("dve", lambda e: e.tensor_copy(out=B_sb[h][nxt][:], in_=P[bk][:, 128:256]), reads=[dP[bk]], writes=[d_B[h][nxt]])
                    fw.op("pe", lambda e: e.matmul(P[bk][:, 256:384], lhsT=A_sb[h][nxt][:], rhs=Mi[h][:], start=True, stop=True),
                          reads=[d_A[h][nxt], d_Mi[h]], writes=[dP[bk]])
                    fw.op("dve", lambda e: e.tensor_tensor(out=Mi[h][:], in0=Mi[h][:], in1=P[bk][:, 256:384], op=ALU.add),
                          reads=[d_Mi[h], dP[bk]], writes=[d_Mi[h]])
                    cur = nxt
            for h in range(2):
                hs_ = slice(64 * h, 64 * h + 64)
                fw.op("pe", lambda e: e.matmul(P[5][:, hs_], lhsT=MakT[h][:], rhs=Vpad[:, h, hs_], start=True, stop=True),
                      reads=[d_Mak[h], d_Vpad], writes=[dP[5]])
            fw.op("act", lambda e: e.activation(out=tl["MV"][:], in_=P[5][:, 0:128], func=AF.Copy), reads=[dP[5]], writes=[dl["MV"]])
            for h in range(2):
                hs_ = slice(64 * h, 64 * h + 64)
                fw.op("pe", lambda e: e.matmul(P[5][:, 128 + 64 * h:192 + 64 * h], lhsT=Mi[h][:], rhs=tl["MV"][:, hs_], start=True, stop=True),
                      reads=[d_Mi[h], dl["MV"]], writes=[dP[5]])
            for h in range(2):
                fw.op("pe", lambda e: e.matmul(P[5][:, 256:384], lhsT=Apad[:, h, :], rhs=Mi[h][:], start=(h == 0), stop=(h == 1)),
                      reads=[d_Mi[h], d_Apad], writes=[dP[5]])
            fw.op("act", lambda e: e.activation(out=tl["W2"][:], in_=P[5][:, 128:256], func=AF.Copy), reads=[dP[5]], writes=[dl["W2"]])
            fw.op("dve", lambda e: e.tensor_copy(out=tl["W1T"][:], in_=P[5][:, 256:384]), reads=[dP[5]], writes=[dl["W1T"]])
            fw.op("pe", lambda e: e.matmul(P[6][:, 0:128], lhsT=tl["W1T"][:], rhs=ST[:], start=True, stop=True), reads=[dl["W1T"], d_ST], writes=[dP[6]])
            fw.op("dve", lambda e: e.tensor_tensor(out=tl["U"][:], in0=tl["W2"][:], in1=P[6][:, 0:128], op=ALU.add), reads=[dl["W2"], dP[6]], writes=[dl["U"]])
            fw.op("dve", lambda e: e.tensor_copy(out=Upad[:, 0, 0:64], in_=tl["U"][:, 0:64]), reads=[dl["U"]], writes=[d_Upad])
            fw.op("dve", lambda e: e.tensor_copy(out=Upad[:, 1, 64:128], in_=tl["U"][:, 64:128]), reads=[dl["U"]], writes=[d_Upad])
            fw.op("pe", lambda e: e.matmul(P[7][:, 0:128], lhsT=ST[:], rhs=qt[:], start=True, stop=False), reads=[d_ST, dl["qt"]], writes=[dP[7]])
            for h in range(2):
                fw.op("pe", lambda e: e.matmul(P[7][:, 0:128], lhsT=Upad[:, h, :], rhs=MqbT[h][:], start=False, stop=False),
                      reads=[d_Upad, d_Mqb[h]], writes=[dP[7]])
                fw.op("pe", lambda e: e.matmul(P[7][:, 0:128], lhsT=Vpad[:, h, :], rhs=MqkT[h][:], start=False, stop=(h == 1)),
                      reads=[d_Vpad, d_Mqk[h]], writes=[dP[7]])
            fw.op("pe", lambda e: e.matmul(P[6][:, 128:256], lhsT=tl["Bh"][:], rhs=tl["U"][:], start=True, stop=False), reads=[dl["Bh"], dl["U"]], writes=[dP[6]])
            fw.op("pe", lambda e: e.matmul(P[6][:, 128:256], lhsT=tl["Kh"][:], rhs=Vpad[:, 0, :], start=False, stop=False), reads=[dl["Kh"], d_Vpad], writes=[dP[6]])
            fw.op("pe", lambda e: e.matmul(P[6][:, 128:256], lhsT=tl["Kh"][:], rhs=Vpad[:, 1, :], start=False, stop=True), reads=[dl["Kh"], d_Vpad], writes=[dP[6]])
            yb_ = (t0 // 128) % 2
            if dirn == 0:
                fw.op("act", lambda e: e.activation(out=yst[yb_][:], in_=P[7][:, 0:128], func=AF.Copy), reads=[dP[7]], writes=[d_yst[yb_]])
                fw.dma("sp", yscr[:, t0:t0 + 128], yst[yb_][:], d_yst[yb_], reads=[d_yst[yb_]], accw=[d_yscr])
            else:
                fw.op("dve", lambda e: e.tensor_tensor(out=yb[:, sl], in0=yb[:, sl], in1=P[7][:, 0:128], op=ALU.add), reads=[dP[7], d_yb], writes=[d_yb])
            fw.op("dve", lambda e: e.tensor_tensor(out=tl["STt"][:], in0=P[6][:, 128:256], in1=blk, op=ALU.mult), reads=[dP[6], d_cf], writes=[dl["STt"]])
            fw.op("dve", lambda e: e.scalar_tensor_tensor(out=ST[:], in0=ST[:], scalar=tl["eLC"][:, 0:1], in1=tl["STt"][:], op0=ALU.mult, op1=ALU.add),
                  reads=[d_ST, dl["eLC"], dl["STt"]], writes=[d_ST])

        def rblock(st, n, s, dirn):
            lo, hi = st - 1, st + n + 1
            s_lo, s_hi = (0, CTX) if s == 1 else (CTX, TA)
            has_l, has_r = lo >= s_lo, hi <= s_hi
            a_ = lo if has_l else st
            b_ = hi if has_r else st + n
            for h in range(2):
                fw.dma("sp", xg[:, h * 16:(h + 1) * 16, a_ - lo:b_ - lo], xnT_v[:, h * 16:(h + 1) * 16, a_:b_], d_xg,
                       reads=[d_xnT], writes=[d_xg] if h == 0 else (), accw=() if h == 0 else [d_xg])
            if not has_l:
                fw.op("dve", lambda e: e.memset(xg[:, :, 0:1], 0.0), writes=[d_xg])
            if not has_r:
                fw.op("dve", lambda e: e.memset(xg[:, :, n + 1:n + 2], 0.0), writes=[d_xg])
            if dirn == 1:
                fw.dma("sp", yb[:, 0:n], yscr[:, st:st + n], d_yb, reads=[d_yscr], writes=[d_yb])
            wcols = [(0, 128), (128, 128), (256, 128), (512, 128)]
            for j, (wc, wn) in enumerate(wcols):
                for kc in range(32):
                    fw.op("pe", lambda e: e.matmul(P[0][:, 0:n], lhsT=Wr[:, kc, wc:wc + wn], rhs=xg[:, kc, 1:n + 1],
                                                   start=(kc == 0), stop=(kc == 31)), reads=[d_Wr, d_xg], writes=[dP[0]])
                for kc in range(32):
                    fw.op("pe", lambda e: e.matmul(P[1][:, 0:2], lhsT=Wr[:, kc, wc:wc + wn], rhs=xg[:, kc, 0:n + 2:n + 1],
                                                   start=(kc == 0), stop=(kc == 31)), reads=[d_Wr, d_xg], writes=[dP[1]])
                fw.op("act", lambda e: e.activation(out=raw[:, j, 1:n + 1], in_=P[0][:, 0:n], func=AF.Copy), reads=[dP[0]], writes=[d_raw[j]])
                fw.op("dve", lambda e: e.tensor_copy(out=raw[:, j, 0:n + 2:n + 1], in_=P[1][:, 0:2]), reads=[dP[1]], writes=[d_raw[j]])
                fw.op("dve", lambda e: e.tensor_tensor(out=tmpa[:, 0:n], in0=raw[:, j, 0:n], in1=raw[:, j, 2:n + 2], op=ALU.add),
                      reads=[d_raw[j]], writes=[d_tmpa])
                fw.op("dve", lambda e: e.tensor_scalar(out=tmpa[:, 0:n], in0=tmpa[:, 0:n], scalar1=pc[:, 4 + j:5 + j], scalar2=None, op0=ALU.mult),
                      reads=[d_tmpa, d_pc], writes=[d_tmpa])
                fw.op("dve", lambda e: e.scalar_tensor_tensor(out=fr[:, j, 0:n], in0=raw[:, j, 1:n + 1], scalar=pc[:, j:j + 1], in1=tmpa[:, 0:n],
                                                              op0=ALU.mult, op1=ALU.add), reads=[d_raw[j], d_pc, d_tmpa], writes=[d_fr[j]])
            fw.op("act", lambda e: e.activation(out=fr[0:64, 3, 0:n], in_=fr[0:64, 3, 0:n], func=AF.Tanh), reads=[d_fr[3]], writes=[d_fr[3]])
            fw.op("pe", lambda e: e.matmul(P[2][:, 0:n], lhsT=lw[0:64, dirn, :], rhs=fr[0:64, 3, 0:n], start=True, stop=True),
                  reads=[d_lw, d_fr[3]], writes=[dP[2]])
            fw.op("act", lambda e: e.activation(out=Lf[:, 0:n], in_=P[2][:, 0:n], func=AF.Sigmoid, bias=pvs[:, PV["w0"] + dirn:PV["w0"] + dirn + 1]),
                  reads=[dP[2], d_pv], writes=[d_Lf])
            fw.op("dve", lambda e: e.tensor_scalar(out=Lf[:, 0:n], in0=Lf[:, 0:n], scalar1=NEGC, scalar2=None, op0=ALU.mult), reads=[d_Lf], writes=[d_Lf])
            fw.op("pe", lambda e: e.matmul(P[2][:, 0:n], lhsT=lw[64:128, dirn, :], rhs=fr[64:128, 3, 0:n], start=True, stop=True),
                  reads=[d_lw, d_fr[3]], writes=[dP[2]])
            fw.op("act", lambda e: e.activation(out=af[:, 0:n], in_=P[2][:, 0:n], func=AF.Sigmoid, bias=pvs[:, PV["a0"] + dirn:PV["a0"] + dirn + 1]),
                  reads=[dP[2], d_pv], writes=[d_af])
            fw.op("dve", lambda e: e.tensor_scalar(out=kkf[:, 0:n], in0=fr[:, 1, 0:n], scalar1=pvs[:, PV["kk"]:PV["kk"] + 1], scalar2=None, op0=ALU.mult),
                  reads=[d_fr[1], d_pv], writes=[d_kkf])
            fw.op("act", lambda e: e.activation(out=tmpa[:, 0:n], in_=kkf[:, 0:n], func=AF.Square), reads=[d_kkf], writes=[d_tmpa])
            fw.op("pe", lambda e: e.matmul(P[2][:, 0:n], lhsT=blk, rhs=tmpa[:, 0:n], start=True, stop=True), reads=[d_cf, d_tmpa], writes=[dP[2]])
            fw.op("act", lambda e: e.activation(out=tmpb[:, 0:n], in_=P[2][:, 0:n], func=AF.Sqrt, bias=1e-6), reads=[dP[2]], writes=[d_tmpb])
            fw.op("dve", lambda e: e.reciprocal(out=tmpb[:, 0:n], in_=tmpb[:, 0:n]), reads=[d_tmpb], writes=[d_tmpb])
            fw.op("dve", lambda e: e.tensor_tensor(out=kkf[:, 0:n], in0=kkf[:, 0:n], in1=tmpb[:, 0:n], op=ALU.mult), reads=[d_kkf, d_tmpb], writes=[d_kkf])
            fw.op("dve", lambda e: e.tensor_scalar(out=tmpa[:, 0:n], in0=af[:, 0:n], scalar1=-1.0, scalar2=pvs[:, PV["ka"]:PV["ka"] + 1],
                                                   op0=ALU.add, op1=ALU.mult), reads=[d_af, d_pv], writes=[d_tmpa])
            fw.op("dve", lambda e: e.scalar_tensor_tensor(out=kdf[:, 0:n], in0=tmpa[:, 0:n], scalar=1.0, in1=fr[:, 1, 0:n], op0=ALU.add, op1=ALU.mult),
                  reads=[d_tmpa, d_fr[1]], writes=[d_kdf])
            fw.op("dve", lambda e: e.tensor_tensor(out=bf_[:, 0:n], in0=kkf[:, 0:n], in1=af[:, 0:n], op=ALU.mult), reads=[d_kkf, d_af], writes=[d_bf])
            nch = n // 128
            order = range(nch) if dirn == 0 else range(nch - 1, -1, -1)
            for ci in order:
                rchunk(ci, st, dirn)
            if dirn == 1:
                ub = (st // TB) % 2
                for kc in range(32):
                    fw.op("pe", lambda e: e.matmul(P[0][:, 0:n], lhsT=Wr[:, kc, 384:512], rhs=xg[:, kc, 1:n + 1],
                                                   start=(kc == 0), stop=(kc == 31)), reads=[d_Wr, d_xg], writes=[dP[0]])
                fw.op("act", lambda e: e.activation(out=szr[:, 0:n], in_=P[0][:, 0:n], func=AF.Silu), reads=[dP[0]], writes=[d_szr])
                fw.op("pe", lambda e: e.matmul(P[2][:, 0:n], lhsT=blk, rhs=yb[:, 0:n], start=True, stop=True), reads=[d_cf, d_yb], writes=[dP[2]])
                fw.op("dve", lambda e: e.scalar_tensor_tensor(out=yb[:, 0:n], in0=P[2][:, 0:n], scalar=-1.0 / 64, in1=yb[:, 0:n], op0=ALU.mult, op1=ALU.add),
                      reads=[dP[2], d_yb], writes=[d_yb])
                fw.op("act", lambda e: e.activation(out=tmpa[:, 0:n], in_=yb[:, 0:n], func=AF.Square), reads=[d_yb], writes=[d_tmpa])
                fw.op("pe", lambda e: e.matmul(P[2][:, 0:n], lhsT=blk, rhs=tmpa[:, 0:n], start=True, stop=True), reads=[d_cf, d_tmpa], writes=[dP[2]])
                fw.op("act", lambda e: e.activation(out=tmpb[:, 0:n], in_=P[2][:, 0:n], func=AF.Sqrt, scale=1.0 / 64, bias=64e-5), reads=[dP[2]], writes=[d_tmpb])
                fw.op("dve", lambda e: e.reciprocal(out=tmpb[:, 0:n], in_=tmpb[:, 0:n]), reads=[d_tmpb], writes=[d_tmpb])
                fw.op("dve", lambda e: e.tensor_tensor(out=yb[:, 0:n], in0=yb[:, 0:n], in1=tmpb[:, 0:n], op=ALU.mult), reads=[d_yb, d_tmpb], writes=[d_yb])
                fw.op("act", lambda e: e.activation(out=yb[:, 0:n], in_=yb[:, 0:n], func=AF.Identity, scale=pvs[:, PV["lnw"]:PV["lnw"] + 1],
                                                    bias=pvs[:, PV["lnb"]:PV["lnb"] + 1]), reads=[d_yb, d_pv], writes=[d_yb])
                fw.op("pe", lambda e: e.matmul(P[2][:, 0:n], lhsT=lw[64:128, 0, :], rhs=fr[64:128, 3, 0:n], start=True, stop=True),
                      reads=[d_lw, d_fr[3]], writes=[dP[2]])
                fw.op("act", lambda e: e.activation(out=ao[:, 0:n], in_=P[2][:, 0:n], func=AF.Sigmoid, bias=pvs[:, PV["a0"]:PV["a0"] + 1]),
                      reads=[dP[2], d_pv], writes=[d_ao])
                fw.op("dve", lambda e: e.tensor_tensor(out=ao[:, 0:n], in0=ao[:, 0:n], in1=af[:, 0:n], op=ALU.add), reads=[d_ao, d_af], writes=[d_ao])
                fw.op("dve", lambda e: e.tensor_scalar(out=ao[:, 0:n], in0=ao[:, 0:n], scalar1=0.5, scalar2=-1.0, op0=ALU.mult, op1=ALU.add),
                      reads=[d_ao], writes=[d_ao])
                fw.op("dve", lambda e: e.tensor_scalar(out=ao[:, 0:n], in0=ao[:, 0:n], scalar1=pvs[:, PV["ka"]:PV["ka"] + 1], scalar2=1.0, op0=ALU.mult, op1=ALU.add),
                      reads=[d_ao, d_pv], writes=[d_ao])
                fw.op("dve", lambda e: e.tensor_tensor(out=ao[:, 0:n], in0=ao[:, 0:n], in1=fr[:, 1, 0:n], op=ALU.mult), reads=[d_ao, d_fr[1]], writes=[d_ao])
                fw.op("dve", lambda e: e.scalar_tensor_tensor(out=ao[:, 0:n], in0=fr[:, 0, 0:n], scalar=pvs[:, PV["rk"]:PV["rk"] + 1], in1=ao[:, 0:n],
                                                              op0=ALU.mult, op1=ALU.mult), reads=[d_fr[0], d_pv, d_ao], writes=[d_ao])
                fw.op("pe", lambda e: e.matmul(P[2][:, 0:n], lhsT=blk, rhs=ao[:, 0:n], start=True, stop=True), reads=[d_cf, d_ao], writes=[dP[2]])
                fw.op("dve", lambda e: e.tensor_tensor(out=tmpa[:, 0:n], in0=P[2][:, 0:n], in1=fr[:, 2, 0:n], op=ALU.mult), reads=[dP[2], d_fr[2]], writes=[d_tmpa])
                fw.op("dve", lambda e: e.tensor_tensor(out=yb[:, 0:n], in0=yb[:, 0:n], in1=tmpa[:, 0:n], op=ALU.add), reads=[d_yb, d_tmpa], writes=[d_yb])
                fw.op("dve", lambda e: e.tensor_tensor(out=ustr[ub][:, 0:n], in0=yb[:, 0:n], in1=szr[:, 0:n], op=ALU.mult), reads=[d_yb, d_szr], writes=[d_ustr[ub]])
                fw.dma("sp", uT[0:128, st:st + n], ustr[ub][:, 0:n], d_ustr[ub], reads=[d_ustr[ub]], accw=[d_uT])

        for dirn in range(2):
            fw.op("dve", lambda e: e.memset(ST[:, :], 0.0), writes=[d_ST])
            lat = blocks[1:]
            seq = [blocks[0]] + (lat if dirn == 0 else lat[::-1])
            for (st, n, s) in seq:
                rblock(st, n, s, dirn)
            fw.barrier()
        sc.__exit__(None, None, None)

    if "a" in do:
        rwkv_pass()

    fw.finish("sp", [d_uT, d_dbg])
    fw.stack.close()
    return nc


NOWN = 2048


def prep_B(inp, l, xT_all, uT_full, NL_own):
    cst = _cst_array()
    cvec = np.stack([_fm(inp["c"][0]), _fm(inp["c_ctx"])], axis=2).reshape(128, 64)
    wmod = inp["w_mod"][l]
    bmod = _fm(inp["b_mod"][l])
    normw = _fm(inp["norm_w"][l])
    wg = inp["w_in"][l][:, O_G:O_G + 4 * D].reshape(D, 4, 32, 128).transpose(0, 2, 1, 3).reshape(D, 4 * D)
    wg = np.ascontiguousarray(wg)
    wbr = inp["w_branch"][l].reshape(4, 1024, 32, 128).transpose(1, 2, 0, 3).reshape(1024, 4 * D)
    wbr = np.ascontiguousarray(wbr)
    wout = inp["w_out"][l]
    ncores = (xT_all.shape[1] - CTX) // NL_own
    maps = []
    for c in range(ncores):
        sl = np.r_[0:CTX, CTX + NL_own * c:CTX + NL_own * (c + 1)]
        maps.append({"xT": np.ascontiguousarray(xT_all[:, sl]), "uT": np.ascontiguousarray(uT_full[:, sl]),
                     "cvec": cvec, "wmod": wmod, "bmod": bmod, "normw": normw, "wg": wg, "wbr": wbr, "wout": wout, "cst": cst})
    return maps


def build_B(NL_own):
    NT = CTX + NL_own
    nc = bass.Bass("TRN2", target_bir_lowering=False)
    fw = FW(nc)

    def din(n, s, dt=F32):
        return nc.dram_tensor(n, list(s), dt, kind="ExternalInput").ap()

    xT = din("xT", [D, NT])
    uT = din("uT", [D, NT], BF16)
    cvec = din("cvec", [128, 64])
    wmod = din("wmod", [D, 3 * D])
    bmod = din("bmod", [128, 96])
    normw = din("normw", [128, 32])
    wg = din("wg", [D, 4 * D])
    wbr = din("wbr", [1024, 4 * D])
    wout = din("wout", [D, D])
    cst = din("cst", [128, 128 * len(CST_NAMES)])
    x1T = nc.dram_tensor("x1T", [D, NT], F32, kind="ExternalOutput").ap()
    xnT = nc.dram_tensor("xnTb", [D, NT], BF16).ap()
    d_xnT, d_x1 = Dep(), Dep()
    xT_v = xT.rearrange("(kc p) t -> p kc t", p=128)
    xnT_v = xnT.rearrange("(kc p) t -> p kc t", p=128)
    uT_v = uT.rearrange("(kc p) t -> p kc t", p=128)
    x1T_v = x1T.rearrange("(kc p) t -> p kc t", p=128)
    wg_v = wg.rearrange("(kc p) n -> p kc n", p=128)
    wbr_v = wbr.rearrange("(kc p) n -> p kc n", p=128)
    wout_v = wout.rearrange("(kc p) n -> p kc n", p=128)

    cf = fw.sb([128, 256], F32, "cf")
    d_cf = Dep()
    fw.dma("sp", cf[:], cst[:, 0:256], d_cf, writes=[d_cf])
    cb16 = fw.sb([128, 256], BF16, "cb16")
    d_cb = Dep()
    fw.op("dve", lambda e: e.tensor_copy(out=cb16[:], in_=cf[:]), reads=[d_cf], writes=[d_cb])

    def CB(n):
        i = CST_NAMES.index(n)
        return cb16[:, i * 128:(i + 1) * 128]

    P = [fw.ps([128, 512], F32, f"bank{i}") for i in range(8)]
    dP = [Dep(excl=True) for _ in range(8)]
    mod = fw.sb([128, 2, 96], F32, "mod")
    d_mod = Dep()
    NBN = 256
    nblocks = [(0, CTX, 1)] + [(CTX + i * NBN, NBN, 0) for i in range(NL_own // NBN)]
    emit_cond_norm(fw, P, dP, CB, d_cb, cvec, wmod, bmod, normw, xT_v, xnT_v, d_xnT, nblocks, 96, mod, d_mod)

    TB = 512
    blocks = [(0, CTX, 1)] + [(CTX + i * TB, TB, 0) for i in range(NL_own // TB)]
    xk = fw.sb([128, 32, TB], BF16, "xk")
    uk = fw.sb([128, 32, TB], BF16, "uk")
    acc = fw.sb([128, 32, TB], BF16, "acc")
    d_xk, d_uk = Dep(), Dep()
    d_acc = deps(32)
    WG = [fw.sb([128, 32, 512], BF16, f"WG{i}") for i in range(2)]
    d_WG = deps(2)
    WBr = [fw.sb([128, 8, 512], BF16, f"WBr{i}") for i in range(2)]
    d_WBr = deps(2)
    sg = [fw.sb([128, TB], F32, f"sg{i}") for i in range(2)]
    d_sg = deps(2)
    a32 = fw.sb([128, TB], F32, "a32")
    d_a32 = Dep()
    xo = [fw.sb([128, TB], F32, f"xo{i}") for i in range(2)]
    d_xo = deps(2)
    yo = [fw.sb([128, TB], F32, f"yo{i}") for i in range(2)]
    d_yo = deps(2)
    wcnt = [0]

    def load_slab(dst, d_dst, src_v, nk, c0, ncol):
        per = 4 if nk >= 8 else nk
        for h in range(nk // per):
            fw.dma("pool", dst[:, h * per:(h + 1) * per, 0:ncol], src_v[:, h * per:(h + 1) * per, c0:c0 + ncol], d_dst,
                   writes=[d_dst] if h == 0 else (), accw=() if h == 0 else [d_dst])

    for bi, (st, n, s) in enumerate(blocks):
        for h in range(2):
            fw.dma("sp", xk[:, h * 16:(h + 1) * 16, 0:n], xnT_v[:, h * 16:(h + 1) * 16, st:st + n], d_xk,
                   reads=[d_xnT], writes=[d_xk] if h == 0 else (), accw=() if h == 0 else [d_xk])
            fw.dma("sp", uk[:, h * 16:(h + 1) * 16, 0:n], uT_v[:, h * 16:(h + 1) * 16, st:st + n], d_uk,
                   writes=[d_uk] if h == 0 else (), accw=() if h == 0 else [d_uk])
        for cc in range(32):
            wb = wcnt[0] % 2
            wcnt[0] += 1
            load_slab(WG[wb], d_WG[wb], wg_v, 32, cc * 512, 512)
            load_slab(WBr[wb], d_WBr[wb], wbr_v, 8, cc * 512, 512)
            for i in range(4):
                gb = i % 2
                for kc in range(32):
                    fw.op("pe", lambda e: e.matmul(P[gb][:, 0:n], lhsT=WG[wb][:, kc, i * 128:(i + 1) * 128], rhs=xk[:, kc, 0:n],
                                                   start=(kc == 0), stop=(kc == 31)), reads=[d_WG[wb], d_xk], writes=[dP[gb]])
                for k8 in range(8):
                    fw.op("pe", lambda e: e.matmul(P[2 + gb][:, 0:n], lhsT=WBr[wb][:, k8, i * 128:(i + 1) * 128], rhs=uk[:, i * 8 + k8, 0:n],
                                                   start=(k8 == 0), stop=(k8 == 7)), reads=[d_WBr[wb], d_uk], writes=[dP[2 + gb]])
                fw.op("act", lambda e: e.activation(out=sg[gb][:, 0:n], in_=P[gb][:, 0:n], func=AF.Sigmoid), reads=[dP[gb]], writes=[d_sg[gb]])
                if i == 0:
                    fw.op("dve", lambda e: e.tensor_tensor(out=a32[:, 0:n], in0=sg[gb][:, 0:n], in1=P[2 + gb][:, 0:n], op=ALU.mult),
                          reads=[d_sg[gb], dP[2 + gb]], writes=[d_a32])
                else:
                    fw.op("dve", lambda e: e.tensor_tensor(out=sg[gb][:, 0:n], in0=sg[gb][:, 0:n], in1=P[2 + gb][:, 0:n], op=ALU.mult),
                          reads=[d_sg[gb], dP[2 + gb]], writes=[d_sg[gb]])
                    if i < 3:
                        fw.op("dve", lambda e: e.tensor_tensor(out=a32[:, 0:n], in0=a32[:, 0:n], in1=sg[gb][:, 0:n], op=ALU.add),
                              reads=[d_sg[gb], d_a32], writes=[d_a32])
                    else:
                        fw.op("dve", lambda e: e.tensor_tensor(out=acc[:, cc, 0:n], in0=a32[:, 0:n], in1=sg[gb][:, 0:n], op=ALU.add),
                              reads=[d_sg[gb], d_a32], writes=[d_acc[cc]])
        for oc in range(32):
            wb = wcnt[0] % 2
            wcnt[0] += 1
            ob = oc % 2
            load_slab(WG[wb], d_WG[wb], wout_v, 32, oc * 128, 128)
            fw.dma("sp", xo[ob][:, 0:n], xT_v[:, oc, st:st + n], d_xo[ob], writes=[d_xo[ob]])
            for kc in range(32):
                fw.op("pe", lambda e: e.matmul(P[4 + ob][:, 0:n], lhsT=WG[wb][:, kc, 0:128], rhs=acc[:, kc, 0:n],
                                               start=(kc == 0), stop=(kc == 31)), reads=[d_WG[wb], d_acc[kc]], writes=[dP[4 + ob]])
            fw.op("dve", lambda e: e.scalar_tensor_tensor(out=yo[ob][:, 0:n], in0=P[4 + ob][:, 0:n], scalar=mod[:, s, 64 + oc:65 + oc],
                                                          in1=xo[ob][:, 0:n], op0=ALU.mult, op1=ALU.add),
                  reads=[dP[4 + ob], d_mod, d_xo[ob]], writes=[d_yo[ob]])
            fw.dma("sp", x1T_v[:, oc, st:st + n], yo[ob][:, 0:n], d_yo[ob], reads=[d_yo[ob]], accw=[d_x1])
    fw.finish("sp", [d_x1])
    fw.stack.close()
    return nc


_NC = {}


def _nc(kind, arg):
    key = (kind, arg)
    if key not in _NC:
        _NC[key] = build_A(arg, do=("d", "b", "c", "a")) if kind == "A" else build_B(arg)
    return _NC[key]


def _forward(inp, nl_own):
    x = np.asarray(inp["x"])[0]
    ctx = np.asarray(inp["ctx"])[0]
    TL = x.shape[0]
    TA = CTX + TL
    depth = np.asarray(inp["w_in"]).shape[0]
    xT_all = np.ascontiguousarray(np.concatenate([ctx, x], axis=0).T.astype(np.float32))
    nb = TL // nl_own
    for l in range(depth):
        mapsA = prep_A(inp, l, xT_all, TL)
        resA = run_bass_kernel_spmd(_nc("A", TL), mapsA, core_ids=list(range(8)))
        uT_full = np.empty((4 * W_BR, TA), ml_dtypes.bfloat16)
        for c in range(8):
            u = np.asarray(resA.results[c]["uT"])
            for b in range(4):
                uT_full[b * W_BR + 128 * c:b * W_BR + 128 * c + 128] = u[128 * b:128 * b + 128]
        del resA, mapsA
        mapsB = prep_B(inp, l, xT_all, uT_full, nl_own)
        resB = run_bass_kernel_spmd(_nc("B", nl_own), mapsB, core_ids=list(range(nb)))
        new = np.empty_like(xT_all)
        new[:, 0:CTX] = np.asarray(resB.results[0]["x1T"])[:, 0:CTX]
        for c in range(nb):
            new[:, CTX + nl_own * c:CTX + nl_own * (c + 1)] = np.asarray(resB.results[c]["x1T"])[:, CTX:]
        xT_all = new
        del resB, mapsB
    return np.ascontiguousarray(xT_all[:, CTX:].T)[None].astype(np.float32)


def kernel(**inputs):
    inp = {k: np.asarray(v) for k, v in inputs.items()}
    return _forward(inp, NOWN)
```

```python
import math
import numpy as np
import ml_dtypes
from contextlib import ExitStack, contextmanager
import concourse.bass as bass
import concourse.mybir as mybir
from concourse.bass_utils import run_bass_kernel_spmd

F32 = mybir.dt.float32
BF16 = mybir.dt.bfloat16
AF = mybir.ActivationFunctionType
ALU = mybir.AluOpType
AX = mybir.AxisListType

D = 4096
KC = 32
CTX = 256
GRID_W = 64
W_BR = 1024
ROPE_THETA = 10000.0
NORM_EPS = 1e-6
O_AF, O_AZ, O_BF, O_BZ, O_CF, O_CZ, O_DF, O_DZ, O_G = 0, 3200, 4224, 7296, 8320, 11424, 12448, 13984, 15008
N_IN = 31392
import os
GSTOP = int(os.environ.get('GSTOP', '99'))
GLIM = int(os.environ.get('GLIM', '100000'))
GFWD = int(os.environ.get('GFWD', '0'))
GNL = int(os.environ.get('GNL', '6'))
GBLK = int(os.environ.get('GBLK', '1000'))
GSW = int(os.environ.get('GSW', '2'))
GBANKS = int(os.environ.get('GBANKS', '1'))


class Dep:
    __slots__ = ("w", "r", "sem", "cnt", "excl")

    def __init__(self, excl=False):
        self.w = {}
        self.r = {}
        self.sem = None
        self.cnt = 0
        self.excl = excl


def deps(n):
    return [Dep() for _ in range(n)]


class FW:
    def __init__(self, nc):
        self.nc = nc
        self.stack = ExitStack()
        self.stacks = [self.stack]
        self.E = {"pe": nc.tensor, "dve": nc.vector, "act": nc.scalar, "pool": nc.gpsimd, "sp": nc.sync}
        self.sems = []
        self.semi = {}
        self.cnt = {}
        self.waited = {k: {} for k in self.E}
        self.dma_deps = []
        for k in ("pe", "dve", "act", "pool"):
            self.semi[k] = self.newsem("e_" + k)
            self.cnt[k] = 0
        self.nt = 0

    def newsem(self, name):
        h = self.stack.enter_context(self.nc.semaphore(name))
        self.sems.append(h)
        return len(self.sems) - 1

    @contextmanager
    def scope(self):
        st = ExitStack()
        self.stacks.append(st)
        try:
            yield
        finally:
            self.stacks.pop()
            st.close()

    def sb(self, shape, dt, name=None):
        self.nt += 1
        return self.stacks[-1].enter_context(self.nc.sbuf_tensor(name or f"t{self.nt}", list(shape), dt))

    def ps(self, shape, dt=F32, name=None):
        self.nt += 1
        return self.stack.enter_context(self.nc.psum_tensor(name or f"p{self.nt}", list(shape), dt))

    def _collect(self, reads, writes):
        t = {}
        for d in reads:
            for s, v in d.w.items():
                if t.get(s, 0) < v:
                    t[s] = v
            if d.excl:
                for s, v in d.r.items():
                    if t.get(s, 0) < v:
                        t[s] = v
        for d in writes:
            for dd in (d.w, d.r):
                for s, v in dd.items():
                    if t.get(s, 0) < v:
                        t[s] = v
        return t

    def _wait(self, e, toks):
        w = self.waited[e]
        for s, v in toks.items():
            if e == "pe" and s == self.semi["pe"]:
                continue
            if w.get(s, 0) >= v:
                continue
            self.E[e].wait_ge(self.sems[s], v)
            w[s] = v

    def op(self, e, thunk, reads=(), writes=(), accw=()):
        self._wait(e, self._collect(reads, writes))
        ins = thunk(self.E[e])
        self.cnt[e] += 1
        s = self.semi[e]
        v = self.cnt[e]
        ins.then_inc(self.sems[s], 1)
        for d in reads:
            d.r[s] = v
        for d in writes:
            d.w = {s: v}
            d.r = {}
        for d in accw:
            d.w[s] = v

    def dma(self, q, out, in_, sdep, reads=(), writes=(), accw=()):
        self._wait(q, self._collect(reads, writes))
        ins = self.E[q].dma_start(out=out, in_=in_)
        if sdep.sem is None:
            sdep.sem = self.newsem(f"d{len(self.sems)}")
            self.dma_deps.append(sdep)
        sdep.cnt += 16
        ins.then_inc(self.sems[sdep.sem], 16)
        s, v = sdep.sem, sdep.cnt
        for d in reads:
            d.r[s] = v
        for d in writes:
            d.w = {s: v}
            d.r = {}
        for d in accw:
            d.w[s] = v

    def finish(self, e, dl):
        self._wait(e, self._collect(dl, ()))

    def barrier(self):
        t = {self.semi[k]: self.cnt[k] for k in self.cnt if self.cnt[k] > 0}
        for d in self.dma_deps:
            t[d.sem] = d.cnt
        for e in self.E:
            self._wait(e, t)


def _consts():
    c = {}
    c["ident"] = np.eye(128, dtype=np.float32)
    c["ones"] = np.ones((128, 128), np.float32)
    b = np.zeros((128, 128), np.float32)
    b[:64, :64] = 1
    b[64:, 64:] = 1
    c["blk64"] = b
    pd = np.zeros((128, 128), np.float32)
    for m in range(128):
        pd[m + 32 if (m % 64) < 32 else m - 32, m] = 1
    c["permD"] = pd
    pb = np.zeros((128, 128), np.float32)
    for m in range(128):
        pb[m + 16 if (m % 32) < 16 else m - 16, m] = 1
    c["permB"] = pb
    BIG = 30000.0
    i = np.arange(128)[:, None]
    j = np.arange(128)[None, :]

    def blk(m):
        return np.ascontiguousarray(m.astype(np.float32))
    c["UI"] = blk((i <= j).astype(np.float32))
    c["LI"] = blk((i >= j).astype(np.float32))
    c["SU"] = blk((i < j).astype(np.float32))
    c["SL"] = blk((i > j).astype(np.float32))
    c["POS_SL"] = blk(np.where(i > j, 0.0, BIG))
    c["POS_SU"] = blk(np.where(i < j, 0.0, BIG))
    c["NEG_SU"] = blk(np.where(i < j, 0.0, -BIG))
    c["NEG_SL"] = blk(np.where(i > j, 0.0, -BIG))
    c["NEG_UI"] = blk(np.where(i <= j, 0.0, -BIG))
    c["NEG_LI"] = blk(np.where(i >= j, 0.0, -BIG))
    return c


CST_NAMES = ["ident", "ones", "blk64", "permD", "permB", "UI", "LI", "SU", "SL", "POS_SL", "POS_SU", "NEG_SU", "NEG_SL", "NEG_UI", "NEG_LI"]


def _cst_array():
    c = _consts()
    return np.concatenate([c[n] for n in CST_NAMES], axis=1)


def _rope_tables(TL):
    t = np.arange(TL)
    rows = (t // GRID_W).astype(np.float32)
    cols = (t % GRID_W).astype(np.float32)

    def tab(head_dim, nrep):
        m = head_dim // 2
        inv = np.power(np.float32(ROPE_THETA), -np.arange(0, m, 2, dtype=np.float32) / np.float32(m)).astype(np.float32)
        q = head_dim // 4
        cos = np.zeros((head_dim, TL), np.float32)
        sin = np.zeros((head_dim, TL), np.float32)
        for p in range(head_dim):
            pos = rows if p < m else cols
            i = p % q
            ang = (pos * inv[i]).astype(np.float32)
            cos[p] = np.cos(ang)
            sg = -1.0 if (p % m) < q else 1.0
            sin[p] = sg * np.sin(ang)
        return np.tile(cos, (nrep, 1)), np.tile(sin, (nrep, 1))

    cd, sd = tab(128, 1)
    cb, sbb = tab(64, 2)
    return cd, sd, cb, sbb


def _fm(v):
    v = np.asarray(v, np.float32).reshape(-1, 128)
    return np.ascontiguousarray(v.T)


PV = {"qnD": 0, "knD": 1, "qnB": 2, "knB": 3, "subln": 4, "conv": 5, "gnorm": 14, "alog": 15, "dtb": 17, "mu": 19, "w0": 23, "a0": 25, "kk": 27, "ka": 28, "rk": 29, "lnw": 30, "lnb": 31}
NPV = 64
NCA_D = 512
NCA_B = 512
NCA_C = 516
NCA_A = 640
NCA = NCA_D + NCA_B + NCA_C + NCA_A


def prep_A(inp, l, xT_all, TL):
    cd, sd, cb, sbb = _rope_tables(TL)
    cst = _cst_array()
    cvec = np.stack([_fm(inp["c"][0]), _fm(inp["c_ctx"])], axis=2).reshape(128, 64)
    wmod = np.ascontiguousarray(inp["w_mod"][l][:, :2 * D])
    bmod = _fm(inp["b_mod"][l][:2 * D])
    normw = _fm(inp["norm_w"][l])
    w_in = inp["w_in"][l]
    lam_init = 0.8 - 0.6 * math.exp(-0.3 * l)
    lamv = np.concatenate([inp["diff_lam"][l].reshape(1, 256), np.full((1, 1), lam_init, np.float32)], axis=1).astype(np.float32)
    maps = []
    for c in range(8):
        n = c // 4
        cols = []
        cols += list(range(O_DF + 128 * c, O_DF + 128 * c + 128))
        cols += list(range(O_DF + 1024 + 128 * n, O_DF + 1024 + 128 * n + 128))
        cols += list(range(O_DF + 1280 + 128 * n, O_DF + 1280 + 128 * n + 128))
        cols += list(range(O_DZ + 128 * c, O_DZ + 128 * c + 128))
        for o in (O_BF, O_BF + 1024, O_BF + 2048, O_BZ):
            cols += list(range(o + 128 * c, o + 128 * c + 128))
        for o in (O_CF, O_CF + 1024, O_CF + 2048, O_CZ):
            cols += list(range(o + 128 * c, o + 128 * c + 128))
        cols += [O_CF + 3072 + c, O_CF + 3072 + 8 + c, O_CF + 3088 + c, O_CF + 3088 + 8 + c]
        for o in (O_AF, O_AF + 1024, O_AF + 2048, O_AZ):
            cols += list(range(o + 128 * c, o + 128 * c + 128))
        cols += list(range(O_AF + 3072, O_AF + 3200))
        wA = np.ascontiguousarray(w_in[:, cols])
        cs_ = slice(128 * c, 128 * c + 128)
        lora = np.concatenate([np.concatenate([inp["rwkv_w2"][l][d_][:, cs_], inp["rwkv_a2"][l][d_][:, cs_]], axis=0) for d_ in range(2)], axis=1)
        lora = np.ascontiguousarray(lora.astype(np.float32))
        pv = np.zeros((128, NPV), np.float32)
        pv[:, PV["qnD"]] = inp["gqa_qn"][l]
        pv[:, PV["knD"]] = inp["gqa_kn"][l]
        pv[:, PV["qnB"]] = np.tile(inp["diff_qn"][l], 2)
        pv[:, PV["knB"]] = np.tile(inp["diff_kn"][l], 2)
        pv[:, PV["subln"]] = inp["diff_subln"][l]
        for j in range(3):
            for i in range(3):
                pv[:, PV["conv"] + j * 3 + i] = inp["gdn_conv"][l][i, j * 1024 + 128 * c:j * 1024 + 128 * c + 128]
        pv[:, PV["gnorm"]] = inp["gdn_norm"][l]
        for d_ in range(2):
            pv[:, PV["alog"] + d_] = inp["gdn_a_log"][l][d_, c]
            pv[:, PV["dtb"] + d_] = inp["gdn_dt_bias"][l][d_, c]
            pv[:, PV["w0"] + d_] = inp["rwkv_w0"][l][d_, cs_]
            pv[:, PV["a0"] + d_] = inp["rwkv_a0"][l][d_, cs_]
        mu_ = inp["rwkv_mu"][l]
        for j in range(3):
            pv[:, PV["mu"] + j] = mu_[j * 1024 + 128 * c:j * 1024 + 128 * c + 128]
        pv[:, PV["mu"] + 3] = mu_[3072:3200]
        pv[:, PV["kk"]] = inp["rwkv_kk"][l][cs_]
        pv[:, PV["ka"]] = inp["rwkv_ka"][l][cs_]
        pv[:, PV["rk"]] = inp["rwkv_rk"][l].reshape(-1)[cs_]
        pv[:, PV["lnw"]] = inp["rwkv_ln_w"][l][cs_]
        pv[:, PV["lnb"]] = inp["rwkv_ln_b"][l][cs_]
        maps.append({"xT": xT_all, "cvec": cvec, "wmod": wmod, "bmod": bmod, "normw": normw, "wA": wA,
                     "cst": cst, "pvec": pv, "lamv": lamv, "lora": lora, "ropeDc": cd, "ropeDs": sd, "ropeBc": cb, "ropeBs": sbb})
    return maps


def emit_cond_norm(fw, P, dP, CB, d_cb, cvec, wmod, bmod, normw, xT_v, xnT_v, d_xnT, nblocks, NMOD, mod, d_mod):
    sc_ = fw.scope()
    sc_.__enter__()
    cv = fw.sb([128, 64], F32, "cv")
    d_cv = Dep()
    fw.dma("sp", cv[:], cvec[:, :], d_cv, writes=[d_cv])
    scv = fw.sb([128, 64], F32, "scv")
    d_scv = Dep()
    fw.op("act", lambda e: e.activation(out=scv[:], in_=cv[:], func=AF.Silu), reads=[d_cv], writes=[d_scv])
    bm = fw.sb([128, NMOD], F32, "bm")
    d_bm = Dep()
    fw.dma("sp", bm[:], bmod[:, :], d_bm, writes=[d_bm])
    nw = fw.sb([128, 32], F32, "nw")
    d_nw = Dep()
    fw.dma("sp", nw[:], normw[:, :], d_nw, writes=[d_nw])
    wm = [fw.sb([128, 32, 256], F32, f"wm{i}") for i in range(2)]
    d_wm = deps(2)
    wmod_v = wmod.rearrange("(kc p) n -> p kc n", p=128)
    modp = P[0]
    for j in range(NMOD // 2):
        b = j % 2
        for h in range(4):
            fw.dma("sp", wm[b][:, h * 8:(h + 1) * 8, :], wmod_v[:, h * 8:(h + 1) * 8, j * 256:(j + 1) * 256], d_wm[b],
                   writes=[d_wm[b]] if h == 0 else (), accw=() if h == 0 else [d_wm[b]])
        for cc in range(2):
            ch = j * 2 + cc
            for kc in range(32):
                fw.op("pe", lambda e: e.matmul(modp[:, ch * 2:ch * 2 + 2], lhsT=wm[b][:, kc, cc * 128:(cc + 1) * 128],
                                               rhs=scv[:, kc * 2:kc * 2 + 2], start=(kc == 0), stop=(kc == 31)),
                      reads=[d_wm[b], d_scv], writes=[dP[0]])
    modp_v = modp[:, 0:2 * NMOD].rearrange("p (c s) -> p s c", s=2)
    for s in range(2):
        fw.op("dve", lambda e: e.tensor_tensor(out=mod[:, s, :], in0=modp_v[:, s, :], in1=bm[:], op=ALU.add),
              reads=[dP[0], d_bm], writes=[d_mod])
    g = fw.sb([128, 2, 32], F32, "g")
    d_g = Dep()
    for s in range(2):
        fw.op("dve", lambda e: e.scalar_tensor_tensor(out=g[:, s, :], in0=mod[:, s, 32:64], scalar=1.0, in1=nw[:],
                                                      op0=ALU.add, op1=ALU.mult),
              reads=[d_mod, d_nw], writes=[d_g])

    NB = 256
    xb = [fw.sb([128, 32, NB], F32, f"xb{i}") for i in range(2)]
    d_xb = deps(2)
    xnb = [fw.sb([128, 32, NB], BF16, f"xnb{i}") for i in range(2)]
    d_xnb = [deps(32) for _ in range(2)]
    d_xnbo = deps(2)
    sq = [fw.sb([128, NB], BF16, f"sq{i}") for i in range(4)]
    d_sq = deps(4)
    rstd = [fw.sb([128, NB], F32, f"rstd{i}") for i in range(2)]
    d_rstd = deps(2)
    tmp = [fw.sb([128, NB], F32, f"tmp{i}") for i in range(4)]
    d_tmp = deps(4)
    for bi, (st, n, s) in enumerate(nblocks):
        b = bi % 2
        for h in range(4):
            fw.dma("sp", xb[b][:, h * 8:(h + 1) * 8, :], xT_v[:, h * 8:(h + 1) * 8, st:st + n], d_xb[b],
                   writes=[d_xb[b]] if h == 0 else (), accw=() if h == 0 else [d_xb[b]])
        ssp = P[1 + b]
        for kc in range(32):
            q = kc % 4
            fw.op("act", lambda e: e.activation(out=sq[q][:], in_=xb[b][:, kc, :], func=AF.Square),
                  reads=[d_xb[b]], writes=[d_sq[q]])
            fw.op("pe", lambda e: e.matmul(ssp[:, 0:NB], lhsT=CB("ones"), rhs=sq[q][:], start=(kc == 0), stop=(kc == 31)),
                  reads=[d_sq[q], d_cb], writes=[dP[1 + b]])
        fw.op("act", lambda e: e.activation(out=rstd[b][:], in_=ssp[:, 0:NB], func=AF.Sqrt, scale=1.0 / D, bias=NORM_EPS),
              reads=[dP[1 + b]], writes=[d_rstd[b]])
        fw.op("dve", lambda e: e.reciprocal(out=rstd[b][:], in_=rstd[b][:]), reads=[d_rstd[b]], writes=[d_rstd[b]])
        for kc in range(32):
            q = kc % 4
            fw.op("dve", lambda e: e.scalar_tensor_tensor(out=tmp[q][:], in0=xb[b][:, kc, :], scalar=g[:, s, kc:kc + 1],
                                                          in1=rstd[b][:], op0=ALU.mult, op1=ALU.mult),
                  reads=[d_xb[b], d_g, d_rstd[b]], writes=[d_tmp[q]])
            fw.op("act", lambda e: e.activation(out=xnb[b][:, kc, :], in_=tmp[q][:], func=AF.Identity,
                                                bias=mod[:, s, kc:kc + 1]),
                  reads=[d_tmp[q], d_mod], writes=[d_xnb[b][kc]])
        fw.dma("pool", xnT_v[:, :, st:st + n], xnb[b][:], d_xnbo[b], reads=d_xnb[b], accw=[d_xnT])
    fw.barrier()

    sc_.__exit__(None, None, None)


def build_A(TL, do=("d", "b"), lam_l=0):
    TA = CTX + TL
    NKC = TA // 128
    nc = bass.Bass("TRN2", target_bir_lowering=False)
    fw = FW(nc)

    def din(n, s, dt=F32):
        return nc.dram_tensor(n, list(s), dt, kind="ExternalInput").ap()

    xT = din("xT", [D, TA])
    cvec = din("cvec", [128, 64])
    wmod = din("wmod", [D, 2 * D])
    bmod = din("bmod", [128, 64])
    normw = din("normw", [128, 32])
    wA = din("wA", [D, NCA])
    cst = din("cst", [128, 128 * len(CST_NAMES)])
    pvec = din("pvec", [128, NPV])
    lamv = din("lamv", [1, 257])
    lora = din("lora", [128, 256])
    ropeDc = din("ropeDc", [128, TL])
    ropeDs = din("ropeDs", [128, TL])
    ropeBc = din("ropeBc", [128, TL])
    ropeBs = din("ropeBs", [128, TL])
    uT = nc.dram_tensor("uT", [512, TA], BF16, kind="ExternalOutput").ap()
    DBG = int(os.environ.get("GDBG", "0"))
    dbg_map = {}
    if DBG:
        dbgT = nc.dram_tensor("dbg", [128, 4096], F32, kind="ExternalOutput").ap()
    dbg_state = {"col": 0}
    d_dbg = Dep()

    def dbg_dump(name, ap, dep, rows, cols):
        if not DBG:
            return
        c0_ = dbg_state["col"]
        if cols == 1:
            c0_ += c0_ % 2
            cols2 = 2
            fw.dma("sp", dbgT[0:rows, c0_:c0_ + 1], ap, dep, reads=[dep], accw=[d_dbg]) if False else None
            ins_ = fw.E["sp"]
            fw._wait("sp", fw._collect([dep], ()))
            i_ = ins_.dma_start(out=dbgT[0:rows, c0_:c0_ + 1], in_=ap, allow_slow_non_contiguous=True)
            if dep.sem is None:
                dep.sem = fw.newsem(f"d{len(fw.sems)}")
                fw.dma_deps.append(dep)
            dep.cnt += 16
            i_.then_inc(fw.sems[dep.sem], 16)
            dep.r[dep.sem] = dep.cnt
            d_dbg.w[dep.sem] = dep.cnt
            cols = 2
        else:
            fw.dma("sp", dbgT[0:rows, c0_:c0_ + cols], ap, dep, reads=[dep], accw=[d_dbg])
        dbg_map[name] = (rows, c0_, 1 if name.endswith(("Gc", "eGl")) else cols)
        dbg_state["col"] = c0_ + cols
    build_A.dbg_map = dbg_map
    xnT = nc.dram_tensor("xnT", [D, TA], BF16).ap()
    d_uT = Dep()
    d_xnT = Dep()
    xT_v = xT.rearrange("(kc p) t -> p kc t", p=128)
    xnT_v = xnT.rearrange("(kc p) t -> p kc t", p=128)
    wA_v = wA.rearrange("(kc p) n -> p kc n", p=128)

    cf = fw.sb([128, 128 * len(CST_NAMES)], F32, "cf")
    d_cf = Dep()
    fw.dma("sp", cf[:], cst[:, :], d_cf, writes=[d_cf])
    cb16 = fw.sb([128, 128 * len(CST_NAMES)], BF16, "cb16")
    d_cb = Dep()
    fw.op("dve", lambda e: e.tensor_copy(out=cb16[:], in_=cf[:]), reads=[d_cf], writes=[d_cb])

    def CF(n):
        i = CST_NAMES.index(n)
        return cf[:, i * 128:(i + 1) * 128]

    def CB(n):
        i = CST_NAMES.index(n)
        return cb16[:, i * 128:(i + 1) * 128]

    pvs = fw.sb([128, NPV], F32, "pvs")
    d_pv = Dep()
    fw.dma("sp", pvs[:], pvec[:, :], d_pv, writes=[d_pv])

    P = [fw.ps([128, 512], F32, f"bank{i}") for i in range(8)]
    dP = [Dep(excl=True) for _ in range(8)]

    mod = fw.sb([128, 2, 64], F32, "mod")
    d_mod = Dep()
    NBN = 256
    nblocks = [(0, CTX, 1)] + [(CTX + i * NBN, NBN, 0) for i in range(TL // NBN)]
    emit_cond_norm(fw, P, dP, CB, d_cb, cvec, wmod, bmod, normw, xT_v, xnT_v, d_xnT, nblocks, 64, mod, d_mod)

    _sc2 = fw.scope()
    _sc2.__enter__()
    TB = 512
    blocks = [(0, CTX, 1)] + [(CTX + i * TB, TB, 0) for i in range(TL // TB)]
    Wb = fw.sb([128, 32, 512], BF16, "Wb")
    d_W = Dep()
    xk = [fw.sb([128, 32, TB], BF16, "xk0")] * 2
    d_xk = [Dep()] * 2
    qT = fw.sb([128, TA], BF16, "qT")
    kT = fw.sb([128, TA], BF16, "kT")
    vA = fw.sb([128, NKC, 128], BF16, "vA")
    szD = nc.dram_tensor("szD", [128, TA], BF16).ap()
    szst = [fw.sb([128, TB], BF16, f"szst{i}") for i in range(2)]
    d_szst = deps(2)
    d_q, d_k, d_v, d_sz = Dep(), Dep(), Dep(), Dep()
    cosb = [fw.sb([128, TB], F32, "cosb0")] * 2
    sinb = [fw.sb([128, TB], F32, "sinb0")] * 2
    d_cos = [Dep()] * 2
    d_sin = [Dep()] * 2
    sqb = fw.sb([128, TB], BF16, "sqb")
    d_sqb = Dep()
    rs = fw.sb([128, TB], F32, "rs")
    d_rs = Dep()
    qn = fw.sb([128, TB], F32, "qn")
    d_qn = Dep()
    t1 = fw.sb([128, TB], F32, "t1")
    t2 = fw.sb([128, TB], F32, "t2")
    d_t1, d_t2 = Dep(), Dep()
    pT = [fw.sb([128, TB], BF16, f"pT{i}") for i in range(4)]
    d_pT = deps(4)
    ust = [fw.sb([128, TB], BF16, f"ust{i}") for i in range(2)]
    d_ust = deps(2)
    lam_sb = fw.sb([128, 257], F32, "lam_sb")
    d_lam = Dep()
    lamw = fw.sb([128, 8], F32, "lamw")
    d_lamw = Dep()

    def load_W(c0, ncol):
        for h in range(8):
            fw.dma("pool", Wb[:, h * 4:(h + 1) * 4, 0:ncol], wA_v[:, h * 4:(h + 1) * 4, c0:c0 + ncol], d_W,
                   writes=[d_W] if h == 0 else (), accw=() if h == 0 else [d_W])

    def load_x(bi, st, n):
        b = bi % 2
        for h in range(2):
            fw.dma("sp", xk[b][:, h * 16:(h + 1) * 16, 0:n], xnT_v[:, h * 16:(h + 1) * 16, st:st + n], d_xk[b],
                   reads=[d_xnT], writes=[d_xk[b]] if h == 0 else (), accw=() if h == 0 else [d_xk[b]])
        return b

    def proj_fm(bank, c0, ncol, b, n):
        for kc in range(32):
            fw.op("pe", lambda e: e.matmul(P[bank][0:ncol, 0:n], lhsT=Wb[:, kc, c0:c0 + ncol], rhs=xk[b][:, kc, 0:n],
                                           start=(kc == 0), stop=(kc == 31)),
                  reads=[d_W, d_xk[b]], writes=[dP[bank]])

    def proj_tm(bank, c0, ncol, b, t0, nt):
        for kc in range(32):
            fw.op("pe", lambda e: e.matmul(P[bank][0:nt, 0:ncol], lhsT=xk[b][:, kc, t0:t0 + nt], rhs=Wb[:, kc, c0:c0 + ncol],
                                           start=(kc == 0), stop=(kc == 31)),
                  reads=[d_W, d_xk[b]], writes=[dP[bank]])

    def headnorm_rope(bank, n, wcol, ones_name, hd, perm, cosT, sinT, st, rope, dst, d_dst, bi):
        fw.op("act", lambda e: e.activation(out=sqb[:, 0:n], in_=P[bank][:, 0:n], func=AF.Square),
              reads=[dP[bank]], writes=[d_sqb])
        fw.op("pe", lambda e: e.matmul(P[3][:, 0:n], lhsT=CB(ones_name), rhs=sqb[:, 0:n], start=True, stop=True),
              reads=[d_sqb, d_cb], writes=[dP[3]])
        fw.op("act", lambda e: e.activation(out=rs[:, 0:n], in_=P[3][:, 0:n], func=AF.Sqrt, scale=1.0 / hd, bias=NORM_EPS),
              reads=[dP[3]], writes=[d_rs])
        fw.op("dve", lambda e: e.reciprocal(out=rs[:, 0:n], in_=rs[:, 0:n]), reads=[d_rs], writes=[d_rs])
        if not rope:
            fw.op("dve", lambda e: e.scalar_tensor_tensor(out=dst[:, st:st + n], in0=P[bank][:, 0:n], scalar=pvs[:, wcol:wcol + 1],
                                                          in1=rs[:, 0:n], op0=ALU.mult, op1=ALU.mult),
                  reads=[dP[bank], d_pv, d_rs], accw=[d_dst])
            return
        fw.op("dve", lambda e: e.scalar_tensor_tensor(out=qn[:, 0:n], in0=P[bank][:, 0:n], scalar=pvs[:, wcol:wcol + 1],
                                                      in1=rs[:, 0:n], op0=ALU.mult, op1=ALU.mult),
              reads=[dP[bank], d_pv, d_rs], writes=[d_qn])
        fw.op("pe", lambda e: e.matmul(P[3][:, 0:n], lhsT=CF(perm), rhs=qn[:, 0:n], start=True, stop=True),
              reads=[d_qn, d_cf], writes=[dP[3]])
        cb_ = bi % 2
        fw.op("dve", lambda e: e.tensor_tensor(out=t1[:, 0:n], in0=qn[:, 0:n], in1=cosb[cb_][:, 0:n], op=ALU.mult),
              reads=[d_qn, d_cos[cb_]], writes=[d_t1])
        fw.op("dve", lambda e: e.tensor_tensor(out=t2[:, 0:n], in0=P[3][:, 0:n], in1=sinb[cb_][:, 0:n], op=ALU.mult),
              reads=[dP[3], d_sin[cb_]], writes=[d_t2])
        fw.op("dve", lambda e: e.tensor_tensor(out=dst[:, st:st + n], in0=t1[:, 0:n], in1=t2[:, 0:n], op=ALU.add),
              reads=[d_t1, d_t2], accw=[d_dst])

    def attn_pass(name, c0, hd, ones_name, perm, ropec, ropes, wq, wk, nsub, urow):
        load_W(c0, 512)
        for bi, (st, n, s) in enumerate(blocks):
            b = load_x(bi, st, n)
            rope = (s == 0)
            if rope:
                cb_ = bi % 2
                fw.dma("sp", cosb[cb_][:, 0:n], ropec[:, st - CTX:st - CTX + n], d_cos[cb_], writes=[d_cos[cb_]])
                fw.dma("sp", sinb[cb_][:, 0:n], ropes[:, st - CTX:st - CTX + n], d_sin[cb_], writes=[d_sin[cb_]])
            proj_fm(0, 0, 128, b, n)
            headnorm_rope(0, n, wq, ones_name, hd, perm, None, None, st, rope, qT, d_q, bi)
            proj_fm(1, 128, 128, b, n)
            headnorm_rope(1, n, wk, ones_name, hd, perm, None, None, st, rope, kT, d_k, bi)
            proj_fm(2, 384, 128, b, n)
            zb_ = bi % 2
            fw.op("act", lambda e: e.activation(out=szst[zb_][:, 0:n], in_=P[2][:, 0:n], func=AF.Silu),
                  reads=[dP[2]], writes=[d_szst[zb_]])
            fw.dma("sp", szD[:, st:st + n], szst[zb_][:, 0:n], d_szst[zb_], reads=[d_szst[zb_]], accw=[d_sz])
            for j in range(n // 128):
                bank = 4 + (j % 2)
                proj_tm(bank, 256, 128, b, j * 128, 128)
                fw.op("dve", lambda e: e.tensor_copy(out=vA[:, st // 128 + j, :], in_=P[bank][:, 0:128]),
                      reads=[dP[bank]], accw=[d_v])
        fw.barrier()
        scale = float(hd) ** -0.5
        if nsub == 2:
            fw.dma("sp", lam_sb[:], lamv[0:1, :].partition_broadcast(128), d_lam, writes=[d_lam])
            fw.op("dve", lambda e: e.tensor_tensor(out=t1[:, 0:64], in0=lam_sb[:, 0:64], in1=lam_sb[:, 64:128], op=ALU.mult),
                  reads=[d_lam], writes=[d_t1])
            fw.op("dve", lambda e: e.reduce_sum(out=lamw[:, 0:1], in_=t1[:, 0:64], axis=AX.X), reads=[d_t1], writes=[d_lamw])
            fw.op("dve", lambda e: e.tensor_tensor(out=t1[:, 0:64], in0=lam_sb[:, 128:192], in1=lam_sb[:, 192:256], op=ALU.mult),
                  reads=[d_lam], writes=[d_t1])
            fw.op("dve", lambda e: e.reduce_sum(out=lamw[:, 1:2], in_=t1[:, 0:64], axis=AX.X), reads=[d_t1], writes=[d_lamw])
            fw.op("act", lambda e: e.activation(out=lamw[:, 2:4], in_=lamw[:, 0:2], func=AF.Exp), reads=[d_lamw], writes=[d_lamw])
            fw.op("dve", lambda e: e.tensor_tensor(out=lamw[:, 4:5], in0=lamw[:, 2:3], in1=lamw[:, 3:4], op=ALU.subtract),
                  reads=[d_lamw], writes=[d_lamw])
            fw.op("dve", lambda e: e.tensor_tensor(out=lamw[:, 4:5], in0=lamw[:, 4:5], in1=lam_sb[:, 256:257], op=ALU.add),
                  reads=[d_lamw, d_lam], writes=[d_lamw])
            fw.op("dve", lambda e: e.tensor_scalar(out=lamw[:, 5:6], in0=lamw[:, 4:5], scalar1=-1.0, scalar2=None, op0=ALU.mult),
                  reads=[d_lamw], writes=[d_lamw])
            fw.op("dve", lambda e: e.tensor_scalar(out=lamw[:, 6:7], in0=lam_sb[:, 256:257], scalar1=-1.0, scalar2=1.0,
                                                   op0=ALU.mult, op1=ALU.add), reads=[d_lam, d_lamw], writes=[d_lamw])
            fw.op("dve", lambda e: e.tensor_tensor(out=lamw[:, 6:7], in0=lamw[:, 6:7], in1=pvs[:, PV["subln"]:PV["subln"] + 1], op=ALU.mult),
                  reads=[d_lamw, d_pv], writes=[d_lamw])
        hs = 128 // nsub
        for bi, (st, n, s) in enumerate(blocks):
            kcs = [0, 1] if s == 1 else list(range(NKC))
            nk = len(kcs)
            def S(i):
                for m in range(nsub):
                    bank = m * 2 + (i % 2)
                    kc = kcs[i]
                    fw.op("pe", lambda e: e.matmul(P[bank][:, 0:n], lhsT=kT[m * hs:(m + 1) * hs, kc * 128:(kc + 1) * 128],
                                                   rhs=qT[m * hs:(m + 1) * hs, st:st + n], start=True, stop=True),
                          reads=[d_k, d_q], writes=[dP[bank]])
            S(0)
            for i in range(nk):
                if i + 1 < nk:
                    S(i + 1)
                kc = kcs[i]
                for m in range(nsub):
                    bank = m * 2 + (i % 2)
                    pi = (i * nsub + m) % 4
                    fw.op("act", lambda e: e.activation(out=pT[pi][:, 0:n], in_=P[bank][:, 0:n], func=AF.Exp, scale=scale),
                          reads=[dP[bank]], writes=[d_pT[pi]])
                    fw.op("pe", lambda e: e.matmul(P[4 + m][:, 0:n], lhsT=vA[:, kc, :], rhs=pT[pi][:, 0:n], start=(i == 0), stop=(i == nk - 1)),
                          reads=[d_v, d_pT[pi]], writes=[dP[4 + m]])
                    fw.op("pe", lambda e: e.matmul(P[6 + m][:, 0:n], lhsT=CB("ones"), rhs=pT[pi][:, 0:n], start=(i == 0), stop=(i == nk - 1)),
                          reads=[d_cb, d_pT[pi]], writes=[dP[6 + m]])
            ub = bi % 2
            if nsub == 1:
                fw.op("dve", lambda e: e.reciprocal(out=rs[:, 0:n], in_=P[6][:, 0:n]), reads=[dP[6]], writes=[d_rs])
                fw.op("dve", lambda e: e.tensor_tensor(out=t1[:, 0:n], in0=P[4][:, 0:n], in1=rs[:, 0:n], op=ALU.mult),
                      reads=[dP[4], d_rs], writes=[d_t1])
            else:
                fw.op("dve", lambda e: e.reciprocal(out=rs[:, 0:n], in_=P[6][:, 0:n]), reads=[dP[6]], writes=[d_rs])
                fw.op("dve", lambda e: e.tensor_tensor(out=t1[:, 0:n], in0=P[4][:, 0:n], in1=rs[:, 0:n], op=ALU.mult),
                      reads=[dP[4], d_rs], writes=[d_t1])
                fw.op("dve", lambda e: e.reciprocal(out=rs[:, 0:n], in_=P[7][:, 0:n]), reads=[dP[7]], writes=[d_rs])
                fw.op("dve", lambda e: e.tensor_tensor(out=t2[:, 0:n], in0=P[5][:, 0:n], in1=rs[:, 0:n], op=ALU.mult),
                      reads=[dP[5], d_rs], writes=[d_t2])
                fw.op("dve", lambda e: e.scalar_tensor_tensor(out=t1[:, 0:n], in0=t2[:, 0:n], scalar=lamw[:, 5:6], in1=t1[:, 0:n],
                                                              op0=ALU.mult, op1=ALU.add), reads=[d_t2, d_lamw, d_t1], writes=[d_t1])
                fw.op("act", lambda e: e.activation(out=sqb[:, 0:n], in_=t1[:, 0:n], func=AF.Square), reads=[d_t1], writes=[d_sqb])
                fw.op("pe", lambda e: e.matmul(P[0][:, 0:n], lhsT=CB("ones"), rhs=sqb[:, 0:n], start=True, stop=True),
                      reads=[d_sqb, d_cb], writes=[dP[0]])
                fw.op("act", lambda e: e.activation(out=rs[:, 0:n], in_=P[0][:, 0:n], func=AF.Sqrt, scale=1.0 / 128, bias=NORM_EPS),
                      reads=[dP[0]], writes=[d_rs])
                fw.op("dve", lambda e: e.reciprocal(out=rs[:, 0:n], in_=rs[:, 0:n]), reads=[d_rs], writes=[d_rs])
                fw.op("dve", lambda e: e.scalar_tensor_tensor(out=t1[:, 0:n], in0=t1[:, 0:n], scalar=lamw[:, 6:7], in1=rs[:, 0:n],
                                                              op0=ALU.mult, op1=ALU.mult), reads=[d_t1, d_lamw, d_rs], writes=[d_t1])
            fw.dma("sp", szst[ub][:, 0:n], szD[:, st:st + n], d_szst[ub], reads=[d_sz], writes=[d_szst[ub]])
            fw.op("dve", lambda e: e.tensor_tensor(out=ust[ub][:, 0:n], in0=t1[:, 0:n], in1=szst[ub][:, 0:n], op=ALU.mult),
                  reads=[d_t1, d_szst[ub]], writes=[d_ust[ub]])
            fw.dma("sp", uT[urow:urow + 128, st:st + n], ust[ub][:, 0:n], d_ust[ub], reads=[d_ust[ub]], accw=[d_uT])
        fw.barrier()

    if "d" in do:
        attn_pass("d", 0, 128, "ones", "permD", ropeDc, ropeDs, PV["qnD"], PV["knD"], 1, 384)
    if "b" in do:
        attn_pass("b", NCA_D, 64, "blk64", "permB", ropeBc, ropeBs, PV["qnB"], PV["knB"], 2, 128)

    _sc2.__exit__(None, None, None)

    def T(shape, dt, name):
        return fw.sb(shape, dt, name), Dep()

    def psr(bank, r0, r1, c0, c1):
        return P[bank][r0:r1, c0:c1]

    def gdn_pass():
        c0 = NCA_D + NCA_B
        sc = fw.scope()
        sc.__enter__()
        Wg, d_Wg = T([128, 32, NCA_C], BF16, "Wg")
        for h in range(8):
            fw.dma("pool", Wg[:, h * 4:(h + 1) * 4, :], wA_v[:, h * 4:(h + 1) * 4, c0:c0 + NCA_C], d_Wg,
                   writes=[d_Wg] if h == 0 else (), accw=() if h == 0 else [d_Wg])
        xg = [fw.sb([128, 32, TB + 2], BF16, "xg0")] * 2
        d_xg = [Dep()] * 2
        oT_all, d_oT = T([128, TA], F32, "oT_all")
        raw = fw.sb([128, 3, TB + 2], F32, "raw")
        d_raw = deps(3)
        cvt, d_cvt = T([128, TB], F32, "cvt")
        act3 = fw.sb([128, 3, TB], F32, "act3")
        d_act3 = deps(3)
        sqg, d_sqg = T([128, TB], BF16, "sqg")
        rsg, d_rsg = T([128, TB], F32, "rsg")
        bgs, d_bgs = T([128, 4, 4], F32, "bgs")
        negA, d_negA = T([128, 2], F32, "negA")
        S, d_S = T([128, 128], F32, "S")
        Stmp, d_Stmp = T([128, 128], F32, "Stmp")
        gB, d_gB = T([128, 128], F32, "gB")
        bB, d_bB = T([128, 128], F32, "bB")
        Gc, d_Gc = T([128, 1], F32, "Gc")
        Gl, d_Gl = T([128, 1], F32, "Gl")
        eGl, d_eGl = T([128, 1], F32, "eGl")
        eGbc, d_eGbc = T([128, 128], F32, "eGbc")
        scl, d_scl = T([128, 4], F32, "scl")
        X1, d_X1 = T([128, 128], F32, "X1")
        X2, d_X2 = T([128, 128], F32, "X2")
        X3, d_X3 = T([128, 128], F32, "X3")
        A_sb = [fw.sb([128, 128], F32, f"A_sb{i}") for i in range(2)]
        B_sb = [fw.sb([128, 128], F32, f"B_sb{i}") for i in range(2)]
        d_A = deps(2)
        d_B = deps(2)
        attnT, d_attnT = T([128, 128], F32, "attnT")
        M_sb, d_M = T([128, 128], F32, "M_sb")
        Ru, d_Ru = T([128, 128], F32, "Ru")
        Rw, d_Rw = T([128, 128], F32, "Rw")
        kdec, d_kdec = T([128, 128], F32, "kdec")
        u_sb, d_u = T([128, 128], F32, "u_sb")
        wT_sb, d_wT = T([128, 128], F32, "wT_sb")
        qgT, d_qgT = T([128, 128], F32, "qgT")
        vnew, d_vnew = T([128, 128], F32, "vnew")
        szb, d_szb = T([128, TB], F32, "szb")
        ustg = [fw.sb([128, TB], BF16, f"ustg{i}") for i in range(2)]
        d_ustg = deps(2)
        dph = dpbg = dP[1]
        dpss = dP[2]
        ident = CF("ident")
        ones = CF("ones")

        fw.op("act", lambda e: e.activation(out=negA[:, 0:2], in_=pvs[:, PV["alog"]:PV["alog"] + 2], func=AF.Exp),
              reads=[d_pv], writes=[d_negA])
        fw.op("dve", lambda e: e.tensor_scalar(out=negA[:, 0:2], in0=negA[:, 0:2], scalar1=-1.0, scalar2=None, op0=ALU.mult),
              reads=[d_negA], writes=[d_negA])

        def chunk(ci, st, dirn):
            cs = ci * 128
            t0 = st + cs
            gcol = bgs[:, ci, 1:2]
            bcol = bgs[:, ci, 0:1]
            qfc = act3[:, 0, cs:cs + 128]
            kfc = act3[:, 1, cs:cs + 128]
            vfc = act3[:, 2, cs:cs + 128]
            if dirn == 0:
                TRI, POSD, NEGS, NEGI, last = CF("UI"), CF("POS_SL"), CF("NEG_SU"), CF("NEG_UI"), 127
            else:
                TRI, POSD, NEGS, NEGI, last = CF("LI"), CF("POS_SU"), CF("NEG_SL"), CF("NEG_LI"), 0
            fw.op("dve", lambda e: e.tensor_scalar(out=gB[:, :], in0=ones, scalar1=gcol, scalar2=None, op0=ALU.mult),
                  reads=[d_cf, d_bgs], writes=[d_gB])
            fw.op("dve", lambda e: e.tensor_scalar(out=bB[:, :], in0=ones, scalar1=bcol, scalar2=None, op0=ALU.mult),
                  reads=[d_cf, d_bgs], writes=[d_bB])
            Gbc = psr(3, 0, 128, 0, 128)
            bbc = psr(3, 0, 128, 128, 256)
            KK = psr(3, 0, 128, 256, 384)
            QKT = psr(3, 0, 128, 384, 512)
            ktok = psr(4, 0, 128, 0, 128)
            vtok = psr(4, 0, 128, 128, 256)
            Gcol = psr(4, 0, 128, 256, 257)
            fw.op("pe", lambda e: e.matmul(Gbc, lhsT=gB[:, :], rhs=TRI, start=True, stop=True), reads=[d_gB, d_cf], writes=[dP[3]])
            fw.op("pe", lambda e: e.matmul(bbc, lhsT=bB[:, :], rhs=ident, start=True, stop=True), reads=[d_bB, d_cf], writes=[dP[3]])
            fw.op("pe", lambda e: e.matmul(KK, lhsT=kfc, rhs=kfc, start=True, stop=True), reads=[d_act3[1]], writes=[dP[3]])
            fw.op("pe", lambda e: e.matmul(QKT, lhsT=kfc, rhs=qfc, start=True, stop=True), reads=[d_act3[1], d_act3[0]], writes=[dP[3]])
            fw.op("pe", lambda e: e.transpose(ktok, kfc, ident), reads=[d_act3[1], d_cf], writes=[dP[4]])
            fw.op("pe", lambda e: e.transpose(vtok, vfc, ident), reads=[d_act3[2], d_cf], writes=[dP[4]])
            fw.op("pe", lambda e: e.matmul(Gcol, lhsT=TRI, rhs=gcol, start=True, stop=True), reads=[d_bgs, d_cf], writes=[dP[4]])
            if GSTOP <= 4:
                return
            fw.op("act", lambda e: e.activation(out=Gc[:, :], in_=Gcol, func=AF.Copy), reads=[dP[4]], writes=[d_Gc])
            fw.op("act", lambda e: e.activation(out=Gl[:, :], in_=Gbc[:, last:last + 1], func=AF.Copy), reads=[dP[3]], writes=[d_Gl])
            fw.op("act", lambda e: e.activation(out=eGl[:, :], in_=Gl[:, :], func=AF.Exp), reads=[d_Gl], writes=[d_eGl])
            fw.op("act", lambda e: e.activation(out=eGbc[:, :], in_=Gbc, func=AF.Exp), reads=[dP[3]], writes=[d_eGbc])
            fw.op("act", lambda e: e.activation(out=scl[:, 0:1], in_=Gc[:, :], func=AF.Exp), reads=[d_Gc], writes=[d_scl])
            fw.op("act", lambda e: e.activation(out=scl[:, 1:2], in_=Gc[:, :], func=AF.Exp, scale=-1.0, bias=Gl[:, 0:1]),
                  reads=[d_Gc, d_Gl, d_scl], writes=[d_scl])
            fw.op("dve", lambda e: e.tensor_tensor(out=scl[:, 2:3], in0=scl[:, 0:1], in1=bcol, op=ALU.mult),
                  reads=[d_scl, d_bgs], writes=[d_scl])
            fw.op("dve", lambda e: e.scalar_tensor_tensor(out=X1[:, :], in0=Gbc, scalar=Gc[:, 0:1], in1=POSD,
                                                          op0=ALU.subtract, op1=ALU.add), reads=[dP[3], d_Gc, d_cf], writes=[d_X1])
            fw.op("act", lambda e: e.activation(out=X1[:, :], in_=X1[:, :], func=AF.Exp, scale=-1.0), reads=[d_X1], writes=[d_X1])
            fw.op("dve", lambda e: e.scalar_tensor_tensor(out=X2[:, :], in0=Gbc, scalar=Gc[:, 0:1], in1=NEGS,
                                                          op0=ALU.subtract, op1=ALU.add), reads=[dP[3], d_Gc, d_cf], writes=[d_X2])
            fw.op("act", lambda e: e.activation(out=X2[:, :], in_=X2[:, :], func=AF.Exp), reads=[d_X2], writes=[d_X2])
            fw.op("dve", lambda e: e.scalar_tensor_tensor(out=X3[:, :], in0=Gbc, scalar=Gc[:, 0:1], in1=NEGI,
                                                          op0=ALU.subtract, op1=ALU.add), reads=[dP[3], d_Gc, d_cf], writes=[d_X3])
            fw.op("act", lambda e: e.activation(out=X3[:, :], in_=X3[:, :], func=AF.Exp), reads=[d_X3], writes=[d_X3])
            fw.op("dve", lambda e: e.scalar_tensor_tensor(out=A_sb[0][:, :], in0=KK, scalar=bcol, in1=X1[:, :], op0=ALU.mult, op1=ALU.mult),
                  reads=[dP[3], d_bgs, d_X1], writes=[d_A[0]])
            fw.op("dve", lambda e: e.tensor_tensor(out=B_sb[0][:, :], in0=KK, in1=X2[:, :], op=ALU.mult), reads=[dP[3], d_X2], writes=[d_B[0]])
            fw.op("dve", lambda e: e.tensor_tensor(out=B_sb[0][:, :], in0=B_sb[0][:, :], in1=bbc, op=ALU.mult), reads=[d_B[0], dP[3]], writes=[d_B[0]])
            fw.op("dve", lambda e: e.tensor_tensor(out=attnT[:, :], in0=QKT, in1=X3[:, :], op=ALU.mult), reads=[dP[3], d_X3], writes=[d_attnT])
            fw.op("dve", lambda e: e.tensor_tensor(out=M_sb[:, :], in0=ident, in1=B_sb[0][:, :], op=ALU.subtract),
                  reads=[d_cf, d_B[0]], writes=[d_M])
            if GSTOP <= 5:
                return
            cur = 0
            NL = GNL
            for lvl in range(NL):
                nxt = 1 - cur
                if GBANKS:
                    A2, B2, MM = psr(5, 0, 128, 0, 128), psr(2, 0, 128, 0, 128), psr(0, 0, 128, 0, 128)
                    dA2, dB2, dMM = dP[5], dP[2], dP[0]
                else:
                    A2, B2, MM = psr(5, 0, 128, 0, 128), psr(5, 0, 128, 128, 256), psr(5, 0, 128, 256, 384)
                    dA2 = dB2 = dMM = dP[5]
                eA = "dve" if GBANKS == 2 else "act"
                fw.op("pe", lambda e: e.matmul(A2, lhsT=B_sb[cur][:, :], rhs=A_sb[cur][:, :], start=True, stop=True),
                      reads=[d_A[cur], d_B[cur]], writes=[dA2])
                if lvl < NL - 1:
                    fw.op("pe", lambda e: e.matmul(B2, lhsT=A_sb[cur][:, :], rhs=B_sb[cur][:, :], start=True, stop=True),
                          reads=[d_A[cur], d_B[cur]], writes=[dB2])
                if eA == "act":
                    fw.op("act", lambda e: e.activation(out=A_sb[nxt][:, :], in_=A2, func=AF.Copy), reads=[dA2], writes=[d_A[nxt]])
                else:
                    fw.op("dve", lambda e: e.tensor_copy(out=A_sb[nxt][:, :], in_=A2), reads=[dA2], writes=[d_A[nxt]])
                if lvl < NL - 1:
                    fw.op("dve", lambda e: e.tensor_copy(out=B_sb[nxt][:, :], in_=B2), reads=[dB2], writes=[d_B[nxt]])
                fw.op("pe", lambda e: e.matmul(MM, lhsT=A_sb[nxt][:, :], rhs=M_sb[:, :], start=True, stop=True),
                      reads=[d_A[nxt], d_M], writes=[dMM])
                fw.op("dve", lambda e: e.tensor_tensor(out=M_sb[:, :], in0=M_sb[:, :], in1=MM, op=ALU.add), reads=[d_M, dMM], writes=[d_M])
                cur = nxt
            if GSTOP <= 6:
                return
            fw.op("dve", lambda e: e.tensor_scalar(out=Ru[:, :], in0=vtok, scalar1=bcol, scalar2=None, op0=ALU.mult),
                  reads=[dP[4], d_bgs], writes=[d_Ru])
            fw.op("act", lambda e: e.activation(out=Rw[:, :], in_=ktok, func=AF.Copy, scale=scl[:, 2:3]), reads=[dP[4], d_scl], writes=[d_Rw])
            fw.op("act", lambda e: e.activation(out=kdec[:, :], in_=ktok, func=AF.Copy, scale=scl[:, 1:2]), reads=[dP[4], d_scl], writes=[d_kdec])
            ups = psr(6, 0, 128, 0, 128)
            wTps = psr(6, 0, 128, 128, 256)
            vnps = psr(6, 0, 128, 256, 384)
            fw.op("pe", lambda e: e.matmul(ups, lhsT=M_sb[:, :], rhs=Ru[:, :], start=True, stop=True), reads=[d_M, d_Ru], writes=[dP[6]])
            fw.op("pe", lambda e: e.matmul(wTps, lhsT=Rw[:, :], rhs=M_sb[:, :], start=True, stop=True), reads=[d_M, d_Rw], writes=[dP[6]])
            fw.op("act", lambda e: e.activation(out=u_sb[:, :], in_=ups, func=AF.Copy), reads=[dP[6]], writes=[d_u])
            fw.op("dve", lambda e: e.tensor_copy(out=wT_sb[:, :], in_=wTps), reads=[dP[6]], writes=[d_wT])
            fw.op("dve", lambda e: e.tensor_tensor(out=qgT[:, :], in0=qfc, in1=eGbc[:, :], op=ALU.mult), reads=[d_act3[0], d_eGbc], writes=[d_qgT])
            if GSTOP <= 7:
                return
            fw.op("pe", lambda e: e.matmul(vnps, lhsT=wT_sb[:, :], rhs=S[:, :], start=True, stop=True), reads=[d_wT, d_S], writes=[dP[6]])
            fw.op("dve", lambda e: e.tensor_tensor(out=vnew[:, :], in0=u_sb[:, :], in1=vnps, op=ALU.subtract), reads=[d_u, dP[6]], writes=[d_vnew])
            if GSTOP <= 8:
                return
            oTps = psr(7, 0, 128, 0, 128)
            Sps = psr(7, 0, 128, 128, 256)
            fw.op("pe", lambda e: e.matmul(oTps, lhsT=S[:, :], rhs=qgT[:, :], start=True, stop=False), reads=[d_S, d_qgT], writes=[dP[7]])
            fw.op("pe", lambda e: e.matmul(oTps, lhsT=vnew[:, :], rhs=attnT[:, :], start=False, stop=True), reads=[d_vnew, d_attnT], writes=[dP[7]])
            fw.op("pe", lambda e: e.matmul(Sps, lhsT=kdec[:, :], rhs=vnew[:, :], start=True, stop=True), reads=[d_kdec, d_vnew], writes=[dP[7]])
            if GSTOP <= 9:
                return
            if dirn == 0:
                fw.op("act", lambda e: e.activation(out=oT_all[:, t0:t0 + 128], in_=oTps, func=AF.Copy), reads=[dP[7]], accw=[d_oT])
            else:
                fw.op("dve", lambda e: e.tensor_tensor(out=oT_all[:, t0:t0 + 128], in0=oT_all[:, t0:t0 + 128], in1=oTps, op=ALU.add),
                      reads=[dP[7], d_oT], accw=[d_oT])
            if GSTOP <= 10:
                return
            fw.op("act", lambda e: e.activation(out=Stmp[:, :], in_=Sps, func=AF.Copy), reads=[dP[7]], writes=[d_Stmp])
            fw.op("dve", lambda e: e.scalar_tensor_tensor(out=S[:, :], in0=S[:, :], scalar=eGl[:, 0:1], in1=Stmp[:, :], op0=ALU.mult, op1=ALU.add),
                  reads=[d_S, d_eGl, d_Stmp], writes=[d_S])

        def gblock(bi, st, n, s, dirn):
            b = bi % 2
            lo, hi = st - 1, st + n + 1
            s_lo, s_hi = (0, CTX) if s == 1 else (CTX, TA)
            has_l, has_r = lo >= s_lo, hi <= s_hi
            a_ = lo if has_l else st
            b_ = hi if has_r else st + n
            for h in range(2):
                fw.dma("sp", xg[b][:, h * 16:(h + 1) * 16, a_ - lo:b_ - lo], xnT_v[:, h * 16:(h + 1) * 16, a_:b_], d_xg[b],
                       reads=[d_xnT], writes=[d_xg[b]] if h == 0 else (), accw=() if h == 0 else [d_xg[b]])
            if not has_l:
                fw.op("dve", lambda e: e.memset(xg[b][:, :, 0:1], 0.0), writes=[d_xg[b]])
            if not has_r:
                fw.op("dve", lambda e: e.memset(xg[b][:, :, n + 1:n + 2], 0.0), writes=[d_xg[b]])
            for j in range(3):
                for kc in range(32):
                    fw.op("pe", lambda e: e.matmul(P[0][:, 0:n], lhsT=Wg[:, kc, j * 128:(j + 1) * 128], rhs=xg[b][:, kc, 1:n + 1],
                                                   start=(kc == 0), stop=(kc == 31)), reads=[d_Wg, d_xg[b]], writes=[dP[0]])
                for kc in range(32):
                    fw.op("pe", lambda e: e.matmul(P[1][:, 0:2], lhsT=Wg[:, kc, j * 128:(j + 1) * 128], rhs=xg[b][:, kc, 0:n + 2:n + 1],
                                                   start=(kc == 0), stop=(kc == 31)), reads=[d_Wg, d_xg[b]], writes=[dph])
                fw.op("act", lambda e: e.activation(out=raw[:, j, 1:n + 1], in_=P[0][:, 0:n], func=AF.Copy), reads=[dP[0]], writes=[d_raw[j]])
                fw.op("dve", lambda e: e.tensor_copy(out=raw[:, j, 0:n + 2:n + 1], in_=P[1][:, 0:2]), reads=[dph], writes=[d_raw[j]])
                cw = PV["conv"] + j * 3
                fw.op("dve", lambda e: e.tensor_scalar(out=cvt[:, 0:n], in0=raw[:, j, 0:n], scalar1=pvs[:, cw:cw + 1], scalar2=None, op0=ALU.mult),
                      reads=[d_raw[j], d_pv], writes=[d_cvt])
                fw.op("dve", lambda e: e.scalar_tensor_tensor(out=cvt[:, 0:n], in0=raw[:, j, 1:n + 1], scalar=pvs[:, cw + 1:cw + 2], in1=cvt[:, 0:n],
                                                              op0=ALU.mult, op1=ALU.add), reads=[d_raw[j], d_pv, d_cvt], writes=[d_cvt])
                fw.op("dve", lambda e: e.scalar_tensor_tensor(out=cvt[:, 0:n], in0=raw[:, j, 2:n + 2], scalar=pvs[:, cw + 2:cw + 3], in1=cvt[:, 0:n],
                                                              op0=ALU.mult, op1=ALU.add), reads=[d_raw[j], d_pv, d_cvt], writes=[d_cvt])
                fw.op("act", lambda e: e.activation(out=act3[:, j, 0:n], in_=cvt[:, 0:n], func=AF.Silu), reads=[d_cvt], writes=[d_act3[j]])
                if j < 2:
                    fw.op("act", lambda e: e.activation(out=sqg[:, 0:n], in_=act3[:, j, 0:n], func=AF.Square), reads=[d_act3[j]], writes=[d_sqg])
                    fw.op("pe", lambda e: e.matmul(P[2][:, 0:n], lhsT=CB("ones"), rhs=sqg[:, 0:n], start=True, stop=True),
                          reads=[d_sqg, d_cb], writes=[dpss])
                    fw.op("act", lambda e: e.activation(out=rsg[:, 0:n], in_=P[2][:, 0:n], func=AF.Sqrt, bias=1e-6), reads=[dpss], writes=[d_rsg])
                    fw.op("dve", lambda e: e.reciprocal(out=rsg[:, 0:n], in_=rsg[:, 0:n]), reads=[d_rsg], writes=[d_rsg])
                    sc_ = (128.0 ** -0.5) if j == 0 else 1.0
                    fw.op("dve", lambda e: e.scalar_tensor_tensor(out=act3[:, j, 0:n], in0=act3[:, j, 0:n], scalar=sc_, in1=rsg[:, 0:n],
                                                                  op0=ALU.mult, op1=ALU.mult), reads=[d_act3[j], d_rsg], writes=[d_act3[j]])
            nch = n // 128
            for ci in range(nch):
                for kc in range(32):
                    fw.op("pe", lambda e: e.matmul(P[1][0:128, 8 + ci * 4:12 + ci * 4], lhsT=xg[b][:, kc, 1 + ci * 128:129 + ci * 128],
                                                   rhs=Wg[:, kc, 512:516], start=(kc == 0), stop=(kc == 31)),
                          reads=[d_Wg, d_xg[b]], writes=[dpbg])
            bgp = P[1][0:128, 8:8 + nch * 4].rearrange("p (c f) -> p c f", f=4)
            fw.op("act", lambda e: e.activation(out=bgs[:, 0:nch, 0], in_=bgp[:, :, dirn], func=AF.Sigmoid), reads=[dpbg], writes=[d_bgs])
            fw.op("act", lambda e: e.activation(out=bgs[:, 0:nch, 1], in_=bgp[:, :, 2 + dirn], func=AF.Exp,
                                                bias=pvs[:, PV["dtb"] + dirn:PV["dtb"] + dirn + 1]), reads=[dpbg, d_pv, d_bgs], writes=[d_bgs])
            fw.op("act", lambda e: e.activation(out=bgs[:, 0:nch, 1], in_=bgs[:, 0:nch, 1], func=AF.Ln, bias=1.0), reads=[d_bgs], writes=[d_bgs])
            fw.op("dve", lambda e: e.tensor_scalar(out=bgs[:, 0:nch, 1], in0=bgs[:, 0:nch, 1], scalar1=negA[:, dirn:dirn + 1], scalar2=None, op0=ALU.mult),
                  reads=[d_bgs, d_negA], writes=[d_bgs])
            order = range(nch) if dirn == 0 else range(nch - 1, -1, -1)
            for ci in order:
                if GFWD and dirn == 1:
                    continue
                if gstate["n"] >= GLIM:
                    continue
                gstate["n"] += 1
                chunk(ci, st, dirn)
            if dirn == 1:
                ub = bi % 2
                for kc in range(32):
                    fw.op("pe", lambda e: e.matmul(P[0][:, 0:n], lhsT=Wg[:, kc, 384:512], rhs=xg[b][:, kc, 1:n + 1],
                                                   start=(kc == 0), stop=(kc == 31)), reads=[d_Wg, d_xg[b]], writes=[dP[0]])
                fw.op("act", lambda e: e.activation(out=szb[:, 0:n], in_=P[0][:, 0:n], func=AF.Silu), reads=[dP[0]], writes=[d_szb])
                fw.op("act", lambda e: e.activation(out=sqg[:, 0:n], in_=oT_all[:, st:st + n], func=AF.Square), reads=[d_oT], writes=[d_sqg])
                fw.op("pe", lambda e: e.matmul(P[2][:, 0:n], lhsT=CB("ones"), rhs=sqg[:, 0:n], start=True, stop=True),
                      reads=[d_sqg, d_cb], writes=[dpss])
                fw.op("act", lambda e: e.activation(out=rsg[:, 0:n], in_=P[2][:, 0:n], func=AF.Sqrt, scale=1.0 / 128, bias=NORM_EPS),
                      reads=[dpss], writes=[d_rsg])
                fw.op("dve", lambda e: e.reciprocal(out=rsg[:, 0:n], in_=rsg[:, 0:n]), reads=[d_rsg], writes=[d_rsg])
                fw.op("dve", lambda e: e.scalar_tensor_tensor(out=cvt[:, 0:n], in0=oT_all[:, st:st + n], scalar=pvs[:, PV["gnorm"]:PV["gnorm"] + 1],
                                                              in1=rsg[:, 0:n], op0=ALU.mult, op1=ALU.mult), reads=[d_oT, d_pv, d_rsg], writes=[d_cvt])
                fw.op("dve", lambda e: e.tensor_tensor(out=ustg[ub][:, 0:n], in0=cvt[:, 0:n], in1=szb[:, 0:n], op=ALU.mult),
                      reads=[d_cvt, d_szb], writes=[d_ustg[ub]])
                fw.dma("sp", uT[256:384, st:st + n], ustg[ub][:, 0:n], d_ustg[ub], reads=[d_ustg[ub]], accw=[d_uT])

        gstate = {"n": 0}
        if GLIM < 100000 or GFWD:
            fw.op("dve", lambda e: e.memset(oT_all[:, :], 0.0), writes=[d_oT])
        for dirn in range(2):
            gstate["n"] = 0
            fw.op("dve", lambda e: e.memset(S[:, :], 0.0), writes=[d_S])
            lat = blocks[1:]
            seq = [blocks[0]] + (lat if dirn == 0 else lat[::-1])
            for bi, (st, n, s) in enumerate(seq):
                if bi >= GBLK or dirn >= GSW:
                    continue
                gblock(bi, st, n, s, dirn)
        fw.barrier()
        sc.__exit__(None, None, None)

    if "c" in do:
        gdn_pass()

    def rwkv_pass():
        c0 = NCA_D + NCA_B + NCA_C
        sc = fw.scope()
        sc.__enter__()
        Wr, d_Wr = T([128, 32, NCA_A], BF16, "Wr")
        for h in range(8):
            fw.dma("pool", Wr[:, h * 4:(h + 1) * 4, :], wA_v[:, h * 4:(h + 1) * 4, c0:c0 + NCA_A], d_Wr,
                   writes=[d_Wr] if h == 0 else (), accw=() if h == 0 else [d_Wr])
        lw, d_lw = T([128, 2, 128], F32, "lw")
        fw.dma("sp", lw[:], lora[:, :].rearrange("p (d m) -> p d m", d=2), d_lw, writes=[d_lw])
        xg, d_xg = T([128, 32, TB + 2], BF16, "xgr")
        raw = fw.sb([128, 4, TB + 2], F32, "rawr")
        d_raw = deps(4)
        fr = fw.sb([128, 4, TB], F32, "fr")
        d_fr = deps(4)
        tmpa, d_tmpa = T([128, TB], F32, "tmpa")
        tmpb, d_tmpb = T([128, TB], F32, "tmpb")
        Lf, d_Lf = T([128, TB], F32, "Lf")
        af, d_af = T([128, TB], F32, "af")
        ao, d_ao = T([128, TB], F32, "ao")
        kkf, d_kkf = T([128, TB], F32, "kkf")
        kdf, d_kdf = T([128, TB], F32, "kdf")
        bf_, d_bf = T([128, TB], F32, "bf_")
        szr, d_szr = T([128, TB], F32, "szr")
        yb, d_yb = T([128, TB], F32, "yb")
        yst = [fw.sb([128, 128], F32, f"yst{i}") for i in range(2)]
        d_yst = deps(2)
        ustr = [fw.sb([128, TB], BF16, f"ustr{i}") for i in range(2)]
        d_ustr = deps(2)
        pc, d_pc = T([128, 8], F32, "pc")
        ST, d_ST = T([128, 128], F32, "ST")
        names = ["Ltok", "kktok", "btok", "kdtok", "eI", "enI", "eE", "eRtok", "eEtok", "at", "qt", "bt", "kt", "Bh", "Kh",
                 "MV", "W2", "W1T", "U", "STt", "eLC"]
        tl = {}
        dl = {}
        for nm in names:
            tl[nm], dl[nm] = T([128, 128], F32, "r_" + nm)
        Vpad, d_Vpad = T([128, 2, 128], F32, "Vpad")
        Upad, d_Upad = T([128, 2, 128], F32, "Upad")
        Apad, d_Apad = T([128, 2, 128], F32, "Apad")
        A_sb = [[fw.sb([128, 128], F32, f"rA{h}{i}") for i in range(2)] for h in range(2)]
        B_sb = [[fw.sb([128, 128], F32, f"rB{h}{i}") for i in range(2)] for h in range(2)]
        d_A = [deps(2) for _ in range(2)]
        d_B = [deps(2) for _ in range(2)]
        Mi = [fw.sb([128, 128], F32, f"rM{h}") for h in range(2)]
        d_Mi = deps(2)
        MakT = [fw.sb([128, 128], F32, f"rMak{h}") for h in range(2)]
        MqbT = [fw.sb([128, 128], F32, f"rMqb{h}") for h in range(2)]
        MqkT = [fw.sb([128, 128], F32, f"rMqk{h}") for h in range(2)]
        d_Mak, d_Mqb, d_Mqk = deps(2), deps(2), deps(2)
        yscr = nc.dram_tensor("yscr", [128, TA], F32).ap()
        d_yscr = Dep()
        ident = CF("ident")
        blk = CF("blk64")
        for t_, d_ in ((Vpad, d_Vpad), (Upad, d_Upad), (Apad, d_Apad)):
            fw.op("dve", lambda e: e.memset(t_[:], 0.0), writes=[d_])
        fw.op("dve", lambda e: e.tensor_scalar(out=pc[:, 0:4], in0=pvs[:, PV["mu"]:PV["mu"] + 4], scalar1=-1.0, scalar2=1.0,
                                               op0=ALU.mult, op1=ALU.add), reads=[d_pv], writes=[d_pc])
        fw.op("dve", lambda e: e.tensor_scalar(out=pc[:, 4:8], in0=pvs[:, PV["mu"]:PV["mu"] + 4], scalar1=0.5, scalar2=None, op0=ALU.mult),
              reads=[d_pv, d_pc], writes=[d_pc])
        NEGC = -math.exp(-0.5)

        def rchunk(ci, st, dirn):
            cs = ci * 128
            t0 = st + cs
            sl = slice(cs, cs + 128)
            if dirn == 0:
                TI, TS, TR, M01S, M01ST, M01IT, last = CF("UI"), CF("SU"), CF("SL"), CF("SL"), CF("SU"), CF("UI"), 127
            else:
                TI, TS, TR, M01S, M01ST, M01IT, last = CF("LI"), CF("SL"), CF("SU"), CF("SU"), CF("SL"), CF("LI"), 0
            for i_, (src, d_src) in enumerate(((Lf, d_Lf), (fr[:, 2, :], d_fr[2]), (kkf, d_kkf), (bf_, d_bf))):
                src_ap = src[:, sl]
                fw.op("pe", lambda e: e.transpose(P[3][:, i_ * 128:(i_ + 1) * 128], src_ap, ident), reads=[d_src, d_cf], writes=[dP[3]])
            fw.op("pe", lambda e: e.transpose(P[4][:, 0:128], kdf[:, sl], ident), reads=[d_kdf, d_cf], writes=[dP[4]])
            fw.op("act", lambda e: e.activation(out=tl["Ltok"][:], in_=P[3][:, 0:128], func=AF.Copy), reads=[dP[3]], writes=[dl["Ltok"]])
            fw.op("dve", lambda e: e.tensor_copy(out=Vpad[:, 0, 0:64], in_=P[3][:, 128:192]), reads=[dP[3]], writes=[d_Vpad])
            fw.op("dve", lambda e: e.tensor_copy(out=Vpad[:, 1, 64:128], in_=P[3][:, 192:256]), reads=[dP[3]], writes=[d_Vpad])
            fw.op("act", lambda e: e.activation(out=tl["kktok"][:], in_=P[3][:, 256:384], func=AF.Copy), reads=[dP[3]], writes=[dl["kktok"]])
            fw.op("dve", lambda e: e.tensor_copy(out=tl["btok"][:], in_=P[3][:, 384:512]), reads=[dP[3]], writes=[dl["btok"]])
            fw.op("act", lambda e: e.activation(out=tl["kdtok"][:], in_=P[4][:, 0:128], func=AF.Copy), reads=[dP[4]], writes=[dl["kdtok"]])
            Lt = tl["Ltok"]
            fw.op("pe", lambda e: e.matmul(P[4][:, 128:256], lhsT=Lt[:], rhs=TI, start=True, stop=True), reads=[dl["Ltok"], d_cf], writes=[dP[4]])
            fw.op("pe", lambda e: e.matmul(P[4][:, 256:384], lhsT=Lt[:], rhs=TS, start=True, stop=True), reads=[dl["Ltok"], d_cf], writes=[dP[4]])
            fw.op("pe", lambda e: e.matmul(P[4][:, 384:512], lhsT=TR, rhs=Lt[:], start=True, stop=True), reads=[dl["Ltok"], d_cf], writes=[dP[4]])
            fw.op("pe", lambda e: e.matmul(P[5][:, 0:128], lhsT=TS, rhs=Lt[:], start=True, stop=True), reads=[dl["Ltok"], d_cf], writes=[dP[5]])
            fw.op("act", lambda e: e.activation(out=tl["eI"][:], in_=P[4][:, 128:256], func=AF.Exp), reads=[dP[4]], writes=[dl["eI"]])
            fw.op("act", lambda e: e.activation(out=tl["enI"][:], in_=P[4][:, 128:256], func=AF.Exp, scale=-1.0), reads=[dP[4]], writes=[dl["enI"]])
            fw.op("act", lambda e: e.activation(out=tl["eE"][:], in_=P[4][:, 256:384], func=AF.Exp), reads=[dP[4]], writes=[dl["eE"]])
            fw.op("act", lambda e: e.activation(out=tl["eRtok"][:], in_=P[4][:, 384:512], func=AF.Exp), reads=[dP[4]], writes=[dl["eRtok"]])
            fw.op("act", lambda e: e.activation(out=tl["eEtok"][:], in_=P[5][:, 0:128], func=AF.Exp), reads=[dP[5]], writes=[dl["eEtok"]])
            fw.op("act", lambda e: e.activation(out=tl["eLC"][:, 0:1], in_=tl["eI"][:, last:last + 1], func=AF.Copy), reads=[dl["eI"]], writes=[dl["eLC"]])
            fw.op("dve", lambda e: e.scalar_tensor_tensor(out=tl["at"][:], in0=kkf[:, sl], scalar=-1.0, in1=tl["eE"][:], op0=ALU.mult, op1=ALU.mult),
                  reads=[d_kkf, dl["eE"]], writes=[dl["at"]])
            fw.op("dve", lambda e: e.tensor_tensor(out=tl["qt"][:], in0=fr[:, 0, sl], in1=tl["eI"][:], op=ALU.mult), reads=[d_fr[0], dl["eI"]], writes=[dl["qt"]])
            fw.op("dve", lambda e: e.tensor_tensor(out=tl["bt"][:], in0=bf_[:, sl], in1=tl["enI"][:], op=ALU.mult), reads=[d_bf, dl["enI"]], writes=[dl["bt"]])
            fw.op("dve", lambda e: e.tensor_tensor(out=tl["kt"][:], in0=kdf[:, sl], in1=tl["enI"][:], op=ALU.mult), reads=[d_kdf, dl["enI"]], writes=[dl["kt"]])
            fw.op("dve", lambda e: e.tensor_tensor(out=tl["Bh"][:], in0=tl["btok"][:], in1=tl["eRtok"][:], op=ALU.mult), reads=[dl["btok"], dl["eRtok"]], writes=[dl["Bh"]])
            fw.op("dve", lambda e: e.tensor_tensor(out=tl["Kh"][:], in0=tl["kdtok"][:], in1=tl["eRtok"][:], op=ALU.mult), reads=[dl["kdtok"], dl["eRtok"]], writes=[dl["Kh"]])
            for h in range(2):
                hs_ = slice(64 * h, 64 * h + 64)
                fw.op("dve", lambda e: e.scalar_tensor_tensor(out=Apad[:, h, hs_], in0=tl["kktok"][:, hs_], scalar=-1.0, in1=tl["eEtok"][:, hs_],
                                                              op0=ALU.mult, op1=ALU.mult), reads=[dl["kktok"], dl["eEtok"]], writes=[d_Apad])
            at, qt, bt, kt = tl["at"], tl["qt"], tl["bt"], tl["kt"]
            for h in range(2):
                hs_ = slice(64 * h, 64 * h + 64)
                fw.op("pe", lambda e: e.matmul(P[6][:, h * 128:(h + 1) * 128], lhsT=at[hs_, :], rhs=bt[hs_, :], start=True, stop=True),
                      reads=[dl["at"], dl["bt"]], writes=[dP[6]])
                fw.op("pe", lambda e: e.matmul(P[6][:, 256 + h * 128:384 + h * 128], lhsT=bt[hs_, :], rhs=at[hs_, :], start=True, stop=True),
                      reads=[dl["at"], dl["bt"]], writes=[dP[6]])
                fw.op("pe", lambda e: e.matmul(P[7][:, h * 128:(h + 1) * 128], lhsT=kt[hs_, :], rhs=at[hs_, :], start=True, stop=True),
                      reads=[dl["at"], dl["kt"]], writes=[dP[7]])
                fw.op("pe", lambda e: e.matmul(P[7][:, 256 + h * 128:384 + h * 128], lhsT=bt[hs_, :], rhs=qt[hs_, :], start=True, stop=True),
                      reads=[dl["qt"], dl["bt"]], writes=[dP[7]])
                fw.op("pe", lambda e: e.matmul(P[5][:, 128 + h * 128:256 + h * 128], lhsT=kt[hs_, :], rhs=qt[hs_, :], start=True, stop=True),
                      reads=[dl["qt"], dl["kt"]], writes=[dP[5]])
            for h in range(2):
                fw.op("dve", lambda e: e.scalar_tensor_tensor(out=A_sb[h][0][:], in0=P[6][:, h * 128:(h + 1) * 128], scalar=-1.0, in1=M01S,
                                                              op0=ALU.mult, op1=ALU.mult), reads=[dP[6], d_cf], writes=[d_A[h][0]])
                fw.op("dve", lambda e: e.scalar_tensor_tensor(out=B_sb[h][0][:], in0=P[6][:, 256 + h * 128:384 + h * 128], scalar=-1.0, in1=M01ST,
                                                              op0=ALU.mult, op1=ALU.mult), reads=[dP[6], d_cf], writes=[d_B[h][0]])
                fw.op("dve", lambda e: e.tensor_tensor(out=MakT[h][:], in0=P[7][:, h * 128:(h + 1) * 128], in1=M01ST, op=ALU.mult),
                      reads=[dP[7], d_cf], writes=[d_Mak[h]])
                fw.op("dve", lambda e: e.tensor_tensor(out=MqbT[h][:], in0=P[7][:, 256 + h * 128:384 + h * 128], in1=M01IT, op=ALU.mult),
                      reads=[dP[7], d_cf], writes=[d_Mqb[h]])
                fw.op("dve", lambda e: e.tensor_tensor(out=MqkT[h][:], in0=P[5][:, 128 + h * 128:256 + h * 128], in1=M01IT, op=ALU.mult),
                      reads=[dP[5], d_cf], writes=[d_Mqk[h]])
                fw.op("dve", lambda e: e.tensor_tensor(out=Mi[h][:], in0=ident, in1=B_sb[h][0][:], op=ALU.subtract),
                      reads=[d_cf, d_B[h][0]], writes=[d_Mi[h]])
            ebanks = [(6, 2, 0), (7, 1, 3)]
            cur = 0
            for lvl in range(6):
                nxt = 1 - cur
                for h in range(2):
                    ba, bb, bm = ebanks[h]
                    fw.op("pe", lambda e: e.matmul(P[ba][:, 0:128], lhsT=B_sb[h][cur][:], rhs=A_sb[h][cur][:], start=True, stop=True),
                          reads=[d_A[h][cur], d_B[h][cur]], writes=[dP[ba]])
                    if lvl < 5:
                        fw.op("pe", lambda e: e.matmul(P[bb][:, 0:128], lhsT=A_sb[h][cur][:], rhs=B_sb[h][cur][:], start=True, stop=True),
                              reads=[d_A[h][cur], d_B[h][cur]], writes=[dP[bb]])
                for h in range(2):
                    ba, bb, bm = ebanks[h]
                    fw.op("act", lambda e: e.activation(out=A_sb[h][nxt][:], in_=P[ba][:, 0:128], func=AF.Copy), reads=[dP[ba]], writes=[d_A[h][nxt]])
                    if lvl < 5:
                        fw.op("dve", lambda e: e.tensor_copy(out=B_sb[h][nxt][:], in_=P[bb][:, 0:128]), reads=[dP[bb]], writes=[d_B[h][nxt]])
                for h in range(2):
                    ba, bb, bm = ebanks[h]
                    fw.op("pe", lambda e: e.matmul(P[bm][:, 0:128], lhsT=A_sb[h][nxt][:], rhs=Mi[h][:], start=True, stop=True),
                          reads=[d_A[h][nxt], d_Mi[h]], writes=[dP[bm]])
                for h in range(2):
                    ba, bb, bm = ebanks[h]
                    fw.op("dve", lambda e: e.tensor_tensor(out=Mi[h][:], in0=Mi[h][:], in1=P[bm][:, 0:128], op=ALU.add),
                          reads=[d_Mi[h], dP[bm]], writes=[d_Mi[h]])
                cur = nxt
            for h in range(2):
                hs_ = slice(64 * h, 64 * h + 64)
                fw.op("pe", lambda e: e.matmul(P[5][:, hs_], lhsT=MakT[h][:], rhs=Vpad[:, h, hs_], start=True, stop=True),
                      reads=[d_Mak[h], d_Vpad], writes=[dP[5]])
            fw.op("act", lambda e: e.activation(out=tl["MV"][:], in_=P[5][:, 0:128], func=AF.Copy), reads=[dP[5]], writes=[dl["MV"]])
            for h in range(2):
                hs_ = slice(64 * h, 64 * h + 64)
                fw.op("pe", lambda e: e.matmul(P[5][:, 128 + 64 * h:192 + 64 * h], lhsT=Mi[h][:], rhs=tl["MV"][:, hs_], start=True, stop=True),
                      reads=[d_Mi[h], dl["MV"]], writes=[dP[5]])
            for h in range(2):
                fw.op("pe", lambda e: e.matmul(P[5][:, 256:384], lhsT=Apad[:, h, :], rhs=Mi[h][:], start=(h == 0), stop=(h == 1)),
                      reads=[d_Mi[h], d_Apad], writes=[dP[5]])
            fw.op("act", lambda e: e.activation(out=tl["W2"][:], in_=P[5][:, 128:256], func=AF.Copy), reads=[dP[5]], writes=[dl["W2"]])
            fw.op("dve", lambda e: e.tensor_copy(out=tl["W1T"][:], in_=P[5][:, 256:384]), reads=[dP[5]], writes=[dl["W1T"]])
            fw.op("pe", lambda e: e.matmul(P[6][:, 0:128], lhsT=tl["W1T"][:], rhs=ST[:], start=True, stop=True), reads=[dl["W1T"], d_ST], writes=[dP[6]])
            fw.op("dve", lambda e: e.tensor_tensor(out=tl["U"][:], in0=tl["W2"][:], in1=P[6][:, 0:128], op=ALU.add), reads=[dl["W2"], dP[6]], writes=[dl["U"]])
            fw.op("dve", lambda e: e.tensor_copy(out=Upad[:, 0, 0:64], in_=tl["U"][:, 0:64]), reads=[dl["U"]], writes=[d_Upad])
            fw.op("dve", lambda e: e.tensor_copy(out=Upad[:, 1, 64:128], in_=tl["U"][:, 64:128]), reads=[dl["U"]], writes=[d_Upad])
            fw.op("pe", lambda e: e.matmul(P[7][:, 0:128], lhsT=ST[:], rhs=qt[:], start=True, stop=False), reads=[d_ST, dl["qt"]], writes=[dP[7]])
            for h in range(2):
                fw.op("pe", lambda e: e.matmul(P[7][:, 0:128], lhsT=Upad[:, h, :], rhs=MqbT[h][:], start=False, stop=False),
                      reads=[d_Upad, d_Mqb[h]], writes=[dP[7]])
                fw.op("pe", lambda e: e.matmul(P[7][:, 0:128], lhsT=Vpad[:, h, :], rhs=MqkT[h][:], start=False, stop=(h == 1)),
                      reads=[d_Vpad, d_Mqk[h]], writes=[dP[7]])
            fw.op("pe", lambda e: e.matmul(P[6][:, 128:256], lhsT=tl["Bh"][:], rhs=tl["U"][:], start=True, stop=False), reads=[dl["Bh"], dl["U"]], writes=[dP[6]])
            fw.op("pe", lambda e: e.matmul(P[6][:, 128:256], lhsT=tl["Kh"][:], rhs=Vpad[:, 0, :], start=False, stop=False), reads=[dl["Kh"], d_Vpad], writes=[dP[6]])
            fw.op("pe", lambda e: e.matmul(P[6][:, 128:256], lhsT=tl["Kh"][:], rhs=Vpad[:, 1, :], start=False, stop=True), reads=[dl["Kh"], d_Vpad], writes=[dP[6]])
            yb_ = (t0 // 128) % 2
            if dirn == 0:
                fw.op("act", lambda e: e.activation(out=yst[yb_][:], in_=P[7][:, 0:128], func=AF.Copy), reads=[dP[7]], writes=[d_yst[yb_]])
                fw.dma("sp", yscr[:, t0:t0 + 128], yst[yb_][:], d_yst[yb_], reads=[d_yst[yb_]], accw=[d_yscr])
            else:
                fw.op("dve", lambda e: e.tensor_tensor(out=yb[:, sl], in0=yb[:, sl], in1=P[7][:, 0:128], op=ALU.add), reads=[dP[7], d_yb], writes=[d_yb])
            fw.op("dve", lambda e: e.tensor_tensor(out=tl["STt"][:], in0=P[6][:, 128:256], in1=blk, op=ALU.mult), reads=[dP[6], d_cf], writes=[dl["STt"]])
            fw.op("dve", lambda e: e.scalar_tensor_tensor(out=ST[:], in0=ST[:], scalar=tl["eLC"][:, 0:1], in1=tl["STt"][:], op0=ALU.mult, op1=ALU.add),
                  reads=[d_ST, dl["eLC"], dl["STt"]], writes=[d_ST])

        def rblock(st, n, s, dirn):
            lo, hi = st - 1, st + n + 1
            s_lo, s_hi = (0, CTX) if s == 1 else (CTX, TA)
            has_l, has_r = lo >= s_lo, hi <= s_hi
            a_ = lo if has_l else st
            b_ = hi if has_r else st + n
            for h in range(2):
                fw.dma("sp", xg[:, h * 16:(h + 1) * 16, a_ - lo:b_ - lo], xnT_v[:, h * 16:(h + 1) * 16, a_:b_], d_xg,
                       reads=[d_xnT], writes=[d_xg] if h == 0 else (), accw=() if h == 0 else [d_xg])
            if not has_l:
                fw.op("dve", lambda e: e.memset(xg[:, :, 0:1], 0.0), writes=[d_xg])
            if not has_r:
                fw.op("dve", lambda e: e.memset(xg[:, :, n + 1:n + 2], 0.0), writes=[d_xg])
            if dirn == 1:
                fw.dma("sp", yb[:, 0:n], yscr[:, st:st + n], d_yb, reads=[d_yscr], writes=[d_yb])
            wcols = [(0, 128), (128, 128), (256, 128), (512, 128)]
            for j, (wc, wn) in enumerate(wcols):
                for kc in range(32):
                    fw.op("pe", lambda e: e.matmul(P[0][:, 0:n], lhsT=Wr[:, kc, wc:wc + wn], rhs=xg[:, kc, 1:n + 1],
                                                   start=(kc == 0), stop=(kc == 31)), reads=[d_Wr, d_xg], writes=[dP[0]])
                for kc in range(32):
                    fw.op("pe", lambda e: e.matmul(P[1][:, 0:2], lhsT=Wr[:, kc, wc:wc + wn], rhs=xg[:, kc, 0:n + 2:n + 1],
                                                   start=(kc == 0), stop=(kc == 31)), reads=[d_Wr, d_xg], writes=[dP[1]])
                fw.op("act", lambda e: e.activation(out=raw[:, j, 1:n + 1], in_=P[0][:, 0:n], func=AF.Copy), reads=[dP[0]], writes=[d_raw[j]])
                fw.op("dve", lambda e: e.tensor_copy(out=raw[:, j, 0:n + 2:n + 1], in_=P[1][:, 0:2]), reads=[dP[1]], writes=[d_raw[j]])
                fw.op("dve", lambda e: e.tensor_tensor(out=tmpa[:, 0:n], in0=raw[:, j, 0:n], in1=raw[:, j, 2:n + 2], op=ALU.add),
                      reads=[d_raw[j]], writes=[d_tmpa])
                fw.op("dve", lambda e: e.tensor_scalar(out=tmpa[:, 0:n], in0=tmpa[:, 0:n], scalar1=pc[:, 4 + j:5 + j], scalar2=None, op0=ALU.mult),
                      reads=[d_tmpa, d_pc], writes=[d_tmpa])
                fw.op("dve", lambda e: e.scalar_tensor_tensor(out=fr[:, j, 0:n], in0=raw[:, j, 1:n + 1], scalar=pc[:, j:j + 1], in1=tmpa[:, 0:n],
                                                              op0=ALU.mult, op1=ALU.add), reads=[d_raw[j], d_pc, d_tmpa], writes=[d_fr[j]])
            fw.op("act", lambda e: e.activation(out=fr[0:64, 3, 0:n], in_=fr[0:64, 3, 0:n], func=AF.Tanh), reads=[d_fr[3]], writes=[d_fr[3]])
            fw.op("pe", lambda e: e.matmul(P[2][:, 0:n], lhsT=lw[0:64, dirn, :], rhs=fr[0:64, 3, 0:n], start=True, stop=True),
                  reads=[d_lw, d_fr[3]], writes=[dP[2]])
            fw.op("act", lambda e: e.activation(out=Lf[:, 0:n], in_=P[2][:, 0:n], func=AF.Sigmoid, bias=pvs[:, PV["w0"] + dirn:PV["w0"] + dirn + 1]),
                  reads=[dP[2], d_pv], writes=[d_Lf])
            fw.op("dve", lambda e: e.tensor_scalar(out=Lf[:, 0:n], in0=Lf[:, 0:n], scalar1=NEGC, scalar2=None, op0=ALU.mult), reads=[d_Lf], writes=[d_Lf])
            fw.op("pe", lambda e: e.matmul(P[2][:, 0:n], lhsT=lw[64:128, dirn, :], rhs=fr[64:128, 3, 0:n], start=True, stop=True),
                  reads=[d_lw, d_fr[3]], writes=[dP[2]])
            fw.op("act", lambda e: e.activation(out=af[:, 0:n], in_=P[2][:, 0:n], func=AF.Sigmoid, bias=pvs[:, PV["a0"] + dirn:PV["a0"] + dirn + 1]),
                  reads=[dP[2], d_pv], writes=[d_af])
            fw.op("dve", lambda e: e.tensor_scalar(out=kkf[:, 0:n], in0=fr[:, 1, 0:n], scalar1=pvs[:, PV["kk"]:PV["kk"] + 1], scalar2=None, op0=ALU.mult),
                  reads=[d_fr[1], d_pv], writes=[d_kkf])
            fw.op("act", lambda e: e.activation(out=tmpa[:, 0:n], in_=kkf[:, 0:n], func=AF.Square), reads=[d_kkf], writes=[d_tmpa])
            fw.op("pe", lambda e: e.matmul(P[2][:, 0:n], lhsT=blk, rhs=tmpa[:, 0:n], start=True, stop=True), reads=[d_cf, d_tmpa], writes=[dP[2]])
            fw.op("act", lambda e: e.activation(out=tmpb[:, 0:n], in_=P[2][:, 0:n], func=AF.Sqrt, bias=1e-6), reads=[dP[2]], writes=[d_tmpb])
            fw.op("dve", lambda e: e.reciprocal(out=tmpb[:, 0:n], in_=tmpb[:, 0:n]), reads=[d_tmpb], writes=[d_tmpb])
            fw.op("dve", lambda e: e.tensor_tensor(out=kkf[:, 0:n], in0=kkf[:, 0:n], in1=tmpb[:, 0:n], op=ALU.mult), reads=[d_kkf, d_tmpb], writes=[d_kkf])
            fw.op("dve", lambda e: e.tensor_scalar(out=tmpa[:, 0:n], in0=af[:, 0:n], scalar1=-1.0, scalar2=pvs[:, PV["ka"]:PV["ka"] + 1],
                                                   op0=ALU.add, op1=ALU.mult), reads=[d_af, d_pv], writes=[d_tmpa])
            fw.op("dve", lambda e: e.scalar_tensor_tensor(out=kdf[:, 0:n], in0=tmpa[:, 0:n], scalar=1.0, in1=fr[:, 1, 0:n], op0=ALU.add, op1=ALU.mult),
                  reads=[d_tmpa, d_fr[1]], writes=[d_kdf])
            fw.op("dve", lambda e: e.tensor_tensor(out=bf_[:, 0:n], in0=kkf[:, 0:n], in1=af[:, 0:n], op=ALU.mult), reads=[d_kkf, d_af], writes=[d_bf])
            nch = n // 128
            order = range(nch) if dirn == 0 else range(nch - 1, -1, -1)
            for ci in order:
                rchunk(ci, st, dirn)
            if dirn == 1:
                ub = (st // TB) % 2
                for kc in range(32):
                    fw.op("pe", lambda e: e.matmul(P[0][:, 0:n], lhsT=Wr[:, kc, 384:512], rhs=xg[:, kc, 1:n + 1],
                                                   start=(kc == 0), stop=(kc == 31)), reads=[d_Wr, d_xg], writes=[dP[0]])
                fw.op("act", lambda e: e.activation(out=szr[:, 0:n], in_=P[0][:, 0:n], func=AF.Silu), reads=[dP[0]], writes=[d_szr])
                fw.op("pe", lambda e: e.matmul(P[2][:, 0:n], lhsT=blk, rhs=yb[:, 0:n], start=True, stop=True), reads=[d_cf, d_yb], writes=[dP[2]])
                fw.op("dve", lambda e: e.scalar_tensor_tensor(out=yb[:, 0:n], in0=P[2][:, 0:n], scalar=-1.0 / 64, in1=yb[:, 0:n], op0=ALU.mult, op1=ALU.add),
                      reads=[dP[2], d_yb], writes=[d_yb])
                fw.op("act", lambda e: e.activation(out=tmpa[:, 0:n], in_=yb[:, 0:n], func=AF.Square), reads=[d_yb], writes=[d_tmpa])
                fw.op("pe", lambda e: e.matmul(P[2][:, 0:n], lhsT=blk, rhs=tmpa[:, 0:n], start=True, stop=True), reads=[d_cf, d_tmpa], writes=[dP[2]])
                fw.op("act", lambda e: e.activation(out=tmpb[:, 0:n], in_=P[2][:, 0:n], func=AF.Sqrt, scale=1.0 / 64, bias=64e-5), reads=[dP[2]], writes=[d_tmpb])
                fw.op("dve", lambda e: e.reciprocal(out=tmpb[:, 0:n], in_=tmpb[:, 0:n]), reads=[d_tmpb], writes=[d_tmpb])
                fw.op("dve", lambda e: e.tensor_tensor(out=yb[:, 0:n], in0=yb[:, 0:n], in1=tmpb[:, 0:n], op=ALU.mult), reads=[d_yb, d_tmpb], writes=[d_yb])
                fw.op("act", lambda e: e.activation(out=yb[:, 0:n], in_=yb[:, 0:n], func=AF.Identity, scale=pvs[:, PV["lnw"]:PV["lnw"] + 1],
                                                    bias=pvs[:, PV["lnb"]:PV["lnb"] + 1]), reads=[d_yb, d_pv], writes=[d_yb])
                fw.op("pe", lambda e: e.matmul(P[2][:, 0:n], lhsT=lw[64:128, 0, :], rhs=fr[64:128, 3, 0:n], start=True, stop=True),
                      reads=[d_lw, d_fr[3]], writes=[dP[2]])
                fw.op("act", lambda e: e.activation(out=ao[:, 0:n], in_=P[2][:, 0:n], func=AF.Sigmoid, bias=pvs[:, PV["a0"]:PV["a0"] + 1]),
                      reads=[dP[2], d_pv], writes=[d_ao])
                fw.op("dve", lambda e: e.tensor_tensor(out=ao[:, 0:n], in0=ao[:, 0:n], in1=af[:, 0:n], op=ALU.add), reads=[d_ao, d_af], writes=[d_ao])
                fw.op("dve", lambda e: e.tensor_scalar(out=ao[:, 0:n], in0=ao[:, 0:n], scalar1=0.5, scalar2=-1.0, op0=ALU.mult, op1=ALU.add),
                      reads=[d_ao], writes=[d_ao])
                fw.op("dve", lambda e: e.tensor_scalar(out=ao[:, 0:n], in0=ao[:, 0:n], scalar1=pvs[:, PV["ka"]:PV["ka"] + 1], scalar2=1.0, op0=ALU.mult, op1=ALU.add),
                      reads=[d_ao, d_pv], writes=[d_ao])
                fw.op("dve", lambda e: e.tensor_tensor(out=ao[:, 0:n], in0=ao[:, 0:n], in1=fr[:, 1, 0:n], op=ALU.mult), reads=[d_ao, d_fr[1]], writes=[d_ao])
                fw.op("dve", lambda e: e.scalar_tensor_tensor(out=ao[:, 0:n], in0=fr[:, 0, 0:n], scalar=pvs[:, PV["rk"]:PV["rk"] + 1], in1=ao[:, 0:n],
                                                              op0=ALU.mult, op1=ALU.mult), reads=[d_fr[0], d_pv, d_ao], writes=[d_ao])
                fw.op("pe", lambda e: e.matmul(P[2][:, 0:n], lhsT=blk, rhs=ao[:, 0:n], start=True, stop=True), reads=[d_cf, d_ao], writes=[dP[2]])
                fw.op("dve", lambda e: e.tensor_tensor(out=tmpa[:, 0:n], in0=P[2][:, 0:n], in1=fr[:, 2, 0:n], op=ALU.mult), reads=[dP[2], d_fr[2]], writes=[d_tmpa])
                fw.op("dve", lambda e: e.tensor_tensor(out=yb[:, 0:n], in0=yb[:, 0:n], in1=tmpa[:, 0:n], op=ALU.add), reads=[d_yb, d_tmpa], writes=[d_yb])
                fw.op("dve", lambda e: e.tensor_tensor(out=ustr[ub][:, 0:n], in0=yb[:, 0:n], in1=szr[:, 0:n], op=ALU.mult), reads=[d_yb, d_szr], writes=[d_ustr[ub]])
                fw.dma("sp", uT[0:128, st:st + n], ustr[ub][:, 0:n], d_ustr[ub], reads=[d_ustr[ub]], accw=[d_uT])

        for dirn in range(2):
            fw.op("dve", lambda e: e.memset(ST[:, :], 0.0), writes=[d_ST])
            lat = blocks[1:]
            seq = [blocks[0]] + (lat if dirn == 0 else lat[::-1])
            for (st, n, s) in seq:
                rblock(st, n, s, dirn)
            fw.barrier()
        sc.__exit__(None, None, None)

    if "a" in do:
        rwkv_pass()

    fw.finish("sp", [d_uT, d_dbg])
    fw.stack.close()
    return nc


NOWN = 2048


def prep_B(inp, l, xT_all, uT_full, NL_own):
    cst = _cst_array()
    cvec = np.stack([_fm(inp["c"][0]), _fm(inp["c_ctx"])], axis=2).reshape(128, 64)
    wmod = inp["w_mod"][l]
    bmod = _fm(inp["b_mod"][l])
    normw = _fm(inp["norm_w"][l])
    wg = inp["w_in"][l][:, O_G:O_G + 4 * D].reshape(D, 4, 32, 128).transpose(0, 2, 1, 3).reshape(D, 4 * D)
    wg = np.ascontiguousarray(wg)
    wbr = inp["w_branch"][l].reshape(4, 1024, 32, 128).transpose(1, 2, 0, 3).reshape(1024, 4 * D)
    wbr = np.ascontiguousarray(wbr)
    wout = inp["w_out"][l]
    ncores = (xT_all.shape[1] - CTX) // NL_own
    maps = []
    for c in range(ncores):
        sl = np.r_[0:CTX, CTX + NL_own * c:CTX + NL_own * (c + 1)]
        maps.append({"xT": np.ascontiguousarray(xT_all[:, sl]), "uT": np.ascontiguousarray(uT_full[:, sl]),
                     "cvec": cvec, "wmod": wmod, "bmod": bmod, "normw": normw, "wg": wg, "wbr": wbr, "wout": wout, "cst": cst})
    return maps


def build_B(NL_own):
    NT = CTX + NL_own
    nc = bass.Bass("TRN2", target_bir_lowering=False)
    fw = FW(nc)

    def din(n, s, dt=F32):
        return nc.dram_tensor(n, list(s), dt, kind="ExternalInput").ap()

    xT = din("xT", [D, NT])
    uT = din("uT", [D, NT], BF16)
    cvec = din("cvec", [128, 64])
    wmod = din("wmod", [D, 3 * D])
    bmod = din("bmod", [128, 96])
    normw = din("normw", [128, 32])
    wg = din("wg", [D, 4 * D])
    wbr = din("wbr", [1024, 4 * D])
    wout = din("wout", [D, D])
    cst = din("cst", [128, 128 * len(CST_NAMES)])
    x1T = nc.dram_tensor("x1T", [D, NT], F32, kind="ExternalOutput").ap()
    xnT = nc.dram_tensor("xnTb", [D, NT], BF16).ap()
    d_xnT, d_x1 = Dep(), Dep()
    xT_v = xT.rearrange("(kc p) t -> p kc t", p=128)
    xnT_v = xnT.rearrange("(kc p) t -> p kc t", p=128)
    uT_v = uT.rearrange("(kc p) t -> p kc t", p=128)
    x1T_v = x1T.rearrange("(kc p) t -> p kc t", p=128)
    wg_v = wg.rearrange("(kc p) n -> p kc n", p=128)
    wbr_v = wbr.rearrange("(kc p) n -> p kc n", p=128)
    wout_v = wout.rearrange("(kc p) n -> p kc n", p=128)

    cf = fw.sb([128, 256], F32, "cf")
    d_cf = Dep()
    fw.dma("sp", cf[:], cst[:, 0:256], d_cf, writes=[d_cf])
    cb16 = fw.sb([128, 256], BF16, "cb16")
    d_cb = Dep()
    fw.op("dve", lambda e: e.tensor_copy(out=cb16[:], in_=cf[:]), reads=[d_cf], writes=[d_cb])

    def CB(n):
        i = CST_NAMES.index(n)
        return cb16[:, i * 128:(i + 1) * 128]

    P = [fw.ps([128, 512], F32, f"bank{i}") for i in range(8)]
    dP = [Dep(excl=True) for _ in range(8)]
    mod = fw.sb([128, 2, 96], F32, "mod")
    d_mod = Dep()
    NBN = 256
    nblocks = [(0, CTX, 1)] + [(CTX + i * NBN, NBN, 0) for i in range(NL_own // NBN)]
    emit_cond_norm(fw, P, dP, CB, d_cb, cvec, wmod, bmod, normw, xT_v, xnT_v, d_xnT, nblocks, 96, mod, d_mod)

    TB = 512
    blocks = [(0, CTX, 1)] + [(CTX + i * TB, TB, 0) for i in range(NL_own // TB)]
    xk = fw.sb([128, 32, TB], BF16, "xk")
    uk = fw.sb([128, 32, TB], BF16, "uk")
    acc = fw.sb([128, 32, TB], BF16, "acc")
    d_xk, d_uk = Dep(), Dep()
    d_acc = deps(32)
    WG = [fw.sb([128, 32, 512], BF16, f"WG{i}") for i in range(2)]
    d_WG = deps(2)
    WBr = [fw.sb([128, 8, 512], BF16, f"WBr{i}") for i in range(2)]
    d_WBr = deps(2)
    sg = [fw.sb([128, TB], F32, f"sg{i}") for i in range(2)]
    d_sg = deps(2)
    a32 = fw.sb([128, TB], F32, "a32")
    d_a32 = Dep()
    xo = [fw.sb([128, TB], F32, f"xo{i}") for i in range(2)]
    d_xo = deps(2)
    yo = [fw.sb([128, TB], F32, f"yo{i}") for i in range(2)]
    d_yo = deps(2)
    wcnt = [0]

    def load_slab(dst, d_dst, src_v, nk, c0, ncol):
        per = 4 if nk >= 8 else nk
        for h in range(nk // per):
            fw.dma("pool", dst[:, h * per:(h + 1) * per, 0:ncol], src_v[:, h * per:(h + 1) * per, c0:c0 + ncol], d_dst,
                   writes=[d_dst] if h == 0 else (), accw=() if h == 0 else [d_dst])

    for bi, (st, n, s) in enumerate(blocks):
        for h in range(2):
            fw.dma("sp", xk[:, h * 16:(h + 1) * 16, 0:n], xnT_v[:, h * 16:(h + 1) * 16, st:st + n], d_xk,
                   reads=[d_xnT], writes=[d_xk] if h == 0 else (), accw=() if h == 0 else [d_xk])
            fw.dma("sp", uk[:, h * 16:(h + 1) * 16, 0:n], uT_v[:, h * 16:(h + 1) * 16, st:st + n], d_uk,
                   writes=[d_uk] if h == 0 else (), accw=() if h == 0 else [d_uk])
        for cc in range(32):
            wb = wcnt[0] % 2
            wcnt[0] += 1
            load_slab(WG[wb], d_WG[wb], wg_v, 32, cc * 512, 512)
            load_slab(WBr[wb], d_WBr[wb], wbr_v, 8, cc * 512, 512)
            for i in range(4):
                gb = i % 2
                for kc in range(32):
                    fw.op("pe", lambda e: e.matmul(P[gb][:, 0:n], lhsT=WG[wb][:, kc, i * 128:(i + 1) * 128], rhs=xk[:, kc, 0:n],
                                                   start=(kc == 0), stop=(kc == 31)), reads=[d_WG[wb], d_xk], writes=[dP[gb]])
                for k8 in range(8):
                    fw.op("pe", lambda e: e.matmul(P[2 + gb][:, 0:n], lhsT=WBr[wb][:, k8, i * 128:(i + 1) * 128], rhs=uk[:, i * 8 + k8, 0:n],
                                                   start=(k8 == 0), stop=(k8 == 7)), reads=[d_WBr[wb], d_uk], writes=[dP[2 + gb]])
                fw.op("act", lambda e: e.activation(out=sg[gb][:, 0:n], in_=P[gb][:, 0:n], func=AF.Sigmoid), reads=[dP[gb]], writes=[d_sg[gb]])
                if i == 0:
                    fw.op("dve", lambda e: e.tensor_tensor(out=a32[:, 0:n], in0=sg[gb][:, 0:n], in1=P[2 + gb][:, 0:n], op=ALU.mult),
                          reads=[d_sg[gb], dP[2 + gb]], writes=[d_a32])
                else:
                    fw.op("dve", lambda e: e.tensor_tensor(out=sg[gb][:, 0:n], in0=sg[gb][:, 0:n], in1=P[2 + gb][:, 0:n], op=ALU.mult),
                          reads=[d_sg[gb], dP[2 + gb]], writes=[d_sg[gb]])
                    if i < 3:
                        fw.op("dve", lambda e: e.tensor_tensor(out=a32[:, 0:n], in0=a32[:, 0:n], in1=sg[gb][:, 0:n], op=ALU.add),
                              reads=[d_sg[gb], d_a32], writes=[d_a32])
                    else:
                        fw.op("dve", lambda e: e.tensor_tensor(out=acc[:, cc, 0:n], in0=a32[:, 0:n], in1=sg[gb][:, 0:n], op=ALU.add),
                              reads=[d_sg[gb], d_a32], writes=[d_acc[cc]])
        for oc in range(32):
            wb = wcnt[0] % 2
            wcnt[0] += 1
            ob = oc % 2
            load_slab(WG[wb], d_WG[wb], wout_v, 32, oc * 128, 128)
            fw.dma("sp", xo[ob][:, 0:n], xT_v[:, oc, st:st + n], d_xo[ob], writes=[d_xo[ob]])
            for kc in range(32):
                fw.op("pe", lambda e: e.matmul(P[4 + ob][:, 0:n], lhsT=WG[wb][:, kc, 0:128], rhs=acc[:, kc, 0:n],
                                               start=(kc == 0), stop=(kc == 31)), reads=[d_WG[wb], d_acc[kc]], writes=[dP[4 + ob]])
            fw.op("dve", lambda e: e.scalar_tensor_tensor(out=yo[ob][:, 0:n], in0=P[4 + ob][:, 0:n], scalar=mod[:, s, 64 + oc:65 + oc],
                                                          in1=xo[ob][:, 0:n], op0=ALU.mult, op1=ALU.add),
                  reads=[dP[4 + ob], d_mod, d_xo[ob]], writes=[d_yo[ob]])
            fw.dma("sp", x1T_v[:, oc, st:st + n], yo[ob][:, 0:n], d_yo[ob], reads=[d_yo[ob]], accw=[d_x1])
    fw.finish("sp", [d_x1])
    fw.stack.close()
    return nc


_NC = {}


def _nc(kind, arg):
    key = (kind, arg)
    if key not in _NC:
        _NC[key] = build_A(arg, do=("d", "b", "c", "a")) if kind == "A" else build_B(arg)
    return _NC[key]


def _forward(inp, nl_own):
    x = np.asarray(inp["x"])[0]
    ctx = np.asarray(inp["ctx"])[0]
    TL = x.shape[0]
    TA = CTX + TL
    depth = np.asarray(inp["w_in"]).shape[0]
    xT_all = np.ascontiguousarray(np.concatenate([ctx, x], axis=0).T.astype(np.float32))
    nb = TL // nl_own
    for l in range(depth):
        mapsA = prep_A(inp, l, xT_all, TL)
        resA = run_bass_kernel_spmd(_nc("A", TL), mapsA, core_ids=list(range(8)))
        uT_full = np.empty((4 * W_BR, TA), ml_dtypes.bfloat16)
        for c in range(8):
            u = np.asarray(resA.results[c]["uT"])
            for b in range(4):
                uT_full[b * W_BR + 128 * c:b * W_BR + 128 * c + 128] = u[128 * b:128 * b + 128]
        del resA, mapsA
        mapsB = prep_B(inp, l, xT_all, uT_full, nl_own)
        resB = run_bass_kernel_spmd(_nc("B", nl_own), mapsB, core_ids=list(range(nb)))
        new = np.empty_like(xT_all)
        new[:, 0:CTX] = np.asarray(resB.results[0]["x1T"])[:, 0:CTX]
        for c in range(nb):
            new[:, CTX + nl_own * c:CTX + nl_own * (c + 1)] = np.asarray(resB.results[c]["x1T"])[:, CTX:]
        xT_all = new
        del resB, mapsB
    return np.ascontiguousarray(xT_all[:, CTX:].T)[None].astype(np.float32)


def kernel(**inputs):
    inp = {k: np.asarray(v) for k, v in inputs.items()}
    return _forward(inp, NOWN)
```
